# Optimizing a Trainium2 kernel written in Bass

```python
import jax, jax.numpy as jnp
from jax import lax
import numpy as np

D_MODEL = 1024
BATCH = 4
SEQ = 4096
DEPTH = 2

GRID_W = 64
CTX_LEN = 256
N_MOD = 9
EPS = 1e-6
D_FF = int(round(8 * D_MODEL / 3 / 256)) * 256

HEAD_DIM = 64
GROUP_WIDTH = D_MODEL // 4
MIX_WIDTH = 4 * GROUP_WIDTH

RET_HEADS = GROUP_WIDTH // HEAD_DIM
RET_QK_DIM = HEAD_DIM
RET_V_DIM = HEAD_DIM
RET_CHUNK = 128
RET_THETA = 10000.0

FNET_GROUPS = GROUP_WIDTH // HEAD_DIM
FNET_GROUP_DIM = HEAD_DIM

ATT_Q_HEADS = GROUP_WIDTH // HEAD_DIM
ATT_KV_HEADS = ATT_Q_HEADS // 2
ATT_GROUP = ATT_Q_HEADS // ATT_KV_HEADS
ATT_QBLOCK = 128
ROPE_THETA = 10000.0

GM_GROUPS = GROUP_WIDTH // HEAD_DIM
GM_GROUP_DIM = HEAD_DIM
GM_CHUNK = 128

PROJ_WIDTHS = (RET_HEADS * RET_QK_DIM, RET_HEADS * RET_QK_DIM, RET_HEADS * RET_V_DIM, RET_HEADS * RET_V_DIM,
               FNET_GROUPS * FNET_GROUP_DIM,
               ATT_Q_HEADS * HEAD_DIM, ATT_KV_HEADS * HEAD_DIM, ATT_KV_HEADS * HEAD_DIM,
               GM_GROUPS * GM_GROUP_DIM, GM_GROUPS * GM_GROUP_DIM)
PROJ_DIM = sum(PROJ_WIDTHS)

kernel_name = "hybrid_retention_fourier_gqa_gmlp_dit"

F32 = jnp.float32


def _rms_norm(x, g):
    xf = x.astype(F32)
    y = xf * lax.rsqrt(jnp.mean(xf * xf, axis=-1, keepdims=True) + EPS)
    return y.astype(x.dtype) * g


def _layer_norm(x, g):
    xf = x.astype(F32)
    mu = jnp.mean(xf, axis=-1, keepdims=True)
    var = jnp.mean(jnp.square(xf - mu), axis=-1, keepdims=True)
    return ((xf - mu) * lax.rsqrt(var + EPS)).astype(x.dtype) * g


def _modulate(x, g, shift, scale):
    return _rms_norm(x, g) * (1 + scale) + shift


def _ada(cond, w, b):
    m = jax.nn.silu(cond) @ w + b
    m = m.reshape(m.shape[:-1] + (N_MOD, D_MODEL))
    return [m[..., i, None, :] for i in range(N_MOD)]


def _swiglu(h, w_gu, w_down):
    a, b = jnp.split(h @ w_gu, 2, axis=-1)
    return (jax.nn.silu(a) * b) @ w_down


def _split_heads(x, h):
    B, N, _ = x.shape
    return x.reshape(B, N, h, -1).transpose(0, 2, 1, 3)


def _rope(x, cos, sin):
    half = x.shape[-1] // 2
    x1, x2 = x[..., :half], x[..., half:]
    return jnp.concatenate([x1 * cos - x2 * sin, x1 * sin + x2 * cos], axis=-1).astype(x.dtype)


def _ret_qkv(pq, pk, pv, cos, sin):
    q = _rope(_split_heads(pq, RET_HEADS), cos, sin) * (RET_QK_DIM ** -0.5)
    k = _rope(_split_heads(pk, RET_HEADS), cos, sin)
    v = _split_heads(pv, RET_HEADS)
    return q, k, v


def _final_state(k, v, log_gamma):
    N = k.shape[2]
    lg = log_gamma.astype(F32)
    w = jnp.exp(lg[:, None] * (N - 1 - jnp.arange(N, dtype=F32))[None, :])
    return jnp.einsum('bhnd,bhne->bhde', k.astype(F32) * w[None, :, :, None], v.astype(F32))


def _retention_dir(q, k, v, log_gamma, state0):
    B, H, N, dk = q.shape
    dv = v.shape[-1]
    L = RET_CHUNK
    nc = N // L
    lg = log_gamma.astype(F32)
    idx = jnp.arange(L, dtype=F32)
    diff = idx[:, None] - idx[None, :]
    intra = jnp.where(diff >= 0, jnp.exp(lg[:, None, None] * jnp.maximum(diff, 0.0)[None]), 0.0)
    q_in = jnp.exp(lg[:, None] * (idx + 1.0)[None])
    k_out = jnp.exp(lg[:, None] * (L - 1.0 - idx)[None])
    chunk_decay = jnp.exp(lg * L)
    qc = q.reshape(B, H, nc, L, dk)
    kc = k.reshape(B, H, nc, L, dk)
    vc = v.reshape(B, H, nc, L, dv)
    scores = jnp.einsum('bhcld,bhcmd->bhclm', qc, kc) * intra[None, :, None]
    o_intra = jnp.einsum('bhclm,bhcme->bhcle', scores, vc.astype(F32))
    kv = jnp.einsum('bhcld,bhcle->cbhde', kc.astype(F32) * k_out[None, :, None, :, None], vc.astype(F32))

    def step(s, kv_c):
        return chunk_decay[None, :, None, None] * s + kv_c, s

    _, s_prev = lax.scan(step, state0, kv)
    o_cross = jnp.einsum('bhcld,cbhde->bhcle', qc.astype(F32) * q_in[None, :, None, :, None], s_prev)
    return (o_intra + o_cross).reshape(B, H, N, dv)


def _retention_out(q, k, v, gate, lg_f, lg_b, s_f, s_b, gain):
    o = _retention_dir(q, k, v, lg_f, s_f)
    o = o + jnp.flip(_retention_dir(jnp.flip(q, 2), jnp.flip(k, 2), jnp.flip(v, 2), lg_b, s_b), 2)
    mu = jnp.mean(o, axis=-1, keepdims=True)
    var = jnp.mean(jnp.square(o - mu), axis=-1, keepdims=True)
    o = (o - mu) * lax.rsqrt(var + EPS)
    B, H, N, dv = o.shape
    o = o.transpose(0, 2, 1, 3).reshape(B, N, H * dv).astype(gate.dtype) * gain
    return o * jax.nn.silu(gate)


def _fourier(f):
    B, N, _ = f.shape
    fg = f.reshape(B, N, FNET_GROUPS, FNET_GROUP_DIM).astype(F32)
    y = jnp.real(jnp.fft.fft2(fg, axes=(1, 3), norm='ortho'))
    return y.reshape(B, N, -1).astype(f.dtype)


def _att_qkv(pq, pk, pv, q_norm, k_norm, cos, sin):
    q = _rms_norm(_split_heads(pq, ATT_Q_HEADS), q_norm)
    k = _rms_norm(_split_heads(pk, ATT_KV_HEADS), k_norm)
    v = _split_heads(pv, ATT_KV_HEADS)
    if cos is not None:
        q = _rope(q, cos, sin)
        k = _rope(k, cos, sin)
    B, _, N, d = q.shape
    return q.reshape(B, ATT_KV_HEADS, ATT_GROUP, N, d), k, v


def _attend(q, k, v):
    s = jnp.einsum('bkgqd,bknd->bkgqn', q, k).astype(F32) * (HEAD_DIM ** -0.5)
    p = jax.nn.softmax(s, axis=-1).astype(v.dtype)
    return jnp.einsum('bkgqn,bknd->bkgqd', p, v)


def _merge_att_heads(o):
    B, K, G, N, d = o.shape
    return o.transpose(0, 3, 1, 2, 4).reshape(B, N, K * G * d)


def _spatial_gate(u, v, norm_g, w_s, b_s):
    u = jax.nn.gelu(u)
    v = _layer_norm(jax.nn.gelu(v), norm_g)
    B, N, _ = v.shape
    vc = v.reshape(B, N // GM_CHUNK, GM_CHUNK, GM_GROUPS, GM_GROUP_DIM)
    mixed = jnp.einsum('gij,bcjge->bcige', w_s, vc) + b_s.T[None, None, :, :, None]
    return u * mixed.reshape(B, N, -1)


def _token_mix(n_lat, n_ctx, w_in, w_out, lg_f, lg_b, ret_norm, q_norm, k_norm,
               gm_norm, gm_w, gm_b, ax_cos, ax_sin, rc_cos, rc_sin, rl_cos, rl_sin, need_ctx):
    offs = np.cumsum(PROJ_WIDTHS)[:-1].tolist()
    pl = jnp.split(n_lat @ w_in, offs, axis=-1)
    pc = jnp.split(n_ctx @ w_in, offs, axis=-1)
    B = n_lat.shape[0]

    qc, kc, vc = _ret_qkv(pc[0], pc[1], pc[2], rc_cos, rc_sin)
    ql, kl, vl = _ret_qkv(pl[0], pl[1], pl[2], rl_cos, rl_sin)
    s_f = _final_state(kc, vc, lg_f)
    s_b = _final_state(jnp.flip(kc, 2), jnp.flip(vc, 2), lg_b)
    ret_lat = _retention_out(ql, kl, vl, pl[3], lg_f, lg_b, s_f, s_b, ret_norm)

    fft_lat = _fourier(pl[4])

    aq_l, ak_l, av_l = _att_qkv(pl[5], pl[6], pl[7], q_norm, k_norm, ax_cos, ax_sin)
    aq_c, ak_c, av_c = _att_qkv(pc[5], pc[6], pc[7], q_norm, k_norm, None, None)
    k_all = jnp.concatenate([ak_c, ak_l], axis=2)
    v_all = jnp.concatenate([av_c, av_l], axis=2)
    _, K, G, S, d = aq_l.shape
    nb = S // ATT_QBLOCK
    qb = jnp.moveaxis(aq_l.reshape(B, K, G, nb, ATT_QBLOCK, d), 3, 0)
    ob = lax.map(lambda qblk: _attend(qblk, k_all, v_all), qb)
    att_lat = ob.transpose(1, 0, 4, 2, 3, 5).reshape(B, S, K * G * d)

    gm_lat = _spatial_gate(pl[8], pl[9], gm_norm, gm_w, gm_b)

    mix_lat = jnp.concatenate([ret_lat, fft_lat, att_lat, gm_lat], axis=-1) @ w_out
    if not need_ctx:
        return mix_lat, None

    zero = jnp.zeros((B, RET_HEADS, RET_QK_DIM, RET_V_DIM), F32)
    ret_ctx = _retention_out(qc, kc, vc, pc[3], lg_f, lg_b, zero, zero, ret_norm)
    fft_ctx = _fourier(pc[4])
    att_ctx = _merge_att_heads(_attend(aq_c, ak_c, av_c))
    gm_ctx = _spatial_gate(pc[8], pc[9], gm_norm, gm_w, gm_b)
    mix_ctx = jnp.concatenate([ret_ctx, fft_ctx, att_ctx, gm_ctx], axis=-1) @ w_out
    return mix_lat, mix_ctx


def setup_inputs(seed: int = 0) -> dict:
    key = jax.random.key(seed)
    ks = jax.random.split(key, 26)
    nrm = jax.random.normal
    D = D_MODEL
    base_lg = jnp.log(1.0 - 2.0 ** (-5.0 - jnp.arange(RET_HEADS, dtype=F32)))
    return {
        "x": nrm(ks[0], (BATCH, SEQ, D), F32),
        "c": nrm(ks[1], (BATCH, D), F32),
        "ctx": nrm(ks[2], (BATCH, CTX_LEN, D), F32),
        "c_ctx": nrm(ks[3], (D,), F32),
        "ada_w": nrm(ks[4], (DEPTH, D, N_MOD * D), F32) * (0.5 * D ** -0.5),
        "ada_b": nrm(ks[5], (DEPTH, N_MOD * D), F32) * 0.01,
        "norm_ffn1": 1.0 + 0.01 * nrm(ks[6], (DEPTH, D), F32),
        "ffn1_w_gu": nrm(ks[7], (DEPTH, D, 2 * D_FF), F32) * D ** -0.5,
        "ffn1_w_down": nrm(ks[8], (DEPTH, D_FF, D), F32) * D_FF ** -0.5,
        "norm_mix": 1.0 + 0.01 * nrm(ks[9], (DEPTH, D), F32),
        "w_in": nrm(ks[10], (DEPTH, D, PROJ_DIM), F32) * D ** -0.5,
        "ret_log_decay_fwd": base_lg[None] * jnp.exp(0.05 * nrm(ks[11], (DEPTH, RET_HEADS), F32)),
        "ret_log_decay_bwd": base_lg[None] * jnp.exp(0.05 * nrm(ks[12], (DEPTH, RET_HEADS), F32)),
        "ret_norm": 1.0 + 0.01 * nrm(ks[13], (DEPTH, RET_HEADS * RET_V_DIM), F32),
        "att_q_norm": 1.0 + 0.01 * nrm(ks[14], (DEPTH, HEAD_DIM), F32),
        "att_k_norm": 1.0 + 0.01 * nrm(ks[15], (DEPTH, HEAD_DIM), F32),
        "gmlp_norm": 1.0 + 0.01 * nrm(ks[16], (DEPTH, GM_GROUPS * GM_GROUP_DIM), F32),
        "gmlp_w_s": nrm(ks[17], (DEPTH, GM_GROUPS, GM_CHUNK, GM_CHUNK), F32) * GM_CHUNK ** -0.5,
        "gmlp_b_s": 1.0 + 0.01 * nrm(ks[18], (DEPTH, GM_GROUPS, GM_CHUNK), F32),
        "w_out": nrm(ks[19], (DEPTH, MIX_WIDTH, D), F32) * MIX_WIDTH ** -0.5,
        "norm_ffn2": 1.0 + 0.01 * nrm(ks[20], (DEPTH, D), F32),
        "ffn2_w_gu": nrm(ks[21], (DEPTH, D, 2 * D_FF), F32) * D ** -0.5,
        "ffn2_w_down": nrm(ks[22], (DEPTH, D_FF, D), F32) * D_FF ** -0.5,
        "final_norm": 1.0 + 0.01 * nrm(ks[23], (D,), F32),
    }


def reference(x, c, ctx, c_ctx, ada_w, ada_b, norm_ffn1, ffn1_w_gu, ffn1_w_down, norm_mix, w_in,
              ret_log_decay_fwd, ret_log_decay_bwd, ret_norm, att_q_norm, att_k_norm,
              gmlp_norm, gmlp_w_s, gmlp_b_s, w_out, norm_ffn2, ffn2_w_gu, ffn2_w_down, final_norm):
    S = x.shape[1]
    C = ctx.shape[1]
    rows = S // GRID_W
    row = jnp.repeat(jnp.arange(rows, dtype=F32), GRID_W)
    col = jnp.broadcast_to(jnp.arange(GRID_W, dtype=F32), (rows, GRID_W)).reshape(-1)
    n_axis = HEAD_DIM // 4
    ax_freq = ROPE_THETA ** (-jnp.arange(n_axis, dtype=F32) / n_axis)
    ax_ang = jnp.concatenate([row[:, None] * ax_freq, col[:, None] * ax_freq], axis=-1)
    ax_cos, ax_sin = jnp.cos(ax_ang), jnp.sin(ax_ang)
    ret_freq = 1.0 / (RET_THETA ** jnp.linspace(0.0, 1.0, RET_QK_DIM // 2, dtype=F32))
    rc_ang = jnp.arange(C, dtype=F32)[:, None] * ret_freq
    rl_ang = (C + jnp.arange(S, dtype=F32))[:, None] * ret_freq
    rc_cos, rc_sin = jnp.cos(rc_ang), jnp.sin(rc_ang)
    rl_cos, rl_sin = jnp.cos(rl_ang), jnp.sin(rl_ang)

    h_lat, h_ctx = x, ctx
    for l in range(DEPTH):
        need_ctx = l < DEPTH - 1
        ml = _ada(c, ada_w[l], ada_b[l])
        mc = _ada(c_ctx, ada_w[l], ada_b[l])
        h_lat = h_lat + 0.5 * ml[2] * _swiglu(_modulate(h_lat, norm_ffn1[l], ml[0], ml[1]), ffn1_w_gu[l], ffn1_w_down[l])
        h_ctx = h_ctx + 0.5 * mc[2] * _swiglu(_modulate(h_ctx, norm_ffn1[l], mc[0], mc[1]), ffn1_w_gu[l], ffn1_w_down[l])
        n_lat = _modulate(h_lat, norm_mix[l], ml[3], ml[4])
        n_ctx = _modulate(h_ctx, norm_mix[l], mc[3], mc[4])
        mix_lat, mix_ctx = _token_mix(n_lat, n_ctx, w_in[l], w_out[l], ret_log_decay_fwd[l], ret_log_decay_bwd[l],
                                      ret_norm[l], att_q_norm[l], att_k_norm[l], gmlp_norm[l], gmlp_w_s[l],
                                      gmlp_b_s[l], ax_cos, ax_sin, rc_cos, rc_sin, rl_cos, rl_sin, need_ctx)
        h_lat = h_lat + ml[5] * mix_lat
        h_lat = h_lat + 0.5 * ml[8] * _swiglu(_modulate(h_lat, norm_ffn2[l], ml[6], ml[7]), ffn2_w_gu[l], ffn2_w_down[l])
        if need_ctx:
            h_ctx = h_ctx + mc[5] * mix_ctx
            h_ctx = h_ctx + 0.5 * mc[8] * _swiglu(_modulate(h_ctx, norm_ffn2[l], mc[6], mc[7]), ffn2_w_gu[l], ffn2_w_down[l])
    return _rms_norm(h_lat, final_norm)
```

```python
import contextlib
import os
import math
import numpy as np
import ml_dtypes
import concourse.bass as bass
import concourse.mybir as mybir
from concourse.bass_utils import run_bass_kernel_spmd

F32 = mybir.dt.float32
BF16 = mybir.dt.bfloat16
AF = mybir.ActivationFunctionType
ALU = mybir.AluOpType
NPBF = ml_dtypes.bfloat16

D = 1024
DC = 8
S = 4096
SL = 2048
C = 256
NT = SL + C
DFF = 2816
EPS = 1e-6
BLOCKS = [(0, 512, 0), (512, 512, 0), (1024, 512, 0), (1536, 512, 0), (2048, 256, 1)]
LAT = BLOCKS[:4]
NKC = 34
FFN_GENS = [(0, 3), (3, 3), (6, 3), (9, 3), (12, 3), (15, 3), (18, 3), (21, 1)]
SAME_ENG_SYNC = True

CST = {}
_off = 0


def _c(name, w):
    global _off
    CST[name] = (_off, w)
    _off += w


for _l in range(2):
    _c(f"adab{_l}", 72)
    _c(f"g1_{_l}", 8)
    _c(f"gm_{_l}", 8)
    _c(f"g2_{_l}", 8)
    _c(f"retn{_l}", 2)
    _c(f"gmn{_l}", 2)
    _c(f"aqn{_l}", 1)
    _c(f"aqs{_l}", 1)
    _c(f"akn{_l}", 1)
    _c(f"aks{_l}", 1)
    _c(f"lgf{_l}", 4)
    _c(f"lgb{_l}", 4)
    _c(f"bt{_l}", 256)
_c("fng", 8)
_c("cc", 16)
_c("pmd", 8)
_c("pmdn", 8)
_c("sgf", 56)
_c("sgb", 56)
_c("dlt", 56)
_c("d1", 8)
_c("d2", 8)
NCST = _off

CF = {"ones": (0, 128), "bones": (128, 128), "shift": (256, 64), "iota": (320, 512)}
NCF = 832
CB = {"ident": (0, 128), "ones": (128, 128), "bones": (256, 128), "fd": (384, 256)}
NCB = 640


def _fm(v):
    v = np.asarray(v, np.float32).reshape(-1, 128)
    return np.ascontiguousarray(v.T)


def _swap32(a, axis=-1):
    a = np.moveaxis(a, axis, -1)
    sh = a.shape
    b = a.reshape(sh[:-1] + (sh[-1] // 64, 2, 32))[..., ::-1, :].reshape(sh)
    return np.moveaxis(b, -1, axis)


def host_consts(inp, b, s):
    cst = np.zeros((128, NCST), np.float32)

    def put(name, arr):
        o, w = CST[name]
        arr = np.asarray(arr, np.float32)
        assert arr.shape == (128, w), (name, arr.shape)
        cst[:, o:o + w] = arr

    for l in range(2):
        put(f"adab{l}", _fm(inp["ada_b"][l]))
        put(f"g1_{l}", _fm(inp["norm_ffn1"][l]))
        put(f"gm_{l}", _fm(inp["norm_mix"][l]))
        put(f"g2_{l}", _fm(inp["norm_ffn2"][l]))
        put(f"retn{l}", _fm(inp["ret_norm"][l]))
        put(f"gmn{l}", _fm(inp["gmlp_norm"][l]))
        qn = np.asarray(inp["att_q_norm"][l], np.float32)
        kn = np.asarray(inp["att_k_norm"][l], np.float32)
        put(f"aqn{l}", np.tile(qn, 2)[:, None])
        put(f"aqs{l}", np.tile(_swap32(qn), 2)[:, None])
        put(f"akn{l}", np.tile(kn, 2)[:, None])
        put(f"aks{l}", np.tile(_swap32(kn), 2)[:, None])
        put(f"lgf{l}", np.broadcast_to(np.asarray(inp["ret_log_decay_fwd"][l], np.float32)[None, :], (128, 4)))
        put(f"lgb{l}", np.broadcast_to(np.asarray(inp["ret_log_decay_bwd"][l], np.float32)[None, :], (128, 4)))
        bs = np.asarray(inp["gmlp_b_s"][l], np.float32)
        bt = np.zeros((128, 2, 128), np.float32)
        for g in range(4):
            bt[(g % 2) * 64:(g % 2) * 64 + 64, g // 2, :] = bs[g][None, :]
        put(f"bt{l}", bt.reshape(128, 256))
    put("fng", _fm(inp["final_norm"]))
    cc = np.zeros((128, 8, 2), np.float32)
    cc[:, :, 0] = _fm(inp["c"][b])
    cc[:, :, 1] = _fm(inp["c_ctx"])
    put("cc", cc.reshape(128, 16))
    pmd = np.array([(s - r) * 2048 - 128 * m for r in range(2) for m in range(4)], np.float32)
    put("pmd", np.broadcast_to(pmd[None, :], (128, 8)))
    put("pmdn", np.broadcast_to(-pmd[None, :], (128, 8)))
    dlt = np.array([(s - r) * 2048 + 128 * (di - 15) for r in range(2) for di in range(28)], np.float32)
    sgf = (dlt > 0).astype(np.float32)
    put("dlt", np.broadcast_to(dlt[None, :], (128, 56)))
    put("sgf", np.broadcast_to(sgf[None, :], (128, 56)))
    put("sgb", np.broadcast_to((sgf - 1.0)[None, :], (128, 56)))
    d1 = np.zeros(8, np.float32)
    d2 = np.zeros(8, np.float32)
    for qb in range(4):
        for kc in range(2):
            d1[qb * 2 + kc] = s * 2048 + qb * 512 + 256 - kc * 128
            d2[qb * 2 + kc] = 4096 - s * 2048 - qb * 512 + kc * 128
    put("d1", np.broadcast_to(d1[None, :], (128, 8)))
    put("d2", np.broadcast_to(d2[None, :], (128, 8)))
    return cst


def host_static():
    cf = np.zeros((128, NCF), np.float32)
    cf[:, 0:128] = 1.0
    bo = np.zeros((128, 128), np.float32)
    bo[0:64, 0:64] = 1.0
    bo[64:128, 64:128] = 1.0
    cf[:, 128:256] = bo
    sh = np.zeros((128, 64), np.float32)
    sh[64 + np.arange(64), np.arange(64)] = 1.0
    cf[:, 256:320] = sh
    cf[:, 320:832] = np.arange(512, dtype=np.float32)[None, :] - np.arange(128, dtype=np.float32)[:, None]
    cb = np.zeros((128, NCB), np.float32)
    cb[:, 0:128] = np.eye(128)
    cb[:, 128:256] = 1.0
    cb[:, 256:384] = bo
    de = np.outer(np.arange(64), np.arange(64)).astype(np.float64) * (2 * np.pi / 64)
    c64, s64 = np.cos(de), np.sin(de)
    fd = np.zeros((128, 256))
    fd[0:64, 0:64] = c64
    fd[64:128, 64:128] = c64
    fd[0:64, 128:192] = s64
    fd[64:128, 192:256] = s64
    cb[:, 384:640] = fd
    return cf, cb.astype(NPBF)


def host_rope(s):
    p = np.arange(128)
    f = p % 32
    sign = np.where((p % 64) < 32, -1.0, 1.0)[:, None]
    idx = s * SL + np.arange(SL)
    row = (idx // 64).astype(np.float64)
    col = (idx % 64).astype(np.float64)
    ax_freq = 10000.0 ** (-np.arange(16, dtype=np.float64) / 16)
    ang = np.concatenate([row[:, None] * ax_freq, col[:, None] * ax_freq], -1)
    angA = ang[:, f].T
    ret_freq = 1.0 / (10000.0 ** np.linspace(0.0, 1.0, 32))
    angR = ((C + idx)[:, None] * ret_freq)[:, f].T
    angRc = (np.arange(C)[:, None] * ret_freq)[:, f].T
    t = np.zeros((128, 4, NT), np.float32)
    t[:, 0, :SL] = np.cos(angA)
    t[:, 1, :SL] = np.sin(angA) * sign
    t[:, 0, SL:] = 1.0
    t[:, 2, :SL] = np.cos(angR)
    t[:, 3, :SL] = np.sin(angR) * sign
    t[:, 2, SL:] = np.cos(angRc)
    t[:, 3, SL:] = np.sin(angRc) * sign
    return t


_DFT_CACHE = {}


def host_dft(s):
    if s in _DFT_CACHE:
        return _DFT_CACHE[s]
    n = np.arange(S).astype(np.int64)
    kk = (s * SL + np.arange(SL)).astype(np.int64)
    ph = (np.outer(n, kk) % S).astype(np.float64) * (2 * np.pi / S)
    t = np.empty((S, 2, SL), NPBF)
    t[:, 0, :] = (np.cos(ph) / 512.0).astype(NPBF)
    t[:, 1, :] = (-np.sin(ph) / 512.0).astype(NPBF)
    ph = np.outer(np.arange(C), np.arange(C)).astype(np.float64) * (2 * np.pi / C)
    tc = np.empty((C, 2, C), NPBF)
    tc[:, 0, :] = (np.cos(ph) / 128.0).astype(NPBF)
    tc[:, 1, :] = (-np.sin(ph) / 128.0).astype(NPBF)
    _DFT_CACHE[s] = (t, tc)
    return t, tc


def host_w_in_ext(w):
    def sw(x):
        return _swap32(x, axis=1)
    retq, retk, retv, retg = w[:, 0:256], w[:, 256:512], w[:, 512:768], w[:, 768:1024]
    fnet = w[:, 1024:1280]
    aq = w[:, 1280:1536].reshape(1024, 4, 64)
    aq = np.concatenate([aq[:, 0], aq[:, 2], aq[:, 1], aq[:, 3]], axis=1)
    ak, av = w[:, 1536:1664], w[:, 1664:1792]
    gu, gv = w[:, 1792:2048], w[:, 2048:2304]
    return np.ascontiguousarray(np.concatenate(
        [retq, sw(retq), retk, sw(retk), retv, retg, fnet, aq, sw(aq), ak, sw(ak), av, gu, gv], axis=1), np.float32)


class Sem:
    def __init__(self, h, dma=False):
        self.h = h
        self.n = 0
        self.dma = dma


class Eng:
    def __init__(self, name, e, sem):
        self.name = name
        self.e = e
        self.sem = sem
        self.waited = {}
        self.pending = False


class Buf:
    __slots__ = ("w", "r", "name", "small")

    def __init__(self, name=""):
        self.w = {}
        self.r = {}
        self.name = name
        self.small = {}


class KB:
    def __init__(self, nc, es):
        self.nc = nc
        self.es = es
        self.sems = []
        self.pe = Eng("pe", nc.tensor, self.newsem("pe"))
        self.act = Eng("act", nc.scalar, self.newsem("act"))
        self.dve = Eng("dve", nc.vector, self.newsem("dve"))
        self.pool = Eng("pool", nc.gpsimd, None)
        self.sp = Eng("sp", nc.sync, None)
        self.engs = [self.pe, self.act, self.dve, self.pool, self.sp]
        self.ps = es.enter_context(nc.psum_tensor("ps", [128, 6, 512], F32))
        self.psb = es.enter_context(nc.psum_tensor("psb", [128, 2, 1024], BF16))
        self.banks = [Buf(f"bank{i}") for i in range(6)]
        self.bbanks = [Buf(f"bbank{i}") for i in range(2)]
        self.bptr = 0
        self.bbptr = 0
        self.held = set()
        self.dsems = {}

    def newsem(self, name, dma=False):
        s = Sem(self.es.enter_context(self.nc.semaphore(name)), dma)
        self.sems.append(s)
        return s

    def dsem(self, name):
        if name not in self.dsems or self.dsems[name].n > 2400:
            self.nds = getattr(self, "nds", 0) + 1
            self.dsems[name] = self.newsem(f"d{self.nds}_" + name, dma=True)
        return self.dsems[name]

    def bank(self, hold=False):
        for _ in range(8):
            i = self.bptr
            self.bptr = (self.bptr + 1) % 6
            if i not in self.held:
                if hold:
                    self.held.add(i)
                return i, (self.ps[:, i, :], [self.banks[i]])
        raise RuntimeError("no bank")

    def release(self, i):
        self.held.discard(i)

    def bbank(self):
        i = self.bbptr
        self.bbptr = (self.bbptr + 1) % 2
        return (self.psb[:, i, :], [self.bbanks[i]])

    def _wait(self, E, reads, writes):
        deps = {}
        for b in reads:
            for s, v in b.w.items():
                if deps.get(s, 0) < v:
                    deps[s] = v
        for b in writes:
            for s, v in b.w.items():
                if deps.get(s, 0) < v:
                    deps[s] = v
            for s, v in b.r.items():
                if deps.get(s, 0) < v:
                    deps[s] = v
        for s, v in deps.items():
            if s is E.sem:
                if E is self.pe or not SAME_ENG_SYNC:
                    continue
                if v > s.n:
                    continue
                if E is self.dve:
                    sm = 0
                    for b in list(reads) + list(writes):
                        sm = max(sm, b.small.get(s, 0))
                    if sm == 0:
                        continue
                    v = min(v, sm)
            if s.dma:
                v = s.n
            elif s is not E.sem:
                sm = False
                for b in list(reads) + list(writes):
                    if b.small.get(s, 0) == v:
                        sm = True
                        break
                if sm:
                    v = max(v, min(v + 1, s.n))
            if E.waited.get(s, 0) < v:
                E.e.wait_ge(s.h, v)
                E.waited[s] = v

    def op(self, E, fn, reads, writes, inc=True, small=True):
        if E is not self.pe:
            assert not self.pe.pending
        self._wait(E, reads, writes)
        ins = fn()
        if inc:
            E.sem.n += 1
            ins.then_inc(E.sem.h, 1)
            ev = E.sem.n
            E.pending = False
        else:
            assert E is self.pe
            ev = E.sem.n + 1
            E.pending = True
        s = E.sem
        for b in reads:
            if b.r.get(s, 0) < ev:
                b.r[s] = ev
        for b in writes:
            if b.w.get(s, 0) < ev:
                b.w[s] = ev
            if small:
                b.small[s] = ev
        return ins

    def dma(self, E, out_ap, in_ap, reads, writes, sem):
        assert not self.pe.pending
        self._wait(E, reads, writes)
        ins = E.e.dma_start(out=out_ap, in_=in_ap)
        sem.n += 16
        ins.then_inc(sem.h, 16)
        for b in reads:
            b.r[sem] = sem.n
        for b in writes:
            b.w[sem] = sem.n
        return ins

    def barrier(self):
        assert not self.pe.pending
        for E in self.engs:
            for s in self.sems:
                if s is E.sem:
                    continue
                if E.waited.get(s, 0) < s.n:
                    E.e.wait_ge(s.h, s.n)
                    E.waited[s] = s.n
        for E in (self.pe, self.act, self.dve):
            if E.sem.n > 1500:
                self.nes = getattr(self, "nes", 0) + 1
                E.sem = self.newsem(f"{E.name}{self.nes}")

    def mm(self, out, lhsT, rhs, start=True, stop=True, inc=True):
        return self.op(self.pe, lambda: self.nc.tensor.matmul(out[0], lhsT[0], rhs[0], start=start, stop=stop),
                       lhsT[1] + rhs[1], out[1], inc=inc, small=self._small(out[0]))

    def transpose(self, out, in_, ident):
        return self.op(self.pe, lambda: self.nc.tensor.transpose(out[0], in_[0], ident[0]),
                       in_[1] + ident[1], out[1], small=True)

    def actf(self, out, in_, func, scale=1.0, bias=0.0):
        rd = list(in_[1])
        sc, bi = scale, bias
        if isinstance(scale, tuple):
            rd += scale[1]
            sc = scale[0]
        if isinstance(bias, tuple):
            rd += bias[1]
            bi = bias[0]
        return self.op(self.act, lambda: self.nc.scalar.activation(out=out[0], in_=in_[0], func=func, bias=bi, scale=sc),
                       rd, out[1], small=self._small(out[0]))

    @staticmethod
    def _small(ap):
        n = 1
        for d_ in list(ap.shape)[1:]:
            n *= int(d_)
        return n < 256

    def tt(self, out, a, b, op):
        return self.op(self.dve, lambda: self.nc.vector.tensor_tensor(out=out[0], in0=a[0], in1=b[0], op=op),
                       a[1] + b[1], out[1], small=self._small(out[0]))

    def ts(self, out, a, s1, op0, s2=None, op1=None):
        rd = list(a[1])
        v1, v2 = s1, s2
        if isinstance(s1, tuple):
            rd += s1[1]
            v1 = s1[0]
        if isinstance(s2, tuple):
            rd += s2[1]
            v2 = s2[0]
        if op1 is None:
            return self.op(self.dve, lambda: self.nc.vector.tensor_scalar(out=out[0], in0=a[0], scalar1=v1, scalar2=None, op0=op0),
                           rd, out[1], small=self._small(out[0]))
        return self.op(self.dve, lambda: self.nc.vector.tensor_scalar(out=out[0], in0=a[0], scalar1=v1, scalar2=v2, op0=op0, op1=op1),
                       rd, out[1], small=self._small(out[0]))

    def stt(self, out, a, sc, b, op0, op1):
        rd = list(a[1]) + list(b[1])
        v = sc
        if isinstance(sc, tuple):
            rd += sc[1]
            v = sc[0]
        return self.op(self.dve, lambda: self.nc.vector.scalar_tensor_tensor(out=out[0], in0=a[0], scalar=v, in1=b[0], op0=op0, op1=op1),
                       rd, out[1], small=self._small(out[0]))

    def recip(self, out, a):
        return self.op(self.dve, lambda: self.nc.vector.reciprocal(out=out[0], in_=a[0]), a[1], out[1], small=self._small(out[0]))

    def copy(self, out, a):
        return self.op(self.dve, lambda: self.nc.vector.tensor_copy(out=out[0], in_=a[0]), a[1], out[1], small=self._small(out[0]))

    def memset(self, out, val):
        return self.op(self.dve, lambda: self.nc.vector.memset(out[0], val), [], out[1])


class Ring:
    def __init__(self, n):
        self.n = n
        self.i = 0
        self.bufs = [Buf() for _ in range(n)]

    def next(self):
        i = self.i
        self.i = (self.i + 1) % self.n
        return i, self.bufs[i]


class Prog:
    def __init__(self, seg, fused):
        self.phases = ("L0", "L1", "mix", "fft", "ffn2", "ffn1")
        self.seg = seg
        self.fused = fused
        self.in_names = {}
        self.out_names = {}

    def inp(self, name, shape, dt=F32):
        if name not in self.in_names:
            self.in_names[name] = self.nc.dram_tensor(name, list(shape), dt, kind="ExternalInput").ap()
        return self.in_names[name]

    def outp(self, name, shape, dt=F32):
        if name not in self.out_names:
            self.out_names[name] = self.nc.dram_tensor(name, list(shape), dt, kind="ExternalOutput").ap()
        return self.out_names[name]

    def sb(self, es, name, shape, dt):
        self._sbn = getattr(self, "_sbn", 0) + 1
        return es.enter_context(self.nc.sbuf_tensor(f"s{self._sbn}_{name}", list(shape), dt))

    def build(self):
        nc = bass.Bass("TRN2", target_bir_lowering=False)
        self.nc = nc
        seg = self.seg
        with contextlib.ExitStack() as es:
            k = KB(nc, es)
            self.k = k
            self.hT = self.sb(es, "hT", [128, DC, NT], F32)
            self.hb = [[Buf(f"h{dc}_{bi}") for bi in range(5)] for dc in range(DC)]
            self.cst = self.sb(es, "cst", [128, NCST], F32)
            self.cf = self.sb(es, "cf", [128, NCF], F32)
            self.cb = self.sb(es, "cb", [128, NCB], BF16)
            self.modv = self.sb(es, "modv", [128, 72, 2], F32)
            self.der = self.sb(es, "der", [128, 3, 2, 8, 2], F32)
            self.qs = self.sb(es, "qs", [128, 3, 2, NT], BF16)
            self.cbuf = Buf("const")
            self.modb = Buf("modv")
            self.derb = Buf("der")
            self.qb = [[Buf(f"q{i}_{bi}") for bi in range(5)] for i in range(3)]
            cs = k.dsem("const")
            k.dma(k.sp, self.cst[:], self.inp("cst", [128, NCST]), [], [self.cbuf], cs)
            k.dma(k.sp, self.cf[:], self.inp("cf", [128, NCF]), [], [self.cbuf], cs)
            k.dma(k.sp, self.cb[:], self.inp("cb", [128, NCB], BF16), [], [self.cbuf], cs)
            h0 = self.inp("h0", [D, NT])
            for dc in range(DC):
                k.dma(k.sp, self.hT[:, dc, :], h0[dc * 128:(dc + 1) * 128, :], [],
                      [self.hb[dc][bi] for bi in range(5)], k.dsem("hload"))
            self.ccsem = k.newsem("cc")
            for l in range(2):
                need_ctx = (l == 0)
                self.kvl = [nc.dram_tensor(f"kvloc{l}_{p_}", [4 * 128, SL], BF16).ap() for p_ in range(2)]
                self.kva = [nc.dram_tensor(f"kvall{l}_{p_}", [8 * 128, SL], BF16).ap() for p_ in range(2)]
                self.kvc = nc.dram_tensor(f"kvctx{l}", [8 * 128, C], BF16).ap()
                self.kvlb, self.kvab, self.kvcb = Buf("kvl"), Buf("kva"), Buf("kvc")
                self.kvrd = [self.kvab, self.kvcb]
                ph = self.phases
                if f"L{l}" not in ph:
                    continue
                self.ada(l)
                self.derive(l)
                if "ffn1" in ph:
                    self.ffn(l, 1, 0, BLOCKS)
                self.proj_phase(l, need_ctx)
                k._wait(k.pool, [self.kvlb], [self.kvab])
                for p_ in range(2):
                    ins = nc.gpsimd.collective_compute("AllGather", ALU.bypass, replica_groups=[[0, 1], [2, 3], [4, 5], [6, 7]],
                                                       ins=[self.kvl[p_].opt()], outs=[self.kva[p_].opt()])
                    ins.then_inc(self.ccsem.h)
                    self.ccsem.n += 1
                self.kvab.w[self.ccsem] = self.ccsem.n
                self.kvlb.r[self.ccsem] = self.ccsem.n
                if "mix" in ph:
                    self.mix_phase(l, need_ctx)
                if "attn" in ph:
                    self.attn_phase(l, need_ctx)
                if "ret" in ph:
                    self.ret_phase(l, need_ctx)
                if "fft" in ph:
                    self.fft_phase(l, need_ctx)
                if "ffn2" in ph:
                    self.ffn(l, 2, 2, BLOCKS if need_ctx else LAT)
            self.final_norm()
            k.barrier()
        return nc

    def C_(self, name, a=0, w=None):
        o, ww = CST[name]
        if w is None:
            w = ww - a
        return (self.cst[:, o + a:o + a + w], [self.cbuf])

    def CF_(self, name, parts=slice(0, 128)):
        o, w = CF[name]
        return (self.cf[parts, o:o + w], [self.cbuf])

    def CB_(self, name, a=0, w=None, parts=slice(0, 128)):
        o, ww = CB[name]
        if w is None:
            w = ww - a
        return (self.cb[parts, o + a:o + a + w], [self.cbuf])

    def H(self, dc, bi):
        t0, tn, _ = BLOCKS[bi]
        return (self.hT[:, dc, t0:t0 + tn], [self.hb[dc][bi]])

    def Q(self, i, ci, bi, parts=slice(0, 128)):
        t0, tn, _ = BLOCKS[bi]
        return (self.qs[parts, i, ci, t0:t0 + tn], [self.qb[i][bi]])

    def DER(self, sub, kind, dc, mi):
        return (self.der[:, sub, kind, dc, mi:mi + 1], [self.derb])

    def MODV(self, j, mi):
        return (self.modv[:, j, mi:mi + 1], [self.modb])

    def save_state(self):
        k = self.k
        k.barrier()
        ds = k.dsem("state")
        o = self.outp("o_st_h", [128, DC * NT])
        k.dma(k.sp, o, self.hT[:].rearrange("p a b -> p (a b)"), [b for r in self.hb for b in r], [], ds)
        o = self.outp("o_st_mod", [128, 144])
        k.dma(k.sp, o, self.modv[:].rearrange("p a b -> p (a b)"), [self.modb], [], ds)
        o = self.outp("o_st_q", [128, 6 * NT], BF16)
        k.dma(k.sp, o, self.qs[:].rearrange("p a b c -> p (a b c)"), [b for r in self.qb for b in r], [], ds)

    def load_state(self):
        k = self.k
        ds = k.dsem("state")
        i = self.inp("i_st_h", [128, DC * NT])
        k.dma(k.sp, self.hT[:].rearrange("p a b -> p (a b)"), i, [], [b for r in self.hb for b in r], ds)
        i = self.inp("i_st_mod", [128, 144])
        k.dma(k.sp, self.modv[:].rearrange("p a b -> p (a b)"), i, [], [self.modb], ds)
        i = self.inp("i_st_q", [128, 6 * NT], BF16)
        k.dma(k.sp, self.qs[:].rearrange("p a b c -> p (a b c)"), i, [], [b for r in self.qb for b in r], ds)

    def ada(self, l):
        k, nc = self.k, self.nc
        k.barrier()
        w = self.inp(f"ada_w{l}", [D, 9 * D])
        wv = w.rearrange("(dc p) f -> p dc f", p=128)
        with contextlib.ExitStack() as es:
            ring = self.sb(es, "adaw", [128, 2, DC, 1024], BF16)
            rb = [Buf(), Buf()]
            sT = self.sb(es, "adas", [128, 16], BF16)
            sb_ = Buf()
            k.actf((sT[:], [sb_]), self.C_("cc"), AF.Silu)
            for km in range(min(2, 9)):
                k.dma(k.pool, ring[:, km % 2], wv[:, :, km * 1024:(km + 1) * 1024], [], [rb[km % 2]], k.dsem(f"wr{km % 2}"))
            for km in range(9):
                sl = km % 2
                bi_, bk = k.bank()
                for jc in range(8):
                    for dc in range(DC):
                        k.mm((bk[0][:, jc * 2:jc * 2 + 2], bk[1]), (ring[:, sl, dc, jc * 128:(jc + 1) * 128], [rb[sl]]),
                             (sT[:, dc * 2:dc * 2 + 2], [sb_]), start=(dc == 0), stop=(dc == DC - 1),
                             inc=(dc == DC - 1))
                for jc in range(8):
                    j = km * 8 + jc
                    k.ts((self.modv[:, j, :], [self.modb]), (bk[0][:, jc * 2:jc * 2 + 2], bk[1]),
                         self.C_(f"adab{l}", j, 1), ALU.add)
                if km + 2 < 9:
                    k.dma(k.pool, ring[:, sl], wv[:, :, (km + 2) * 1024:(km + 3) * 1024], [], [rb[sl]], k.dsem(f"wr{sl}"))
            k.barrier()

    def derive(self, l):
        k = self.k
        gn = [f"g1_{l}", f"gm_{l}", f"g2_{l}"]
        for sub in range(3):
            k0 = 3 * sub
            for mi in range(2):
                k.stt((self.der[:, sub, 0, :, mi], [self.derb]), (self.modv[:, (k0 + 1) * 8:(k0 + 2) * 8, mi], [self.modb]),
                      1.0, self.C_(gn[sub]), ALU.add, ALU.mult)
                k.ts((self.der[:, sub, 1, :, mi], [self.derb]), (self.modv[:, (k0 + 2) * 8:(k0 + 3) * 8, mi], [self.modb]),
                     1.0 if sub == 1 else 0.5, ALU.mult)

    def modulate(self, l, sub, blocks, nT, nb, es):
        k = self.k
        sq8 = self.sb(es, "sq8", [128, 3, 512], BF16)
        sqr = Ring(3)
        rs = self.sb(es, "rs", [128, 2, 512], F32)
        rsr = Ring(2)
        tt = self.sb(es, "mtt", [128, 2, 512], F32)
        ttr = Ring(2)
        for (t0, tn, mi) in blocks:
            bi = t0 // 512
            _, bk = k.bank()
            for dc in range(DC):
                qi, qbf = sqr.next()
                k.actf((sq8[:, qi, :tn], [qbf]), self.H(dc, bi), AF.Square)
                k.mm((bk[0][:, :tn], bk[1]), self.CB_("ones"), (sq8[:, qi, :tn], [qbf]), start=(dc == 0),
                     stop=(dc == DC - 1))
            ri, rbuf = rsr.next()
            r = (rs[:, ri, :tn], [rbuf])
            k.actf(r, (bk[0][:, :tn], bk[1]), AF.Sqrt, scale=1.0 / D, bias=self.epsap)
            k.recip(r, r)
            for dc in range(DC):
                ti, tb = ttr.next()
                t = (tt[:, ti, :tn], [tb])
                k.tt(t, self.H(dc, bi), r, ALU.mult)
                k.actf((nT[:, dc, t0:t0 + tn], [nb[bi]]), t, AF.Identity, scale=self.DER(sub, 0, dc, mi),
                       bias=self.MODV(3 * sub * 8 + dc, mi))

    def ffn(self, l, which, sub, blocks):
        k = self.k
        k.barrier()
        wgu = self.inp(f"wgu{which}_{l}", [D, 2 * DFF]).rearrange("(dc p) f -> p dc f", p=128)
        wdn = self.inp(f"wd{which}_{l}", [DFF, D]).rearrange("(c p) f -> p c f", p=128)
        with contextlib.ExitStack() as es:
            nT = self.sb(es, "nT", [128, DC, NT], BF16)
            nb = [Buf() for _ in range(5)]
            ring = self.sb(es, "wring", [128, 2, 9216], BF16)
            rb = [Buf(), Buf()]
            hid = self.sb(es, "hid", [128, 2, 3, 512], BF16)
            hr = Ring(2)
            sa = self.sb(es, "sa", [128, 2, 512], F32)
            sar = Ring(2)

            def load(gi):
                c0, G = FFN_GENS[gi]
                sl = gi % 2
                wa = ring[:, sl, 0:8 * G * 128].rearrange("p (c f) -> p c f", c=8)
                wb = ring[:, sl, 3072:3072 + 8 * G * 128].rearrange("p (c f) -> p c f", c=8)
                wd = ring[:, sl, 6144:6144 + G * 1024].rearrange("p (c f) -> p c f", c=G)
                ds = k.dsem(f"wr{sl}")
                k.dma(k.pool, wa, wgu[:, :, c0 * 128:(c0 + G) * 128], [], [rb[sl]], ds)
                k.dma(k.pool, wb, wgu[:, :, DFF + c0 * 128:DFF + (c0 + G) * 128], [], [rb[sl]], ds)
                k.dma(k.pool, wd, wdn[:, c0:c0 + G, :], [], [rb[sl]], ds)

            load(0)
            load(1)
            with contextlib.ExitStack() as es2:
                self.modulate(l, sub, blocks, nT, nb, es2)
            for gi, (c0, G) in enumerate(FFN_GENS):
                sl = gi % 2
                wa = ring[:, sl, 0:8 * G * 128].rearrange("p (c f) -> p c f", c=8)
                wb = ring[:, sl, 3072:3072 + 8 * G * 128].rearrange("p (c f) -> p c f", c=8)
                wd = ring[:, sl, 6144:6144 + G * 1024].rearrange("p (c f) -> p c f", c=G)
                for (t0, tn, mi) in blocks:
                    bi = t0 // 512
                    hi, hbuf = hr.next()
                    for c in range(G):
                        _, pa = k.bank()
                        for dc in range(DC):
                            k.mm((pa[0][:, :tn], pa[1]), (wa[:, dc, c * 128:(c + 1) * 128], [rb[sl]]),
                                 (nT[:, dc, t0:t0 + tn], [nb[bi]]), start=(dc == 0), stop=(dc == DC - 1), inc=(dc == DC - 1))
                        _, pb = k.bank()
                        for dc in range(DC):
                            k.mm((pb[0][:, :tn], pb[1]), (wb[:, dc, c * 128:(c + 1) * 128], [rb[sl]]),
                                 (nT[:, dc, t0:t0 + tn], [nb[bi]]), start=(dc == 0), stop=(dc == DC - 1), inc=(dc == DC - 1))
                        si, sbuf = sar.next()
                        s_ = (sa[:, si, :tn], [sbuf])
                        k.actf(s_, (pa[0][:, :tn], pa[1]), AF.Silu)
                        k.tt((hid[:, hi, c, :tn], [hbuf]), s_, (pb[0][:, :tn], pb[1]), ALU.mult)
                    for j in range(DC):
                        _, pd = k.bank()
                        for c in range(G):
                            k.mm((pd[0][:, :tn], pd[1]), (wd[:, c, j * 128:(j + 1) * 128], [rb[sl]]),
                                 (hid[:, hi, c, :tn], [hbuf]), start=(c == 0), stop=(c == G - 1), inc=(c == G - 1))
                        k.stt(self.H(j, bi), (pd[0][:, :tn], pd[1]), self.DER(sub, 1, j, mi), self.H(j, bi), ALU.mult, ALU.add)
                if gi + 2 < len(FFN_GENS):
                    load(gi + 2)
            k.barrier()

    def proj_phase(self, l, need_ctx):
        k = self.k
        k.barrier()
        win = self.inp(f"win{l}", [D, 25 * 128]).rearrange("(dc p) f -> p dc f", p=128)
        wout = self.inp(f"wout{l}", [D, D])
        rope = self.inp("rope", [128, 4, NT])
        wsT = self.inp(f"wsT{l}", [128, 512])
        with contextlib.ExitStack() as es:
            nT = self.sb(es, "nT", [128, DC, NT], BF16)
            nb = [Buf() for _ in range(5)]
            with contextlib.ExitStack() as es2:
                self.modulate(l, 1, BLOCKS, nT, nb, es2)
                k.barrier()
            wr = self.sb(es, "pw", [128, 2, DC, 512], BF16)
            wrr = Ring(2)

            def loadw(c0, n):
                i, b = wrr.next()
                k.dma(k.pool, wr[:, i, :, 0:n * 128], win[:, :, c0 * 128:(c0 + n) * 128], [], [b], k.dsem(f"wr{i}"))
                return i, b

            def proj(wi, wbuf, ci, bi):
                t0, tn, mi = BLOCKS[bi]
                _, bk = k.bank()
                for dc in range(DC):
                    k.mm((bk[0][:, :tn], bk[1]), (wr[:, wi, dc, ci * 128:(ci + 1) * 128], [wbuf]),
                         (nT[:, dc, t0:t0 + tn], [nb[bi]]), start=(dc == 0), stop=(dc == DC - 1), inc=(dc == DC - 1))
                return (bk[0][:, :tn], bk[1])

            with contextlib.ExitStack() as es3:
                wi, wbuf = loadw(21, 4)
                wo = self.sb(es3, "gwo", [128, 2, D], BF16)
                wob = Buf()
                k.dma(k.pool, wo[:], wout[768:1024, :].rearrange("(c p) f -> p c f", p=128), [], [wob], k.dsem("wo"))
                ws = self.sb(es3, "gws", [128, 4, 128], BF16)
                wsb = Buf()
                k.dma(k.pool, ws[:].rearrange("p a b -> p (a b)"), wsT, [], [wsb], k.dsem("wo"))
                u = self.sb(es3, "gu", [128, 2, 512], F32)
                ub = Buf()
                gv = self.sb(es3, "gv", [128, 2, 512], F32)
                gvb = Buf()
                gq = self.sb(es3, "gq", [128, 2, 512], F32)
                gqb = Buf()
                st = self.sb(es3, "gst", [128, 4, 512], F32)
                stb = [Buf() for _ in range(4)]
                vn = self.sb(es3, "gvn", [128, 2, 512], BF16)
                vnb = Buf()
                vp = self.sb(es3, "gvp", [128, 2, 4, 128], BF16)
                vpr = Ring(2)
                go = self.sb(es3, "ggo", [128, 2, 512], BF16)
                gob = Buf()
                gt = self.sb(es3, "ggt", [128, 2, 128], F32)
                gtr = Ring(2)
                k.memset((vp[:], vpr.bufs), 0.0)
                for bi, (t0, tn, mi) in enumerate(BLOCKS):
                    if mi == 1 and not need_ctx:
                        continue
                    def gelu(dst, P):
                        a_ = (st[:, 0, :tn], [stb[0]])
                        b_ = (st[:, 1, :tn], [stb[1]])
                        k.actf(a_, P, AF.Square)
                        k.ts(a_, a_, 0.044715, ALU.mult, 1.0, ALU.add)
                        k.tt(a_, a_, P, ALU.mult)
                        k.actf(b_, a_, AF.Sigmoid, scale=1.5957691216057308)
                        k.tt(dst, b_, P, ALU.mult)

                    for c in range(2):
                        gelu((u[:, c, :tn], [ub]), proj(wi, wbuf, c, bi))
                    for c in range(2):
                        gelu((gv[:, c, :tn], [gvb]), proj(wi, wbuf, 2 + c, bi))
                        k.actf((gq[:, c, :tn], [gqb]), (gv[:, c, :tn], [gvb]), AF.Square)
                    _, b1 = k.bank()
                    for c in range(2):
                        k.mm((b1[0][:, :tn], b1[1]), self.CF_("ones"), (gv[:, c, :tn], [gvb]), start=(c == 0), stop=(c == 1), inc=(c == 1))
                    _, b2 = k.bank()
                    for c in range(2):
                        k.mm((b2[0][:, :tn], b2[1]), self.CF_("ones"), (gq[:, c, :tn], [gqb]), start=(c == 0), stop=(c == 1), inc=(c == 1))
                    mean = (st[:, 0, :tn], [stb[0]])
                    msq = (st[:, 1, :tn], [stb[1]])
                    var = (st[:, 2, :tn], [stb[2]])
                    dd = (st[:, 3, :tn], [stb[3]])
                    k.actf(mean, (b1[0][:, :tn], b1[1]), AF.Identity, scale=1.0 / 256)
                    k.tt(msq, mean, mean, ALU.mult)
                    k.stt(var, (b2[0][:, :tn], b2[1]), 1.0 / 256, msq, ALU.mult, ALU.subtract)
                    k.actf(var, var, AF.Sqrt, bias=self.epsap)
                    k.recip(var, var)
                    for c in range(2):
                        k.tt(dd, (gv[:, c, :tn], [gvb]), mean, ALU.subtract)
                        k.stt((vn[:, c, :tn], [vnb]), dd, self.C_(f"gmn{l}", c, 1), var, ALU.mult, ALU.mult)
                    for tl in range(tn // 128):
                        vi, vb = vpr.next()
                        for c in range(2):
                            pt = k.bbank()
                            k.transpose((pt[0][:, 0:128], pt[1]), (vn[:, c, tl * 128:(tl + 1) * 128], [vnb]), self.CB_("ident"))
                            k.copy((vp[:, vi, 2 * c, 0:64], [vb]), (pt[0][:, 0:64], pt[1]))
                            k.copy((vp[:, vi, 2 * c + 1, 64:128], [vb]), (pt[0][:, 64:128], pt[1]))
                        for c in range(2):
                            _, mb = k.bank()
                            for gg in range(2):
                                k.mm((mb[0][:, 0:128], mb[1]), (vp[:, vi, 2 * c + gg, :], [vb]), (ws[:, 2 * c + gg, :], [wsb]),
                                     start=(gg == 0), stop=(gg == 1), inc=(gg == 1))
                            gi_, gb_ = gtr.next()
                            g_ = (gt[:, gi_, :], [gb_])
                            o_, w_ = CST[f"bt{l}"]
                            k.tt(g_, (mb[0][:, 0:128], mb[1]), (self.cst[:, o_ + c * 128:o_ + (c + 1) * 128], [self.cbuf]), ALU.add)
                            k.tt((go[:, c, tl * 128:(tl + 1) * 128], [gob]), g_, (u[:, c, tl * 128:(tl + 1) * 128], [ub]), ALU.mult)
                    for j in range(DC):
                        _, pd = k.bank()
                        for c in range(2):
                            k.mm((pd[0][:, :tn], pd[1]), (wo[:, c, j * 128:(j + 1) * 128], [wob]), (go[:, c, :tn], [gob]),
                                 start=(c == 0), stop=(c == 1), inc=(c == 1))
                        k.stt(self.H(j, bi), (pd[0][:, :tn], pd[1]), self.DER(1, 1, j, mi), self.H(j, bi), ALU.mult, ALU.add)
                k.barrier()

            with contextlib.ExitStack() as es3:
                rr = self.sb(es3, "rope", [128, 2, 2, 512], F32)
                rrr = Ring(2)
                tmp = self.sb(es3, "ptmp", [128, 4, 512], F32)
                tr = Ring(4)
                stg = self.sb(es3, "pstg", [128, 3, 512], BF16)
                sr = Ring(3)
                sqt = self.sb(es3, "psq", [128, 2, 512], BF16)
                sqr = Ring(2)
                rst = self.sb(es3, "prs", [128, 2, 512], F32)
                rsr = Ring(2)

                def T_(tn):
                    i, b = tr.next()
                    return (tmp[:, i, :tn], [b])

                def store(src, row, bi):
                    t0, tn, mi = BLOCKS[bi]
                    if mi == 0:
                        k.dma(k.sp, self.kvl[row // 4][(row % 4) * 128:(row % 4 + 1) * 128, t0:t0 + tn], src[0], src[1], [self.kvlb], k.dsem("kvst"))
                    else:
                        k.dma(k.sp, self.kvc[row * 128:(row + 1) * 128, 0:tn], src[0], src[1], [self.kvcb], k.dsem("kvst"))

                def rope_unit(c0, nch, tab, kind, dst):
                    wi, wbuf = loadw(c0, 2 * nch)
                    for bi, (t0, tn, mi) in enumerate(BLOCKS):
                        if mi == 1 and not need_ctx and kind in ("retq", "attq"):
                            continue
                        ri, rbuf = rrr.next()
                        k.dma(k.sp, rr[:, ri, :, :tn], rope[:, tab:tab + 2, t0:t0 + tn], [], [rbuf], k.dsem(f"rope{ri}"))
                        cosT = (rr[:, ri, 0, :tn], [rbuf])
                        sinT = (rr[:, ri, 1, :tn], [rbuf])
                        for ci in range(nch):
                            P = proj(wi, wbuf, ci, bi)
                            Ps = proj(wi, wbuf, nch + ci, bi)
                            t1, t2 = T_(tn), T_(tn)
                            if kind in ("retq", "retk"):
                                if kind == "retq":
                                    k.stt(t1, P, 0.125, cosT, ALU.mult, ALU.mult)
                                    k.stt(t2, Ps, 0.125, sinT, ALU.mult, ALU.mult)
                                else:
                                    k.tt(t1, P, cosT, ALU.mult)
                                    k.tt(t2, Ps, sinT, ALU.mult)
                                if kind == "retq":
                                    k.tt(self.Q(1, ci, bi), t1, t2, ALU.add)
                                else:
                                    si, sbuf = sr.next()
                                    o = (stg[:, si, :tn], [sbuf])
                                    k.tt(o, t1, t2, ALU.add)
                                    store(o, dst + ci, bi)
                            else:
                                gname = "aq" if kind == "attq" else "ak"
                                qi, qbuf = sqr.next()
                                sq = (sqt[:, qi, :tn], [qbuf])
                                k.actf(sq, P, AF.Square)
                                _, sb_ = k.bank()
                                k.mm((sb_[0][:, :tn], sb_[1]), self.CB_("bones"), sq)
                                ri2, rb2 = rsr.next()
                                r = (rst[:, ri2, :tn], [rb2])
                                k.actf(r, (sb_[0][:, :tn], sb_[1]), AF.Sqrt, scale=1.0 / 64, bias=self.epsap)
                                k.recip(r, r)
                                k.stt(t1, P, self.C_(f"{gname}n{l}"), cosT, ALU.mult, ALU.mult)
                                k.stt(t2, Ps, self.C_(f"{gname}s{l}"), sinT, ALU.mult, ALU.mult)
                                k.tt(t1, t1, t2, ALU.add)
                                if kind == "attq":
                                    k.tt(self.Q(0, ci, bi), t1, r, ALU.mult)
                                else:
                                    si, sbuf = sr.next()
                                    o = (stg[:, si, :tn], [sbuf])
                                    k.tt(o, t1, r, ALU.mult)
                                    store(o, dst + ci, bi)

                def plain_unit(c0, nch, kind, dst):
                    wi, wbuf = loadw(c0, nch)
                    for bi, (t0, tn, mi) in enumerate(BLOCKS):
                        if mi == 1 and not need_ctx and kind in ("retg", "fnet"):
                            continue
                        for ci in range(nch):
                            P = proj(wi, wbuf, ci, bi)
                            if kind == "retg":
                                k.actf(self.Q(2, ci, bi), P, AF.Silu)
                            else:
                                si, sbuf = sr.next()
                                o = (stg[:, si, :tn], [sbuf])
                                k.actf(o, P, AF.Identity)
                                store(o, dst + ci, bi)

                rope_unit(0, 2, 2, "retq", None)
                rope_unit(4, 2, 2, "retk", 2)
                plain_unit(8, 2, "retv", 4)
                plain_unit(10, 2, "retg", None)
                plain_unit(12, 2, "fnet", 6)
                rope_unit(14, 2, 0, "attq", None)
                rope_unit(18, 1, 0, "attk", 0)
                plain_unit(20, 1, "attv", 1)
                k.barrier()

    def kv_src(self, row, c0, n):
        out = []
        segs = [(0, C, self.kvc[row * 128:(row + 1) * 128, :]),
                (C, SL, self.kva[row // 4][(row % 4) * 128:(row % 4 + 1) * 128, :]),
                (C + SL, SL, self.kva[row // 4][(4 + row % 4) * 128:(5 + row % 4) * 128, :])]
        for (s0, sn, t) in segs:
            a = max(c0, s0)
            b = min(c0 + n, s0 + sn)
            if a < b:
                out.append((t[:, a - s0:b - s0], a - c0, b - a))
        return out

    def attn_phase(self, l, need_ctx):
        k = self.k
        k.barrier()
        wout = self.inp(f"wout{l}", [D, D])
        with contextlib.ExitStack() as es:
            Kt = self.sb(es, "aK", [128, NKC * 128], BF16)
            Kb = Buf()
            Va = self.sb(es, "aV", [128, NKC, 2, 128], BF16)
            Vb = Buf()
            vst = self.sb(es, "avst", [128, 2, 512], BF16)
            vsr = Ring(2)
            E = self.sb(es, "aE", [128, 4, 512], BF16)
            Er = Ring(4)
            accs = self.sb(es, "aacc", [128, 2, 512], F32)
            acr = Ring(2)
            rden = self.sb(es, "arden", [64, 2, 512], F32)
            rdr = Ring(2)
            aout = self.sb(es, "aout", [64, 4, 512], BF16)
            aob = Buf()
            wo = self.sb(es, "awo", [64, 4, D], BF16)
            wob = Buf()
            for (src, off, n) in self.kv_src(0, 0, NKC * 128):
                k.dma(k.sp, Kt[:, off:off + n], src, self.kvrd, [Kb], k.dsem("kload"))
            k.dma(k.pool, wo[:], wout[512:768, :].rearrange("(h p) f -> p h f", p=64), [], [wob], k.dsem("wo"))
            dbg = int(os.environ.get("KDBG", "99"))
            if dbg <= 0:
                k.barrier()
                return
            k.memset((Va[:, :, :, 64:128], [Vb]), 1.0)
            if dbg <= 1:
                k.barrier()
                return
            for pc in range(0, NKC * 128, 512):
                n = min(512, NKC * 128 - pc)
                vi, vb = vsr.next()
                for (src, off, nn) in self.kv_src(1, pc, n):
                    k.dma(k.sp, vst[:, vi, off:off + nn], src, self.kvrd, [vb], k.dsem(f"vst{vi}"))
                for tl in range(n // 128):
                    kc = pc // 128 + tl
                    if dbg == 12:
                        continue
                    pt = k.bbank()
                    k.transpose((pt[0][:, 0:128], pt[1]), (vst[:, vi, tl * 128:(tl + 1) * 128], [vb]), self.CB_("ident"))
                    if dbg == 13:
                        continue
                    if dbg == 14:
                        k.copy((accs[:, 0, 0:64], [acr.bufs[0]]), (pt[0][:, 0:64], pt[1]))
                        continue
                    if dbg == 15:
                        k.copy((Va[:, kc, 0, 0:64], [Vb]), (accs[:, 0, 0:64], [acr.bufs[0]]))
                        continue
                    k.copy((Va[:, kc, 0, 0:64], [Vb]), (pt[0][:, 0:64], pt[1]))
                    k.copy((Va[:, kc, 1, 0:64], [Vb]), (pt[0][:, 64:128], pt[1]))
            if dbg <= 2 or dbg in (12, 13, 14, 15):
                k.barrier()
                return
            for bi, (t0, tn, mi) in enumerate(BLOCKS):
                if mi == 1 and not need_ctx:
                    continue
                if dbg <= 6 and bi > 0:
                    continue
                kcs = list(range(NKC)) if mi == 0 else [0, 1]
                if dbg <= 3:
                    kcs = kcs[:3]
                for h in range(4):
                    kvh = h // 2
                    pr = slice(kvh * 64, kvh * 64 + 64)
                    q = self.Q(0, h % 2, bi, parts=pr)
                    ai, acc = k.bank(hold=True)
                    pend = []

                    def score(kc):
                        _, sb_ = k.bank()
                        k.mm((sb_[0][:, :tn], sb_[1]), (Kt[pr, kc * 128:(kc + 1) * 128], [Kb]), q)
                        return sb_

                    def finish(kc, sb_, first, last):
                        ei, eb = Er.next()
                        e = (E[:, ei, :tn], [eb])
                        k.actf(e, (sb_[0][:, :tn], sb_[1]), AF.Exp, scale=0.125)
                        k.mm((acc[0][:, :tn], acc[1]), (Va[:, kc, kvh, :], [Vb]), e, start=first, stop=last)

                    for idx, kc in enumerate(kcs):
                        pend.append((kc, score(kc)))
                        if len(pend) > 3:
                            kc0, s0 = pend.pop(0)
                            finish(kc0, s0, kc0 == kcs[0], False)
                    while pend:
                        kc0, s0 = pend.pop(0)
                        finish(kc0, s0, kc0 == kcs[0], len(pend) == 0)
                    ci, cbf = acr.next()
                    a_ = (accs[:, ci, :tn], [cbf])
                    k.actf(a_, (acc[0][:, :tn], acc[1]), AF.Identity)
                    k.release(ai)
                    if dbg <= 4:
                        continue
                    _, dn = k.bank()
                    k.mm((dn[0][0:64, :tn], dn[1]), self.CF_("shift"), a_)
                    di, dbf = rdr.next()
                    rd = (rden[:, di, :tn], [dbf])
                    k.recip(rd, (dn[0][0:64, :tn], dn[1]))
                    k.tt((aout[:, h, :tn], [aob]), (accs[0:64, ci, :tn], [cbf]), rd, ALU.mult)
                if dbg <= 5:
                    continue
                for j in range(DC):
                    _, pd = k.bank()
                    for h in range(4):
                        k.mm((pd[0][:, :tn], pd[1]), (wo[:, h, j * 128:(j + 1) * 128], [wob]), (aout[:, h, :tn], [aob]),
                             start=(h == 0), stop=(h == 3), inc=(h == 3))
                    k.stt(self.H(j, bi), (pd[0][:, :tn], pd[1]), self.DER(1, 1, j, mi), self.H(j, bi), ALU.mult, ALU.add)
            k.barrier()

    def ret_phase(self, l, need_ctx):
        k = self.k
        k.barrier()
        wout = self.inp(f"wout{l}", [D, D])
        with contextlib.ExitStack() as es:
            Kr = self.sb(es, "rK", [128, NKC * 128], BF16)
            Kb = Buf()
            Vr = self.sb(es, "rV", [128, NKC, 128], BF16)
            Vb = Buf()
            vst = self.sb(es, "rvst", [128, 2, 256], BF16)
            vsr = Ring(2)
            PM = self.sb(es, "rPM", [128, 2, 2, 4, 512], BF16)
            PMb = Buf()
            PMc = self.sb(es, "rPMc", [128, 2, 2, 256], BF16) if need_ctx else None
            EE = self.sb(es, "rEE", [128, 2, 2, 2, 512], BF16)
            EEb = Buf()
            mg = self.sb(es, "rmg", [128, 2, 512], BF16)
            mgr = Ring(2)
            mg2 = self.sb(es, "rmg2", [128, 2, 512], BF16)
            mg2r = Ring(2)
            Pm = self.sb(es, "rPm", [128, 4, 512], BF16)
            Pmr = Ring(4)
            st = self.sb(es, "rst", [128, 5, 512], F32)
            stb = [Buf() for _ in range(5)]
            rout = self.sb(es, "rout", [128, 2, 512], BF16)
            ror = Ring(2)
            wo = self.sb(es, "rwo", [128, 2, D], BF16)
            wob = Buf()
            ct = self.sb(es, "rct", [128, 4, 132], F32)
            ctb = Buf()
            sx = self.sb(es, "rsx", [128, 4, 72], F32)
            k.dma(k.pool, wo[:], wout[0:256, :].rearrange("(c p) f -> p c f", p=128), [], [wob], k.dsem("wo"))
            T = self.CF_("iota")
            for h in range(4):
                lgf = self.C_(f"lgf{l}", h, 1)
                lgb = self.C_(f"lgb{l}", h, 1)
                c1 = (ct[:, h, 0:56], [ctb])
                k.ts(c1, self.C_("sgf"), lgf, ALU.mult)
                k.stt(c1, self.C_("sgb"), lgb, c1, ALU.mult, ALU.add)
                k.tt((ct[:, h, 56:112], [ctb]), c1, self.C_("dlt"), ALU.mult)
                k.ts((ct[:, h, 112:120], [ctb]), self.C_("d1"), lgf, ALU.mult)
                k.ts((ct[:, h, 120:128], [ctb]), self.C_("d2"), lgb, ALU.mult)
                k.ts((ct[:, h, 128:129], [ctb]), lgb, -1.0, ALU.mult)
                k.actf((sx[:, h, :], [ctb]), (ct[:, h, 56:128], [ctb]), AF.Exp)

            def CT(h, a):
                return (ct[:, h, a:a + 1], [ctb])

            def SX(h, a):
                return (sx[:, h, a:a + 1], [ctb])

            for hp in range(2):
                for (src, off, n) in self.kv_src(2 + hp, 0, NKC * 128):
                    k.dma(k.sp, Kr[:, off:off + n], src, self.kvrd, [Kb], k.dsem("kload"))
                for pc in range(0, NKC * 128, 256):
                    vi, vb = vsr.next()
                    for (src, off, nn) in self.kv_src(4 + hp, pc, 256):
                        k.dma(k.sp, vst[:, vi, off:off + nn], src, self.kvrd, [vb], k.dsem(f"vst{vi}"))
                    for tl in range(2):
                        kc = pc // 128 + tl
                        pt = k.bbank()
                        k.transpose((pt[0][:, 0:128], pt[1]), (vst[:, vi, tl * 128:(tl + 1) * 128], [vb]), self.CB_("ident"))
                        k.copy((Vr[:, kc, 0:64], [Vb]), (pt[0][:, 0:64], pt[1]))
                        k.copy((Vr[:, kc, 64:128], [Vb]), (pt[0][:, 64:128], pt[1]))
                r1 = (st[:, 0, :], [stb[0]])
                r2 = (st[:, 1, :], [stb[1]])
                e1 = (st[:, 2, :], [stb[2]])
                dg = (st[:, 3, :], [stb[3]])
                for r in range(2):
                    for m in range(4):
                        dcol = self.C_("pmd", r * 4 + m, 1)
                        ncol = self.C_("pmdn", r * 4 + m, 1)
                        k.actf(r1, T, AF.Relu, scale=1.0, bias=dcol)
                        k.actf(r2, T, AF.Relu, scale=-1.0, bias=ncol)
                        k.ts(dg, T, ncol, ALU.is_equal)
                        for hh in range(2):
                            h = 2 * hp + hh
                            k.ts(e1, r1, self.C_(f"lgf{l}", h, 1), ALU.mult)
                            k.stt(e1, r2, self.C_(f"lgb{l}", h, 1), e1, ALU.mult, ALU.add)
                            k.actf(e1, e1, AF.Exp)
                            k.tt((PM[:, hh, r, m, :], [PMb]), e1, dg, ALU.add)
                    for hh in range(2):
                        h = 2 * hp + hh
                        for cls in range(2):
                            k.actf((EE[:, hh, r, cls, :], [EEb]), T, AF.Exp, scale=CT(h, r * 28 + (16 if cls == 0 else 11)))
                if need_ctx:
                    for m in range(2):
                        dl = -128.0 * m
                        r1c = (st[:, 0, 0:C], [stb[0]])
                        r2c = (st[:, 1, 0:C], [stb[1]])
                        e1c = (st[:, 2, 0:C], [stb[2]])
                        dgc = (st[:, 3, 0:C], [stb[3]])
                        Tc = (T[0][:, 0:C], T[1])
                        k.ts(r1c, Tc, dl, ALU.add, 0.0, ALU.max)
                        k.ts(r2c, Tc, -1.0, ALU.mult, -dl, ALU.add)
                        k.ts(r2c, r2c, 0.0, ALU.max)
                        k.ts(dgc, Tc, -dl, ALU.is_equal)
                        for hh in range(2):
                            h = 2 * hp + hh
                            k.ts(e1c, r1c, self.C_(f"lgf{l}", h, 1), ALU.mult)
                            k.stt(e1c, r2c, self.C_(f"lgb{l}", h, 1), e1c, ALU.mult, ALU.add)
                            k.actf(e1c, e1c, AF.Exp)
                            k.tt((PMc[:, hh, m, :], [PMb]), e1c, dgc, ALU.add)

                for bi, (t0, tn, mi) in enumerate(BLOCKS):
                    if mi == 1 and not need_ctx:
                        continue
                    kcs = list(range(NKC)) if mi == 0 else [0, 1]
                    qb_ = bi
                    q0 = t0
                    ai, acc = k.bank(hold=True)
                    for hh in range(2):
                        h = 2 * hp + hh
                        pr = slice(hh * 64, hh * 64 + 64)
                        q = self.Q(1, hp, bi, parts=pr)
                        pend = []

                        def score(kc):
                            _, sb_ = k.bank()
                            k.mm((sb_[0][:, :tn], sb_[1]), (Kr[pr, kc * 128:(kc + 1) * 128], [Kb]), q)
                            return sb_

                        def finish(kc, sb_, first, last):
                            pi, pb = Pmr.next()
                            p_ = (Pm[:, pi, :tn], [pb])
                            S_ = (sb_[0][:, :tn], sb_[1])
                            if mi == 1:
                                k.tt(p_, S_, (PMc[:, hh, kc, :tn], [PMb]), ALU.mult)
                            elif kc < 2:
                                i1, b1 = mgr.next()
                                m1 = (mg[:, i1, :tn], [b1])
                                i2, b2 = mg2r.next()
                                m2 = (mg2[:, i2, :tn], [b2])
                                k.actf(m1, (T[0][:, :tn], T[1]), AF.Exp, scale=self.C_(f"lgf{l}", h, 1), bias=CT(h, 112 + qb_ * 2 + kc))
                                k.actf(m2, (T[0][:, :tn], T[1]), AF.Exp, scale=CT(h, 128), bias=CT(h, 120 + qb_ * 2 + kc))
                                k.tt(m1, m1, m2, ALU.add)
                                k.tt(p_, S_, m1, ALU.mult)
                            else:
                                r = (kc - 2) // 16
                                dl = q0 - ((kc - 2) % 16) * 128
                                if -384 <= dl <= 0:
                                    k.tt(p_, S_, (PM[:, hh, r, (-dl) // 128, :tn], [PMb]), ALU.mult)
                                else:
                                    didx = dl // 128 + 15
                                    cls = 0 if dl >= 128 else 1
                                    k.stt(p_, S_, SX(h, r * 28 + didx), (EE[:, hh, r, cls, :tn], [EEb]), ALU.mult, ALU.mult)
                            k.mm((acc[0][pr, :tn], acc[1]), (Vr[:, kc, hh * 64:(hh + 1) * 64], [Vb]), p_, start=first, stop=last)

                        for kc in kcs:
                            pend.append((kc, score(kc)))
                            if len(pend) > 3:
                                kc0, s0 = pend.pop(0)
                                finish(kc0, s0, kc0 == kcs[0], False)
                        while pend:
                            kc0, s0 = pend.pop(0)
                            finish(kc0, s0, kc0 == kcs[0], len(pend) == 0)
                    o = (st[:, 0, :tn], [stb[0]])
                    sq = (st[:, 1, :tn], [stb[1]])
                    mean = (st[:, 2, :tn], [stb[2]])
                    msq = (st[:, 3, :tn], [stb[3]])
                    var = (st[:, 4, :tn], [stb[4]])
                    dd = (st[:, 1, :tn], [stb[1]])
                    k.actf(o, (acc[0][:, :tn], acc[1]), AF.Identity)
                    k.actf(sq, (acc[0][:, :tn], acc[1]), AF.Square)
                    k.release(ai)
                    _, b1 = k.bank()
                    k.mm((b1[0][:, :tn], b1[1]), self.CF_("bones"), o)
                    _, b2 = k.bank()
                    k.mm((b2[0][:, :tn], b2[1]), self.CF_("bones"), sq)
                    k.actf(mean, (b1[0][:, :tn], b1[1]), AF.Identity, scale=1.0 / 64)
                    k.tt(msq, mean, mean, ALU.mult)
                    k.stt(var, (b2[0][:, :tn], b2[1]), 1.0 / 64, msq, ALU.mult, ALU.subtract)
                    k.actf(var, var, AF.Sqrt, bias=self.epsap)
                    k.recip(var, var)
                    k.tt(dd, o, mean, ALU.subtract)
                    k.stt(dd, dd, self.C_(f"retn{l}", hp, 1), var, ALU.mult, ALU.mult)
                    ri_, rb_ = ror.next()
                    ro = (rout[:, ri_, :tn], [rb_])
                    k.tt(ro, dd, self.Q(2, hp, bi), ALU.mult)
                    for j in range(DC):
                        _, pd = k.bank()
                        k.mm((pd[0][:, :tn], pd[1]), (wo[:, hp, j * 128:(j + 1) * 128], [wob]), ro)
                        k.stt(self.H(j, bi), (pd[0][:, :tn], pd[1]), self.DER(1, 1, j, mi), self.H(j, bi), ALU.mult, ALU.add)
            k.barrier()

    def mix_phase(self, l, need_ctx):
        k = self.k
        k.barrier()
        wout = self.inp(f"wout{l}", [D, D])
        with contextlib.ExitStack() as es:
            Kt = self.sb(es, "aK", [128, NKC * 128], BF16)
            Ktb = Buf()
            Va = self.sb(es, "aV", [128, NKC, 128], BF16)
            Vab = Buf()
            E = self.sb(es, "aE", [128, 3, 512], BF16)
            Er = Ring(3)
            accs = self.sb(es, "aacc", [128, 512], F32)
            acb = Buf()
            aout = self.sb(es, "aout", [64, 2, 512], BF16)
            aob = Buf()
            woa = self.sb(es, "awo", [64, 2, D], BF16)
            woab = Buf()
            Kr = self.sb(es, "rK", [128, NKC * 128], BF16)
            Krb = Buf()
            Vr = self.sb(es, "rV", [128, NKC, 128], BF16)
            Vrb = Buf()
            vst = self.sb(es, "rvst", [128, 2, 256], BF16)
            vsr = Ring(2)
            PM = self.sb(es, "rPM", [128, 2, 2, 4, 512], BF16)
            PMb = Buf()
            PMc = self.sb(es, "rPMc", [128, 2, 2, 256], BF16) if need_ctx else None
            AP_ = self.sb(es, "rAp", [128, 6, 512], BF16)
            APb = Buf()
            QA = self.sb(es, "rQA", [128, 2, 512], BF16)
            QAr = Ring(2)
            Kbt = self.sb(es, "rKb", [128, 2, 128], BF16)
            Kbr = Ring(2)
            Wsum = self.sb(es, "rWs", [128, 4, 6, 64], F32)
            Wsb = Buf()
            Wsq = [[Buf() for _ in range(6)] for _ in range(4)]
            Wb = self.sb(es, "rWb", [128, 1, 6, 64], BF16)
            Wbr = Ring(1)
            c1p = self.sb(es, "rc1p", [128, 6, 4], F32)
            c1b = Buf()
            bcol = self.sb(es, "rbcol", [128, 6, 2], F32)
            sxp = self.sb(es, "rsxp", [128, 72], F32)
            sxb = Buf()
            Pm = self.sb(es, "rPm", [128, 3, 512], BF16)
            Pmr = Ring(3)
            st = self.sb(es, "rst", [128, 5, 512], F32)
            stb = [Buf() for _ in range(5)]
            rout = self.sb(es, "rout", [128, 512], BF16)
            rob = Buf()
            wor = self.sb(es, "rwo", [128, D], BF16)
            worb = Buf()
            ct = self.sb(es, "rct", [128, 4, 132], F32)
            ctb = Buf()
            sx = self.sb(es, "rsx", [128, 4, 72], F32)
            for (src, off, n) in self.kv_src(0, 0, NKC * 128):
                k.dma(k.sp, Kt[:, off:off + n], src, self.kvrd, [Ktb], k.dsem("kload"))
            k.memset((Va[:, :, 64:128], [Vab]), 1.0)
            T = self.CF_("iota")
            for h in range(4):
                lgf = self.C_(f"lgf{l}", h, 1)
                lgb = self.C_(f"lgb{l}", h, 1)
                c1 = (ct[:, h, 0:56], [ctb])
                k.ts(c1, self.C_("sgf"), lgf, ALU.mult)
                k.stt(c1, self.C_("sgb"), lgb, c1, ALU.mult, ALU.add)
                k.tt((ct[:, h, 56:112], [ctb]), c1, self.C_("dlt"), ALU.mult)
                k.ts((ct[:, h, 112:120], [ctb]), self.C_("d1"), lgf, ALU.mult)
                k.ts((ct[:, h, 120:128], [ctb]), self.C_("d2"), lgb, ALU.mult)
                k.ts((ct[:, h, 128:129], [ctb]), lgb, -1.0, ALU.mult)
                k.actf((sx[:, h, :], [ctb]), (ct[:, h, 56:128], [ctb]), AF.Exp)

            def CT(h, a):
                return (ct[:, h, a:a + 1], [ctb])

            def SX(h, a):
                return (sx[:, h, a:a + 1], [ctb])

            for p in range(2):
                k.dma(k.pool, woa[:], wout[512 + p * 128:512 + (p + 1) * 128, :].rearrange("(h p) f -> p h f", p=64), [], [woab], k.dsem("wo"))
                k.dma(k.pool, wor[:], wout[p * 128:(p + 1) * 128, :], [], [worb], k.dsem("wo"))
                for (src, off, n) in self.kv_src(2 + p, 0, NKC * 128):
                    k.dma(k.sp, Kr[:, off:off + n], src, self.kvrd, [Krb], k.dsem("kload"))
                for which in range(2):
                    for pc in range(0, NKC * 128, 256):
                        vi, vb = vsr.next()
                        for (src, off, nn) in self.kv_src(1 if which == 0 else 4 + p, pc, 256):
                            k.dma(k.sp, vst[:, vi, off:off + nn], src, self.kvrd, [vb], k.dsem(f"vst{vi}"))
                        for tl in range(2):
                            kc = pc // 128 + tl
                            pt = k.bbank()
                            k.transpose((pt[0][:, 0:128], pt[1]), (vst[:, vi, tl * 128:(tl + 1) * 128], [vb]), self.CB_("ident"))
                            if which == 0:
                                k.copy((Va[:, kc, 0:64], [Vab]), (pt[0][:, p * 64:(p + 1) * 64], pt[1]))
                            else:
                                k.copy((Vr[:, kc, 0:64], [Vrb]), (pt[0][:, 0:64], pt[1]))
                                k.copy((Vr[:, kc, 64:128], [Vrb]), (pt[0][:, 64:128], pt[1]))
                r1 = (st[:, 0, :], [stb[0]])
                r2 = (st[:, 1, :], [stb[1]])
                e1 = (st[:, 2, :], [stb[2]])
                dg = (st[:, 3, :], [stb[3]])
                for r in range(2):
                    for m in range(4):
                        dcol = self.C_("pmd", r * 4 + m, 1)
                        ncol = self.C_("pmdn", r * 4 + m, 1)
                        k.actf(r1, T, AF.Relu, scale=1.0, bias=dcol)
                        k.actf(r2, T, AF.Relu, scale=-1.0, bias=ncol)
                        k.ts(dg, T, ncol, ALU.is_equal)
                        for hh in range(2):
                            h = 2 * p + hh
                            k.ts(e1, r1, self.C_(f"lgf{l}", h, 1), ALU.mult)
                            k.stt(e1, r2, self.C_(f"lgb{l}", h, 1), e1, ALU.mult, ALU.add)
                            k.actf(e1, e1, AF.Exp)
                            k.tt((PM[:, hh, r, m, :], [PMb]), e1, dg, ALU.add)
                if need_ctx:
                    for m in range(2):
                        dl = -128.0 * m
                        r1c = (st[:, 0, 0:C], [stb[0]])
                        r2c = (st[:, 1, 0:C], [stb[1]])
                        e1c = (st[:, 2, 0:C], [stb[2]])
                        dgc = (st[:, 3, 0:C], [stb[3]])
                        Tc = (T[0][:, 0:C], T[1])
                        k.ts(r1c, Tc, dl, ALU.add, 0.0, ALU.max)
                        k.ts(r2c, Tc, -1.0, ALU.mult, -dl, ALU.add)
                        k.ts(r2c, r2c, 0.0, ALU.max)
                        k.ts(dgc, Tc, -dl, ALU.is_equal)
                        for hh in range(2):
                            h = 2 * p + hh
                            k.ts(e1c, r1c, self.C_(f"lgf{l}", h, 1), ALU.mult)
                            k.stt(e1c, r2c, self.C_(f"lgb{l}", h, 1), e1c, ALU.mult, ALU.add)
                            k.actf(e1c, e1c, AF.Exp)
                            k.tt((PMc[:, hh, m, :], [PMb]), e1c, dgc, ALU.add)

                T0 = (self.cf[:, CF["iota"][0]:CF["iota"][0] + 1], [self.cbuf])
                for hh in range(2):
                    h = 2 * p + hh
                    prs = slice(hh * 64, hh * 64 + 64)
                    k.copy((sxp[prs, :], [sxb]), (sx[prs, h, :], [ctb]))
                    for t in range(6):
                        if t < 4:
                            ci_ = (t // 2) * 28 + (16 if t % 2 == 0 else 11)
                            src = (ct[prs, h, ci_:ci_ + 1], [ctb])
                            srcf = (ct[:, h, ci_:ci_ + 1], [ctb])
                        elif t == 4:
                            o_ = CST[f"lgf{l}"][0] + h
                            src = (self.cst[prs, o_:o_ + 1], [self.cbuf])
                            srcf = (self.cst[:, o_:o_ + 1], [self.cbuf])
                        else:
                            src = (ct[prs, h, 128:129], [ctb])
                            srcf = (ct[:, h, 128:129], [ctb])
                        k.copy((c1p[prs, t, 0:1], [c1b]), src)
                        k.actf((bcol[:, t, hh:hh + 1], [c1b]), T0, AF.Exp, scale=srcf)
                for t in range(6):
                    k.ts((c1p[:, t, 1:2], [c1b]), (c1p[:, t, 0:1], [c1b]), -1.0, ALU.mult)
                    k.tt((c1p[:, t, 2:3], [c1b]), T0, (c1p[:, t, 1:2], [c1b]), ALU.mult)
                    k.actf((AP_[:, t, :], [APb]), T, AF.Exp, scale=(c1p[:, t, 0:1], [c1b]), bias=(c1p[:, t, 2:3], [c1b]))
                k.memset((Wsum[:], [Wsb] + [b_ for r_ in Wsq for b_ in r_]), 0.0)
                for kc in range(NKC):
                    pt = k.bbank()
                    k.transpose((pt[0][:, 0:128], pt[1]), (Kr[:, kc * 128:(kc + 1) * 128], [Krb]), self.CB_("ident"))
                    terms = [4, 5] if kc < 2 else [((kc - 2) // 16) * 2, ((kc - 2) // 16) * 2 + 1]
                    for t in terms:
                        uses = []
                        for qb in range(4):
                            if kc < 2:
                                uses.append((qb, (56 if t == 4 else 64) + qb * 2 + kc))
                            else:
                                r = (kc - 2) // 16
                                dl = qb * 512 - ((kc - 2) % 16) * 128
                                if -384 <= dl <= 0:
                                    continue
                                if (0 if dl >= 128 else 1) == t % 2:
                                    uses.append((qb, r * 28 + dl // 128 + 15))
                        if not uses:
                            continue
                        ki, kb_ = Kbr.next()
                        for hh in range(2):
                            k.ts((Kbt[:, ki, hh * 64:(hh + 1) * 64], [kb_]), (pt[0][:, hh * 64:(hh + 1) * 64], pt[1]),
                                 (bcol[:, t, hh:hh + 1], [c1b]), ALU.mult)
                        _, wb_ = k.bank()
                        for hh in range(2):
                            k.mm((wb_[0][hh * 64:(hh + 1) * 64, 0:64], wb_[1]), (Kbt[:, ki, hh * 64:(hh + 1) * 64], [kb_]),
                                 (Vr[:, kc, hh * 64:(hh + 1) * 64], [Vrb]), inc=(hh == 1))
                        for (qb, sidx) in uses:
                            k.stt((Wsum[:, qb, t, :], [Wsq[qb][t]]), (wb_[0][:, 0:64], wb_[1]), (sxp[:, sidx:sidx + 1], [sxb]),
                                  (Wsum[:, qb, t, :], [Wsq[qb][t]]), ALU.mult, ALU.add)

                def att_gen(bi):
                    t0, tn, mi = BLOCKS[bi]
                    kcs = list(range(NKC)) if mi == 0 else [0, 1]
                    pr = slice(p * 64, p * 64 + 64)
                    for j in range(2):
                        q = self.Q(0, j, bi, parts=pr)
                        ai, acc = k.bank(hold=True)
                        pend = []

                        def score(kc):
                            bi_, sb_ = k.bank(hold=True)
                            k.mm((sb_[0][:, :tn], sb_[1]), (Kt[pr, kc * 128:(kc + 1) * 128], [Ktb]), q)
                            return (bi_, sb_)

                        def finish(kc, sbh, first, last):
                            bi_, sb_ = sbh
                            ei, eb = Er.next()
                            e = (E[:, ei, :tn], [eb])
                            k.actf(e, (sb_[0][:, :tn], sb_[1]), AF.Exp, scale=0.125)
                            k.release(bi_)
                            k.mm((acc[0][:, :tn], acc[1]), (Va[:, kc, :], [Vab]), e, start=first, stop=last)

                        for kc in kcs:
                            pend.append((kc, score(kc)))
                            if len(pend) > 3:
                                kc0, s0 = pend.pop(0)
                                finish(kc0, s0, kc0 == kcs[0], False)
                                yield
                        while pend:
                            kc0, s0 = pend.pop(0)
                            finish(kc0, s0, kc0 == kcs[0], len(pend) == 0)
                            yield
                        a_ = (accs[:, :tn], [acb])
                        k.actf(a_, (acc[0][:, :tn], acc[1]), AF.Identity)
                        k.release(ai)
                        _, dn = k.bank()
                        k.mm((dn[0][0:64, :tn], dn[1]), self.CF_("shift"), a_)
                        rd = (st[0:64, 4, :tn], [stb[4]])
                        k.recip(rd, (dn[0][0:64, :tn], dn[1]))
                        k.tt((aout[:, j, :tn], [aob]), (accs[0:64, :tn], [acb]), rd, ALU.mult)
                        yield
                    for jj in range(DC):
                        _, pd = k.bank()
                        for j in range(2):
                            k.mm((pd[0][:, :tn], pd[1]), (woa[:, j, jj * 128:(jj + 1) * 128], [woab]), (aout[:, j, :tn], [aob]),
                                 start=(j == 0), stop=(j == 1), inc=(j == 1))
                        k.stt(self.H(jj, bi), (pd[0][:, :tn], pd[1]), self.DER(1, 1, jj, mi), self.H(jj, bi), ALU.mult, ALU.add)
                        yield

                def ret_gen(bi):
                    t0, tn, mi = BLOCKS[bi]
                    q0 = t0
                    ai, acc = k.bank(hold=True)
                    started = [False, False]
                    work = []
                    if mi == 1:
                        for hh in range(2):
                            for kc in (0, 1):
                                work.append((hh, kc))
                    else:
                        wi_, wbb = Wbr.next()
                        k.copy((Wb[:, wi_].rearrange("p a b -> p (a b)"), [wbb]), (Wsum[:, bi].rearrange("p a b -> p (a b)"), [Wsb] + Wsq[bi]))
                        for t in range(6):
                            qi_, qab = QAr.next()
                            qa = (QA[:, qi_, :tn], [qab])
                            k.tt(qa, self.Q(1, p, bi), (AP_[:, t, :tn], [APb]), ALU.mult)
                            for hh in range(2):
                                pr = slice(hh * 64, hh * 64 + 64)
                                k.mm((acc[0][pr, :tn], acc[1]), (Wb[pr, wi_, t, :], [wbb]), (QA[pr, qi_, :tn], [qab]),
                                     start=(not started[hh]), stop=False)
                                started[hh] = True
                            yield
                        for hh in range(2):
                            for kc in range(2, NKC):
                                dl = q0 - ((kc - 2) % 16) * 128
                                if -384 <= dl <= 0:
                                    work.append((hh, kc))
                    last_of = {}
                    for (hh, kc) in work:
                        last_of[hh] = kc
                    def rscore(hh, kc):
                        pr = slice(hh * 64, hh * 64 + 64)
                        sbi_, sb_ = k.bank(hold=True)
                        k.mm((sb_[0][:, :tn], sb_[1]), (Kr[pr, kc * 128:(kc + 1) * 128], [Krb]), self.Q(1, p, bi, parts=pr))
                        return (sbi_, sb_)

                    def rfinish(hh, kc, sbh):
                        sbi_, sb_ = sbh
                        pr = slice(hh * 64, hh * 64 + 64)
                        pi, pb = Pmr.next()
                        p_ = (Pm[:, pi, :tn], [pb])
                        S_ = (sb_[0][:, :tn], sb_[1])
                        if mi == 1:
                            k.tt(p_, S_, (PMc[:, hh, kc, :tn], [PMb]), ALU.mult)
                        else:
                            r = (kc - 2) // 16
                            k.tt(p_, S_, (PM[:, hh, r, (-(q0 - ((kc - 2) % 16) * 128)) // 128, :tn], [PMb]), ALU.mult)
                        k.release(sbi_)
                        k.mm((acc[0][pr, :tn], acc[1]), (Vr[:, kc, hh * 64:(hh + 1) * 64], [Vrb]), p_,
                             start=(not started[hh]), stop=(last_of[hh] == kc))
                        started[hh] = True

                    rp = []
                    for (hh, kc) in work:
                        rp.append((hh, kc, rscore(hh, kc)))
                        if len(rp) > 2:
                            a_, b_, c_ = rp.pop(0)
                            rfinish(a_, b_, c_)
                            yield
                    while rp:
                        a_, b_, c_ = rp.pop(0)
                        rfinish(a_, b_, c_)
                        yield
                    o = (st[:, 0, :tn], [stb[0]])
                    sq = (st[:, 1, :tn], [stb[1]])
                    mean = (st[:, 2, :tn], [stb[2]])
                    msq = (st[:, 3, :tn], [stb[3]])
                    var = (st[:, 4, :tn], [stb[4]])
                    dd = (st[:, 1, :tn], [stb[1]])
                    k.actf(o, (acc[0][:, :tn], acc[1]), AF.Identity)
                    k.actf(sq, (acc[0][:, :tn], acc[1]), AF.Square)
                    k.release(ai)
                    _, b1 = k.bank()
                    k.mm((b1[0][:, :tn], b1[1]), self.CF_("bones"), o)
                    k.actf(mean, (b1[0][:, :tn], b1[1]), AF.Identity, scale=1.0 / 64)
                    _, b2 = k.bank()
                    k.mm((b2[0][:, :tn], b2[1]), self.CF_("bones"), sq)
                    k.tt(msq, mean, mean, ALU.mult)
                    k.stt(var, (b2[0][:, :tn], b2[1]), 1.0 / 64, msq, ALU.mult, ALU.subtract)
                    k.actf(var, var, AF.Sqrt, bias=self.epsap)
                    k.recip(var, var)
                    k.tt(dd, o, mean, ALU.subtract)
                    k.stt(dd, dd, self.C_(f"retn{l}", p, 1), var, ALU.mult, ALU.mult)
                    ro = (rout[:, :tn], [rob])
                    k.tt(ro, dd, self.Q(2, p, bi), ALU.mult)
                    yield
                    for jj in range(DC):
                        _, pd = k.bank()
                        k.mm((pd[0][:, :tn], pd[1]), (wor[:, jj * 128:(jj + 1) * 128], [worb]), ro)
                        k.stt(self.H(jj, bi), (pd[0][:, :tn], pd[1]), self.DER(1, 1, jj, mi), self.H(jj, bi), ALU.mult, ALU.add)
                        yield

                for bi, (t0, tn, mi) in enumerate(BLOCKS):
                    if mi == 1 and not need_ctx:
                        continue
                    for g in (ret_gen(bi), att_gen(bi)):
                        for _ in g:
                            pass
            k.barrier()

    def fft_phase(self, l, need_ctx):
        k = self.k
        k.barrier()
        wout = self.inp(f"wout{l}", [D, D])
        dft = self.inp("dft", [S, 2, SL], BF16).rearrange("(nt p) c k -> p nt c k", p=128)
        with contextlib.ExitStack() as es:
            AB = self.sb(es, "fAB", [128, 32, 2, 256], BF16)
            ABb = Buf()
            fst = self.sb(es, "fst", [128, 2, 2, 512], BF16)
            fsr = Ring(2)
            tr = self.sb(es, "ftr", [128, 6, 2, 512], BF16)
            trr = Ring(6)
            fo = self.sb(es, "fo", [128, 2, 512], BF16)
            fob = Buf()
            wo = self.sb(es, "fwo", [128, 2, D], BF16)
            wob = Buf()
            k.dma(k.pool, wo[:], wout[256:512, :].rearrange("(c p) f -> p c f", p=128), [], [wob], k.dsem("wo"))

            def build_ab(dst, dbuf, srcs, ntok):
                for pc in range(0, ntok, 512):
                    n = min(512, ntok - pc)
                    fi, fb = fsr.next()
                    for ci in range(2):
                        for (src, off, nn) in srcs(ci, pc, n):
                            k.dma(k.sp, fst[:, fi, ci, off:off + nn], src, self.kvrd, [fb], k.dsem(f"vst{fi}"))
                    for tl in range(n // 128):
                        tt_ = pc // 128 + tl
                        for ci in range(2):
                            _, bk = k.bank()
                            k.mm((bk[0][:, 0:256], bk[1]), (fst[:, fi, ci, tl * 128:(tl + 1) * 128], [fb]), self.CB_("fd"))
                            k.actf((dst[:, tt_, ci, :], [dbuf]), (bk[0][:, 0:256], bk[1]), AF.Identity)

            def lat_src(ci, c0, n):
                out = []
                for (s0, t) in ((0, self.kva[1][(2 + ci) * 128:(3 + ci) * 128, :]), (SL, self.kva[1][(6 + ci) * 128:(7 + ci) * 128, :])):
                    a, b = max(c0, s0), min(c0 + n, s0 + SL)
                    if a < b:
                        out.append((t[:, a - s0:b - s0], a - c0, b - a))
                return out

            build_ab(AB, ABb, lat_src, S)
            for kb in range(4):
                t0, tn, mi = BLOCKS[kb]
                a0, acc0 = k.bank(hold=True)
                a1, acc1 = k.bank(hold=True)
                accs = [acc0, acc1]
                for nt_ in range(32):
                    ti, tb = trr.next()
                    k.dma(k.sp, tr[:, ti], dft[:, nt_, :, kb * 512:(kb + 1) * 512], [], [tb], k.dsem(f"dft{ti}"))
                    for c in range(2):
                        k.mm((accs[c][0], accs[c][1]), (AB[:, nt_, c, 0:128], [ABb]), (tr[:, ti, 0, :], [tb]),
                             start=(nt_ == 0), stop=False, inc=False)
                        k.mm((accs[c][0], accs[c][1]), (AB[:, nt_, c, 128:256], [ABb]), (tr[:, ti, 1, :], [tb]),
                             start=False, stop=(nt_ == 31), inc=(c == 1))
                for c in range(2):
                    k.actf((fo[:, c, :], [fob]), accs[c], AF.Identity)
                k.release(a0)
                k.release(a1)
                for j in range(DC):
                    _, pd = k.bank()
                    for c in range(2):
                        k.mm((pd[0][:, :tn], pd[1]), (wo[:, c, j * 128:(j + 1) * 128], [wob]), (fo[:, c, :tn], [fob]),
                             start=(c == 0), stop=(c == 1), inc=(c == 1))
                    k.stt(self.H(j, kb), (pd[0][:, :tn], pd[1]), self.DER(1, 1, j, mi), self.H(j, kb), ALU.mult, ALU.add)
            if need_ctx:
                dftc = self.inp("dftc", [C, 2, C], BF16).rearrange("(nt p) c k -> p nt c k", p=128)
                ABc = self.sb(es, "fABc", [128, 2, 2, 256], BF16)
                ABcb = Buf()
                trc = self.sb(es, "ftrc", [128, 2, 2, C], BF16)
                trcb = Buf()
                k.dma(k.sp, trc[:], dftc, [], [trcb], k.dsem("dft0"))
                build_ab(ABc, ABcb, lambda ci, c0, n: [(self.kvc[(6 + ci) * 128:(7 + ci) * 128, c0:c0 + n], 0, n)], C)
                t0, tn, mi = BLOCKS[4]
                a0, acc0 = k.bank(hold=True)
                a1, acc1 = k.bank(hold=True)
                accs = [acc0, acc1]
                for nt_ in range(2):
                    for c in range(2):
                        k.mm((accs[c][0][:, :tn], accs[c][1]), (ABc[:, nt_, c, 0:128], [ABcb]), (trc[:, nt_, 0, :], [trcb]),
                             start=(nt_ == 0), stop=False, inc=False)
                        k.mm((accs[c][0][:, :tn], accs[c][1]), (ABc[:, nt_, c, 128:256], [ABcb]), (trc[:, nt_, 1, :], [trcb]),
                             start=False, stop=(nt_ == 1), inc=(c == 1))
                for c in range(2):
                    k.actf((fo[:, c, :tn], [fob]), (accs[c][0][:, :tn], accs[c][1]), AF.Identity)
                k.release(a0)
                k.release(a1)
                for j in range(DC):
                    _, pd = k.bank()
                    for c in range(2):
                        k.mm((pd[0][:, :tn], pd[1]), (wo[:, c, j * 128:(j + 1) * 128], [wob]), (fo[:, c, :tn], [fob]),
                             start=(c == 0), stop=(c == 1), inc=(c == 1))
                    k.stt(self.H(j, 4), (pd[0][:, :tn], pd[1]), self.DER(1, 1, j, mi), self.H(j, 4), ALU.mult, ALU.add)
            k.barrier()

    def final_norm(self):
        k = self.k
        k.barrier()
        y = self.outp("yT", [D, SL])
        with contextlib.ExitStack() as es:
            sq8 = self.sb(es, "sq8", [128, 3, 512], BF16)
            sqr = Ring(3)
            rs = self.sb(es, "rs", [128, 2, 512], F32)
            rsr = Ring(2)
            tt = self.sb(es, "mtt", [128, 2, 512], F32)
            ttr = Ring(2)
            yo = self.sb(es, "yo", [128, 3, 512], F32)
            yr = Ring(3)
            for bi, (t0, tn, mi) in enumerate(LAT):
                _, bk = k.bank()
                for dc in range(DC):
                    qi, qbf = sqr.next()
                    k.actf((sq8[:, qi, :tn], [qbf]), self.H(dc, bi), AF.Square)
                    k.mm((bk[0][:, :tn], bk[1]), self.CB_("ones"), (sq8[:, qi, :tn], [qbf]), start=(dc == 0),
                         stop=(dc == DC - 1))
                ri, rbuf = rsr.next()
                r = (rs[:, ri, :tn], [rbuf])
                k.actf(r, (bk[0][:, :tn], bk[1]), AF.Sqrt, scale=1.0 / D, bias=self.epsap)
                k.recip(r, r)
                for dc in range(DC):
                    yi, yb = yr.next()
                    o = (yo[:, yi, :tn], [yb])
                    k.stt(o, self.H(dc, bi), self.C_("fng", dc, 1), r, ALU.mult, ALU.mult)
                    k.dma(k.sp, y[dc * 128:(dc + 1) * 128, t0:t0 + tn], o[0], o[1], [], k.dsem("yout"))
            k.barrier()


def build_prog(seg, fused=False, phases=None):
    p = Prog(seg, fused)
    if phases is not None:
        p.phases = phases
    p.epsap = EPS
    nc = p.build()
    return p, nc


_PROGS = {}


def get_prog(seg):
    if seg not in _PROGS:
        _PROGS[seg] = build_prog(seg)
    return _PROGS[seg]


def kernel(**inp):
    inp = {k_: np.asarray(v) for k_, v in inp.items()}
    cf, cb = host_static()
    cores = list(range(8))
    csts = [host_consts(inp, c // 2, c % 2) for c in cores]
    ropes = [host_rope(s) for s in range(2)]
    wext = [host_w_in_ext(inp["w_in"][l]) for l in range(2)]
    wsT = [np.ascontiguousarray(np.transpose(inp["gmlp_w_s"][l], (2, 0, 1)).reshape(128, 512), np.float32) for l in range(2)]
    adaw = [np.ascontiguousarray(inp["ada_w"][l]) for l in range(2)]

    def base(c):
        return {"cst": csts[c], "cf": cf, "cb": cb}

    def full(name, c, state):
        b, s = c // 2, c % 2
        if name in ("cst", "cf", "cb"):
            return base(c)[name]
        if name == "h0":
            return np.ascontiguousarray(np.concatenate([inp["x"][b, s * SL:(s + 1) * SL].T, inp["ctx"][b].T], axis=1), np.float32)
        if name == "rope":
            return ropes[s]
        if name == "dft":
            return host_dft(s)[0]
        if name == "dftc":
            return host_dft(s)[1]
        for l in range(2):
            if name == f"ada_w{l}":
                return adaw[l]
            if name == f"win{l}":
                return wext[l]
            if name == f"wout{l}":
                return np.ascontiguousarray(inp["w_out"][l])
            if name == f"wsT{l}":
                return wsT[l]
            for w in (1, 2):
                if name == f"wgu{w}_{l}":
                    return np.ascontiguousarray(inp[f"ffn{w}_w_gu"][l])
                if name == f"wd{w}_{l}":
                    return np.ascontiguousarray(inp[f"ffn{w}_w_down"][l])
        if name.startswith("i_st_"):
            return state[c]["o_st_" + name[5:]]
        if name == "kvf_own":
            return state[c]["kvf_loc"]
        if name == "kvf_oth":
            return state[c ^ 1]["kvf_loc"]
        if name == "kvf_ctx_in":
            return state[c]["kvf_ctx"]
        raise KeyError(name)

    p, nc = get_prog(0)
    in_maps = [{n: full(n, c, None) for n in p.in_names} for c in cores]
    res = run_bass_kernel_spmd(nc, in_maps, core_ids=cores)
    state = res.results
    out = np.empty((4, S, D), np.float32)
    for c in cores:
        b, s = c // 2, c % 2
        out[b, s * SL:(s + 1) * SL, :] = state[c]["yT"].T
    return out
```

```python
import contextlib
import os
import math
import numpy as np
import ml_dtypes
import concourse.bass as bass
import concourse.mybir as mybir
from concourse.bass_utils import run_bass_kernel_spmd

F32 = mybir.dt.float32
BF16 = mybir.dt.bfloat16
AF = mybir.ActivationFunctionType
ALU = mybir.AluOpType
NPBF = ml_dtypes.bfloat16

D = 1024
DC = 8
S = 4096
SL = 2048
C = 256
NT = SL + C
DFF = 2816
EPS = 1e-6
BLOCKS = [(0, 512, 0), (512, 512, 0), (1024, 512, 0), (1536, 512, 0), (2048, 256, 1)]
LAT = BLOCKS[:4]
NKC = 34
FFN_GENS = [(0, 3), (3, 3), (6, 3), (9, 3), (12, 3), (15, 3), (18, 3), (21, 1)]
SAME_ENG_SYNC = True

CST = {}
_off = 0


def _c(name, w):
    global _off
    CST[name] = (_off, w)
    _off += w


for _l in range(2):
    _c(f"adab{_l}", 72)
    _c(f"g1_{_l}", 8)
    _c(f"gm_{_l}", 8)
    _c(f"g2_{_l}", 8)
    _c(f"retn{_l}", 2)
    _c(f"gmn{_l}", 2)
    _c(f"aqn{_l}", 1)
    _c(f"aqs{_l}", 1)
    _c(f"akn{_l}", 1)
    _c(f"aks{_l}", 1)
    _c(f"lgf{_l}", 4)
    _c(f"lgb{_l}", 4)
    _c(f"bt{_l}", 256)
_c("fng", 8)
_c("cc", 16)
_c("pmd", 8)
_c("pmdn", 8)
_c("sgf", 56)
_c("sgb", 56)
_c("dlt", 56)
_c("d1", 8)
_c("d2", 8)
NCST = _off

CF = {"ones": (0, 128), "bones": (128, 128), "shift": (256, 64), "iota": (320, 512)}
NCF = 832
CB = {"ident": (0, 128), "ones": (128, 128), "bones": (256, 128), "fd": (384, 256)}
NCB = 640


def _fm(v):
    v = np.asarray(v, np.float32).reshape(-1, 128)
    return np.ascontiguousarray(v.T)


def _swap32(a, axis=-1):
    a = np.moveaxis(a, axis, -1)
    sh = a.shape
    b = a.reshape(sh[:-1] + (sh[-1] // 64, 2, 32))[..., ::-1, :].reshape(sh)
    return np.moveaxis(b, -1, axis)


def host_consts(inp, b, s):
    cst = np.zeros((128, NCST), np.float32)

    def put(name, arr):
        o, w = CST[name]
        arr = np.asarray(arr, np.float32)
        assert arr.shape == (128, w), (name, arr.shape)
        cst[:, o:o + w] = arr

    for l in range(2):
        put(f"adab{l}", _fm(inp["ada_b"][l]))
        put(f"g1_{l}", _fm(inp["norm_ffn1"][l]))
        put(f"gm_{l}", _fm(inp["norm_mix"][l]))
        put(f"g2_{l}", _fm(inp["norm_ffn2"][l]))
        put(f"retn{l}", _fm(inp["ret_norm"][l]))
        put(f"gmn{l}", _fm(inp["gmlp_norm"][l]))
        qn = np.asarray(inp["att_q_norm"][l], np.float32)
        kn = np.asarray(inp["att_k_norm"][l], np.float32)
        put(f"aqn{l}", np.tile(qn, 2)[:, None])
        put(f"aqs{l}", np.tile(_swap32(qn), 2)[:, None])
        put(f"akn{l}", np.tile(kn, 2)[:, None])
        put(f"aks{l}", np.tile(_swap32(kn), 2)[:, None])
        put(f"lgf{l}", np.broadcast_to(np.asarray(inp["ret_log_decay_fwd"][l], np.float32)[None, :], (128, 4)))
        put(f"lgb{l}", np.broadcast_to(np.asarray(inp["ret_log_decay_bwd"][l], np.float32)[None, :], (128, 4)))
        bs = np.asarray(inp["gmlp_b_s"][l], np.float32)
        bt = np.zeros((128, 2, 128), np.float32)
        for g in range(4):
            bt[(g % 2) * 64:(g % 2) * 64 + 64, g // 2, :] = bs[g][None, :]
        put(f"bt{l}", bt.reshape(128, 256))
    put("fng", _fm(inp["final_norm"]))
    cc = np.zeros((128, 8, 2), np.float32)
    cc[:, :, 0] = _fm(inp["c"][b])
    cc[:, :, 1] = _fm(inp["c_ctx"])
    put("cc", cc.reshape(128, 16))
    pmd = np.array([(s - r) * 2048 - 128 * m for r in range(2) for m in range(4)], np.float32)
    put("pmd", np.broadcast_to(pmd[None, :], (128, 8)))
    put("pmdn", np.broadcast_to(-pmd[None, :], (128, 8)))
    dlt = np.array([(s - r) * 2048 + 128 * (di - 15) for r in range(2) for di in range(28)], np.float32)
    sgf = (dlt > 0).astype(np.float32)
    put("dlt", np.broadcast_to(dlt[None, :], (128, 56)))
    put("sgf", np.broadcast_to(sgf[None, :], (128, 56)))
    put("sgb", np.broadcast_to((sgf - 1.0)[None, :], (128, 56)))
    d1 = np.zeros(8, np.float32)
    d2 = np.zeros(8, np.float32)
    for qb in range(4):
        for kc in range(2):
            d1[qb * 2 + kc] = s * 2048 + qb * 512 + 256 - kc * 128
            d2[qb * 2 + kc] = 4096 - s * 2048 - qb * 512 + kc * 128
    put("d1", np.broadcast_to(d1[None, :], (128, 8)))
    put("d2", np.broadcast_to(d2[None, :], (128, 8)))
    return cst


def host_static():
    cf = np.zeros((128, NCF), np.float32)
    cf[:, 0:128] = 1.0
    bo = np.zeros((128, 128), np.float32)
    bo[0:64, 0:64] = 1.0
    bo[64:128, 64:128] = 1.0
    cf[:, 128:256] = bo
    sh = np.zeros((128, 64), np.float32)
    sh[64 + np.arange(64), np.arange(64)] = 1.0
    cf[:, 256:320] = sh
    cf[:, 320:832] = np.arange(512, dtype=np.float32)[None, :] - np.arange(128, dtype=np.float32)[:, None]
    cb = np.zeros((128, NCB), np.float32)
    cb[:, 0:128] = np.eye(128)
    cb[:, 128:256] = 1.0
    cb[:, 256:384] = bo
    de = np.outer(np.arange(64), np.arange(64)).astype(np.float64) * (2 * np.pi / 64)
    c64, s64 = np.cos(de), np.sin(de)
    fd = np.zeros((128, 256))
    fd[0:64, 0:64] = c64
    fd[64:128, 64:128] = c64
    fd[0:64, 128:192] = s64
    fd[64:128, 192:256] = s64
    cb[:, 384:640] = fd
    return cf, cb.astype(NPBF)


def host_rope(s):
    p = np.arange(128)
    f = p % 32
    sign = np.where((p % 64) < 32, -1.0, 1.0)[:, None]
    idx = s * SL + np.arange(SL)
    row = (idx // 64).astype(np.float64)
    col = (idx % 64).astype(np.float64)
    ax_freq = 10000.0 ** (-np.arange(16, dtype=np.float64) / 16)
    ang = np.concatenate([row[:, None] * ax_freq, col[:, None] * ax_freq], -1)
    angA = ang[:, f].T
    ret_freq = 1.0 / (10000.0 ** np.linspace(0.0, 1.0, 32))
    angR = ((C + idx)[:, None] * ret_freq)[:, f].T
    angRc = (np.arange(C)[:, None] * ret_freq)[:, f].T
    t = np.zeros((128, 4, NT), np.float32)
    t[:, 0, :SL] = np.cos(angA)
    t[:, 1, :SL] = np.sin(angA) * sign
    t[:, 0, SL:] = 1.0
    t[:, 2, :SL] = np.cos(angR)
    t[:, 3, :SL] = np.sin(angR) * sign
    t[:, 2, SL:] = np.cos(angRc)
    t[:, 3, SL:] = np.sin(angRc) * sign
    return t


_DFT_CACHE = {}


def host_dft(s):
    if s in _DFT_CACHE:
        return _DFT_CACHE[s]
    n = np.arange(S).astype(np.int64)
    kk = (s * SL + np.arange(SL)).astype(np.int64)
    ph = (np.outer(n, kk) % S).astype(np.float64) * (2 * np.pi / S)
    t = np.empty((S, 2, SL), NPBF)
    t[:, 0, :] = (np.cos(ph) / 512.0).astype(NPBF)
    t[:, 1, :] = (-np.sin(ph) / 512.0).astype(NPBF)
    ph = np.outer(np.arange(C), np.arange(C)).astype(np.float64) * (2 * np.pi / C)
    tc = np.empty((C, 2, C), NPBF)
    tc[:, 0, :] = (np.cos(ph) / 128.0).astype(NPBF)
    tc[:, 1, :] = (-np.sin(ph) / 128.0).astype(NPBF)
    _DFT_CACHE[s] = (t, tc)
    return t, tc


def host_w_in_ext(w):
    def sw(x):
        return _swap32(x, axis=1)
    retq, retk, retv, retg = w[:, 0:256], w[:, 256:512], w[:, 512:768], w[:, 768:1024]
    fnet = w[:, 1024:1280]
    aq = w[:, 1280:1536].reshape(1024, 4, 64)
    aq = np.concatenate([aq[:, 0], aq[:, 2], aq[:, 1], aq[:, 3]], axis=1)
    ak, av = w[:, 1536:1664], w[:, 1664:1792]
    gu, gv = w[:, 1792:2048], w[:, 2048:2304]
    return np.ascontiguousarray(np.concatenate(
        [retq, sw(retq), retk, sw(retk), retv, retg, fnet, aq, sw(aq), ak, sw(ak), av, gu, gv], axis=1), np.float32)


class Sem:
    def __init__(self, h, dma=False):
        self.h = h
        self.n = 0
        self.dma = dma


class Eng:
    def __init__(self, name, e, sem):
        self.name = name
        self.e = e
        self.sem = sem
        self.waited = {}
        self.pending = False


class Buf:
    __slots__ = ("w", "r", "name", "small")

    def __init__(self, name=""):
        self.w = {}
        self.r = {}
        self.name = name
        self.small = {}


class KB:
    def __init__(self, nc, es):
        self.nc = nc
        self.es = es
        self.sems = []
        self.pe = Eng("pe", nc.tensor, self.newsem("pe"))
        self.act = Eng("act", nc.scalar, self.newsem("act"))
        self.dve = Eng("dve", nc.vector, self.newsem("dve"))
        self.pool = Eng("pool", nc.gpsimd, None)
        self.sp = Eng("sp", nc.sync, None)
        self.engs = [self.pe, self.act, self.dve, self.pool, self.sp]
        self.ps = es.enter_context(nc.psum_tensor("ps", [128, 6, 512], F32))
        self.psb = es.enter_context(nc.psum_tensor("psb", [128, 2, 1024], BF16))
        self.banks = [Buf(f"bank{i}") for i in range(6)]
        self.bbanks = [Buf(f"bbank{i}") for i in range(2)]
        self.bptr = 0
        self.bbptr = 0
        self.held = set()
        self.dsems = {}

    def newsem(self, name, dma=False):
        s = Sem(self.es.enter_context(self.nc.semaphore(name)), dma)
        self.sems.append(s)
        return s

    def dsem(self, name):
        if name not in self.dsems or self.dsems[name].n > 2400:
            self.nds = getattr(self, "nds", 0) + 1
            self.dsems[name] = self.newsem(f"d{self.nds}_" + name, dma=True)
        return self.dsems[name]

    def bank(self, hold=False):
        for _ in range(8):
            i = self.bptr
            self.bptr = (self.bptr + 1) % 6
            if i not in self.held:
                if hold:
                    self.held.add(i)
                return i, (self.ps[:, i, :], [self.banks[i]])
        raise RuntimeError("no bank")

    def release(self, i):
        self.held.discard(i)

    def bbank(self):
        i = self.bbptr
        self.bbptr = (self.bbptr + 1) % 2
        return (self.psb[:, i, :], [self.bbanks[i]])

    def _wait(self, E, reads, writes):
        deps = {}
        for b in reads:
            for s, v in b.w.items():
                if deps.get(s, 0) < v:
                    deps[s] = v
        for b in writes:
            for s, v in b.w.items():
                if deps.get(s, 0) < v:
                    deps[s] = v
            for s, v in b.r.items():
                if deps.get(s, 0) < v:
                    deps[s] = v
        for s, v in deps.items():
            if s is E.sem:
                if E is self.pe or not SAME_ENG_SYNC:
                    continue
                if v > s.n:
                    continue
                if E is self.dve:
                    sm = 0
                    for b in list(reads) + list(writes):
                        sm = max(sm, b.small.get(s, 0))
                    if sm == 0:
                        continue
                    v = min(v, sm)
            if s.dma:
                v = s.n
            elif s is not E.sem:
                sm = False
                for b in list(reads) + list(writes):
                    if b.small.get(s, 0) == v:
                        sm = True
                        break
                if sm:
                    v = max(v, min(v + 1, s.n))
            if E.waited.get(s, 0) < v:
                E.e.wait_ge(s.h, v)
                E.waited[s] = v

    def op(self, E, fn, reads, writes, inc=True, small=True):
        if E is not self.pe:
            assert not self.pe.pending
        self._wait(E, reads, writes)
        ins = fn()
        if inc:
            E.sem.n += 1
            ins.then_inc(E.sem.h, 1)
            ev = E.sem.n
            E.pending = False
        else:
            assert E is self.pe
            ev = E.sem.n + 1
            E.pending = True
        s = E.sem
        for b in reads:
            if b.r.get(s, 0) < ev:
                b.r[s] = ev
        for b in writes:
            if b.w.get(s, 0) < ev:
                b.w[s] = ev
            if small:
                b.small[s] = ev
        return ins

    def dma(self, E, out_ap, in_ap, reads, writes, sem):
        assert not self.pe.pending
        self._wait(E, reads, writes)
        ins = E.e.dma_start(out=out_ap, in_=in_ap)
        sem.n += 16
        ins.then_inc(sem.h, 16)
        for b in reads:
            b.r[sem] = sem.n
        for b in writes:
            b.w[sem] = sem.n
        return ins

    def barrier(self):
        assert not self.pe.pending
        for E in self.engs:
            for s in self.sems:
                if s is E.sem:
                    continue
                if E.waited.get(s, 0) < s.n:
                    E.e.wait_ge(s.h, s.n)
                    E.waited[s] = s.n
        for E in (self.pe, self.act, self.dve):
            if E.sem.n > 1500:
                self.nes = getattr(self, "nes", 0) + 1
                E.sem = self.newsem(f"{E.name}{self.nes}")

    def mm(self, out, lhsT, rhs, start=True, stop=True, inc=True):
        return self.op(self.pe, lambda: self.nc.tensor.matmul(out[0], lhsT[0], rhs[0], start=start, stop=stop),
                       lhsT[1] + rhs[1], out[1], inc=inc, small=self._small(out[0]))

    def transpose(self, out, in_, ident):
        return self.op(self.pe, lambda: self.nc.tensor.transpose(out[0], in_[0], ident[0]),
                       in_[1] + ident[1], out[1], small=True)

    def actf(self, out, in_, func, scale=1.0, bias=0.0):
        rd = list(in_[1])
        sc, bi = scale, bias
        if isinstance(scale, tuple):
            rd += scale[1]
            sc = scale[0]
        if isinstance(bias, tuple):
            rd += bias[1]
            bi = bias[0]
        return self.op(self.act, lambda: self.nc.scalar.activation(out=out[0], in_=in_[0], func=func, bias=bi, scale=sc),
                       rd, out[1], small=self._small(out[0]))

    @staticmethod
    def _small(ap):
        n = 1
        for d_ in list(ap.shape)[1:]:
            n *= int(d_)
        return n < 256

    def tt(self, out, a, b, op):
        return self.op(self.dve, lambda: self.nc.vector.tensor_tensor(out=out[0], in0=a[0], in1=b[0], op=op),
                       a[1] + b[1], out[1], small=self._small(out[0]))

    def ts(self, out, a, s1, op0, s2=None, op1=None):
        rd = list(a[1])
        v1, v2 = s1, s2
        if isinstance(s1, tuple):
            rd += s1[1]
            v1 = s1[0]
        if isinstance(s2, tuple):
            rd += s2[1]
            v2 = s2[0]
        if op1 is None:
            return self.op(self.dve, lambda: self.nc.vector.tensor_scalar(out=out[0], in0=a[0], scalar1=v1, scalar2=None, op0=op0),
                           rd, out[1], small=self._small(out[0]))
        return self.op(self.dve, lambda: self.nc.vector.tensor_scalar(out=out[0], in0=a[0], scalar1=v1, scalar2=v2, op0=op0, op1=op1),
                       rd, out[1], small=self._small(out[0]))

    def stt(self, out, a, sc, b, op0, op1):
        rd = list(a[1]) + list(b[1])
        v = sc
        if isinstance(sc, tuple):
            rd += sc[1]
            v = sc[0]
        return self.op(self.dve, lambda: self.nc.vector.scalar_tensor_tensor(out=out[0], in0=a[0], scalar=v, in1=b[0], op0=op0, op1=op1),
                       rd, out[1], small=self._small(out[0]))

    def recip(self, out, a):
        return self.op(self.dve, lambda: self.nc.vector.reciprocal(out=out[0], in_=a[0]), a[1], out[1], small=self._small(out[0]))

    def copy(self, out, a):
        return self.op(self.dve, lambda: self.nc.vector.tensor_copy(out=out[0], in_=a[0]), a[1], out[1], small=self._small(out[0]))

    def memset(self, out, val):
        return self.op(self.dve, lambda: self.nc.vector.memset(out[0], val), [], out[1])


class Ring:
    def __init__(self, n):
        self.n = n
        self.i = 0
        self.bufs = [Buf() for _ in range(n)]

    def next(self):
        i = self.i
        self.i = (self.i + 1) % self.n
        return i, self.bufs[i]


class Prog:
    def __init__(self, seg, fused):
        self.phases = ("L0", "L1", "mix", "fft", "ffn2", "ffn1")
        self.seg = seg
        self.fused = fused
        self.in_names = {}
        self.out_names = {}

    def inp(self, name, shape, dt=F32):
        if name not in self.in_names:
            self.in_names[name] = self.nc.dram_tensor(name, list(shape), dt, kind="ExternalInput").ap()
        return self.in_names[name]

    def outp(self, name, shape, dt=F32):
        if name not in self.out_names:
            self.out_names[name] = self.nc.dram_tensor(name, list(shape), dt, kind="ExternalOutput").ap()
        return self.out_names[name]

    def sb(self, es, name, shape, dt):
        self._sbn = getattr(self, "_sbn", 0) + 1
        return es.enter_context(self.nc.sbuf_tensor(f"s{self._sbn}_{name}", list(shape), dt))

    def build(self):
        nc = bass.Bass("TRN2", target_bir_lowering=False)
        self.nc = nc
        seg = self.seg
        with contextlib.ExitStack() as es:
            k = KB(nc, es)
            self.k = k
            self.hT = self.sb(es, "hT", [128, DC, NT], F32)
            self.hb = [[Buf(f"h{dc}_{bi}") for bi in range(5)] for dc in range(DC)]
            self.cst = self.sb(es, "cst", [128, NCST], F32)
            self.cf = self.sb(es, "cf", [128, NCF], F32)
            self.cb = self.sb(es, "cb", [128, NCB], BF16)
            self.modv = self.sb(es, "modv", [128, 72, 2], F32)
            self.der = self.sb(es, "der", [128, 3, 2, 8, 2], F32)
            self.qs = self.sb(es, "qs", [128, 3, 2, NT], BF16)
            self.cbuf = Buf("const")
            self.modb = Buf("modv")
            self.derb = Buf("der")
            self.qb = [[Buf(f"q{i}_{bi}") for bi in range(5)] for i in range(3)]
            cs = k.dsem("const")
            k.dma(k.sp, self.cst[:], self.inp("cst", [128, NCST]), [], [self.cbuf], cs)
            k.dma(k.sp, self.cf[:], self.inp("cf", [128, NCF]), [], [self.cbuf], cs)
            k.dma(k.sp, self.cb[:], self.inp("cb", [128, NCB], BF16), [], [self.cbuf], cs)
            h0 = self.inp("h0", [D, NT])
            for dc in range(DC):
                k.dma(k.sp, self.hT[:, dc, :], h0[dc * 128:(dc + 1) * 128, :], [],
                      [self.hb[dc][bi] for bi in range(5)], k.dsem("hload"))
            self.ccsem = k.newsem("cc")
            for l in range(2):
                need_ctx = (l == 0)
                self.kvl = [nc.dram_tensor(f"kvloc{l}_{p_}", [4 * 128, SL], BF16).ap() for p_ in range(2)]
                self.kva = [nc.dram_tensor(f"kvall{l}_{p_}", [8 * 128, SL], BF16).ap() for p_ in range(2)]
                self.kvc = nc.dram_tensor(f"kvctx{l}", [8 * 128, C], BF16).ap()
                self.kvlb, self.kvab, self.kvcb = Buf("kvl"), Buf("kva"), Buf("kvc")
                self.kvrd = [self.kvab, self.kvcb]
                ph = self.phases
                if f"L{l}" not in ph:
                    continue
                self.ada(l)
                self.derive(l)
                if "ffn1" in ph:
                    self.ffn(l, 1, 0, BLOCKS)
                self.proj_phase(l, need_ctx)
                k._wait(k.pool, [self.kvlb], [self.kvab])
                for p_ in range(2):
                    ins = nc.gpsimd.collective_compute("AllGather", ALU.bypass, replica_groups=[[0, 1], [2, 3], [4, 5], [6, 7]],
                                                       ins=[self.kvl[p_].opt()], outs=[self.kva[p_].opt()])
                    ins.then_inc(self.ccsem.h)
                    self.ccsem.n += 1
                self.kvab.w[self.ccsem] = self.ccsem.n
                self.kvlb.r[self.ccsem] = self.ccsem.n
                if "mix" in ph:
                    self.mix_phase(l, need_ctx)
                if "attn" in ph:
                    self.attn_phase(l, need_ctx)
                if "ret" in ph:
                    self.ret_phase(l, need_ctx)
                if "fft" in ph:
                    self.fft_phase(l, need_ctx)
                if "ffn2" in ph:
                    self.ffn(l, 2, 2, BLOCKS if need_ctx else LAT)
            self.final_norm()
            k.barrier()
        return nc

    def C_(self, name, a=0, w=None):
        o, ww = CST[name]
        if w is None:
            w = ww - a
        return (self.cst[:, o + a:o + a + w], [self.cbuf])

    def CF_(self, name, parts=slice(0, 128)):
        o, w = CF[name]
        return (self.cf[parts, o:o + w], [self.cbuf])

    def CB_(self, name, a=0, w=None, parts=slice(0, 128)):
        o, ww = CB[name]
        if w is None:
            w = ww - a
        return (self.cb[parts, o + a:o + a + w], [self.cbuf])

    def H(self, dc, bi):
        t0, tn, _ = BLOCKS[bi]
        return (self.hT[:, dc, t0:t0 + tn], [self.hb[dc][bi]])

    def Q(self, i, ci, bi, parts=slice(0, 128)):
        t0, tn, _ = BLOCKS[bi]
        return (self.qs[parts, i, ci, t0:t0 + tn], [self.qb[i][bi]])

    def DER(self, sub, kind, dc, mi):
        return (self.der[:, sub, kind, dc, mi:mi + 1], [self.derb])

    def MODV(self, j, mi):
        return (self.modv[:, j, mi:mi + 1], [self.modb])

    def save_state(self):
        k = self.k
        k.barrier()
        ds = k.dsem("state")
        o = self.outp("o_st_h", [128, DC * NT])
        k.dma(k.sp, o, self.hT[:].rearrange("p a b -> p (a b)"), [b for r in self.hb for b in r], [], ds)
        o = self.outp("o_st_mod", [128, 144])
        k.dma(k.sp, o, self.modv[:].rearrange("p a b -> p (a b)"), [self.modb], [], ds)
        o = self.outp("o_st_q", [128, 6 * NT], BF16)
        k.dma(k.sp, o, self.qs[:].rearrange("p a b c -> p (a b c)"), [b for r in self.qb for b in r], [], ds)

    def load_state(self):
        k = self.k
        ds = k.dsem("state")
        i = self.inp("i_st_h", [128, DC * NT])
        k.dma(k.sp, self.hT[:].rearrange("p a b -> p (a b)"), i, [], [b for r in self.hb for b in r], ds)
        i = self.inp("i_st_mod", [128, 144])
        k.dma(k.sp, self.modv[:].rearrange("p a b -> p (a b)"), i, [], [self.modb], ds)
        i = self.inp("i_st_q", [128, 6 * NT], BF16)
        k.dma(k.sp, self.qs[:].rearrange("p a b c -> p (a b c)"), i, [], [b for r in self.qb for b in r], ds)

    def ada(self, l):
        k, nc = self.k, self.nc
        k.barrier()
        w = self.inp(f"ada_w{l}", [D, 9 * D])
        wv = w.rearrange("(dc p) f -> p dc f", p=128)
        with contextlib.ExitStack() as es:
            ring = self.sb(es, "adaw", [128, 2, DC, 1024], BF16)
            rb = [Buf(), Buf()]
            sT = self.sb(es, "adas", [128, 16], BF16)
            sb_ = Buf()
            k.actf((sT[:], [sb_]), self.C_("cc"), AF.Silu)
            for km in range(min(2, 9)):
                k.dma(k.pool, ring[:, km % 2], wv[:, :, km * 1024:(km + 1) * 1024], [], [rb[km % 2]], k.dsem(f"wr{km % 2}"))
            for km in range(9):
                sl = km % 2
                bi_, bk = k.bank()
                for jc in range(8):
                    for dc in range(DC):
                        k.mm((bk[0][:, jc * 2:jc * 2 + 2], bk[1]), (ring[:, sl, dc, jc * 128:(jc + 1) * 128], [rb[sl]]),
                             (sT[:, dc * 2:dc * 2 + 2], [sb_]), start=(dc == 0), stop=(dc == DC - 1),
                             inc=(dc == DC - 1))
                for jc in range(8):
                    j = km * 8 + jc
                    k.ts((self.modv[:, j, :], [self.modb]), (bk[0][:, jc * 2:jc * 2 + 2], bk[1]),
                         self.C_(f"adab{l}", j, 1), ALU.add)
                if km + 2 < 9:
                    k.dma(k.pool, ring[:, sl], wv[:, :, (km + 2) * 1024:(km + 3) * 1024], [], [rb[sl]], k.dsem(f"wr{sl}"))
            k.barrier()

    def derive(self, l):
        k = self.k
        gn = [f"g1_{l}", f"gm_{l}", f"g2_{l}"]
        for sub in range(3):
            k0 = 3 * sub
            for mi in range(2):
                k.stt((self.der[:, sub, 0, :, mi], [self.derb]), (self.modv[:, (k0 + 1) * 8:(k0 + 2) * 8, mi], [self.modb]),
                      1.0, self.C_(gn[sub]), ALU.add, ALU.mult)
                k.ts((self.der[:, sub, 1, :, mi], [self.derb]), (self.modv[:, (k0 + 2) * 8:(k0 + 3) * 8, mi], [self.modb]),
                     1.0 if sub == 1 else 0.5, ALU.mult)

    def modulate(self, l, sub, blocks, nT, nb, es):
        k = self.k
        sq8 = self.sb(es, "sq8", [128, 3, 512], BF16)
        sqr = Ring(3)
        rs = self.sb(es, "rs", [128, 2, 512], F32)
        rsr = Ring(2)
        tt = self.sb(es, "mtt", [128, 2, 512], F32)
        ttr = Ring(2)
        for (t0, tn, mi) in blocks:
            bi = t0 // 512
            _, bk = k.bank()
            for dc in range(DC):
                qi, qbf = sqr.next()
                k.actf((sq8[:, qi, :tn], [qbf]), self.H(dc, bi), AF.Square)
                k.mm((bk[0][:, :tn], bk[1]), self.CB_("ones"), (sq8[:, qi, :tn], [qbf]), start=(dc == 0),
                     stop=(dc == DC - 1))
            ri, rbuf = rsr.next()
            r = (rs[:, ri, :tn], [rbuf])
            k.actf(r, (bk[0][:, :tn], bk[1]), AF.Sqrt, scale=1.0 / D, bias=self.epsap)
            k.recip(r, r)
            for dc in range(DC):
                ti, tb = ttr.next()
                t = (tt[:, ti, :tn], [tb])
                k.tt(t, self.H(dc, bi), r, ALU.mult)
                k.actf((nT[:, dc, t0:t0 + tn], [nb[bi]]), t, AF.Identity, scale=self.DER(sub, 0, dc, mi),
                       bias=self.MODV(3 * sub * 8 + dc, mi))

    def ffn(self, l, which, sub, blocks):
        k = self.k
        k.barrier()
        wgu = self.inp(f"wgu{which}_{l}", [D, 2 * DFF]).rearrange("(dc p) f -> p dc f", p=128)
        wdn = self.inp(f"wd{which}_{l}", [DFF, D]).rearrange("(c p) f -> p c f", p=128)
        with contextlib.ExitStack() as es:
            nT = self.sb(es, "nT", [128, DC, NT], BF16)
            nb = [Buf() for _ in range(5)]
            ring = self.sb(es, "wring", [128, 2, 9216], BF16)
            rb = [Buf(), Buf()]
            hid = self.sb(es, "hid", [128, 2, 3, 512], BF16)
            hr = Ring(2)
            sa = self.sb(es, "sa", [128, 2, 512], F32)
            sar = Ring(2)

            def load(gi):
                c0, G = FFN_GENS[gi]
                sl = gi % 2
                wa = ring[:, sl, 0:8 * G * 128].rearrange("p (c f) -> p c f", c=8)
                wb = ring[:, sl, 3072:3072 + 8 * G * 128].rearrange("p (c f) -> p c f", c=8)
                wd = ring[:, sl, 6144:6144 + G * 1024].rearrange("p (c f) -> p c f", c=G)
                ds = k.dsem(f"wr{sl}")
                k.dma(k.pool, wa, wgu[:, :, c0 * 128:(c0 + G) * 128], [], [rb[sl]], ds)
                k.dma(k.pool, wb, wgu[:, :, DFF + c0 * 128:DFF + (c0 + G) * 128], [], [rb[sl]], ds)
                k.dma(k.pool, wd, wdn[:, c0:c0 + G, :], [], [rb[sl]], ds)

            load(0)
            load(1)
            with contextlib.ExitStack() as es2:
                self.modulate(l, sub, blocks, nT, nb, es2)
            for gi, (c0, G) in enumerate(FFN_GENS):
                sl = gi % 2
                wa = ring[:, sl, 0:8 * G * 128].rearrange("p (c f) -> p c f", c=8)
                wb = ring[:, sl, 3072:3072 + 8 * G * 128].rearrange("p (c f) -> p c f", c=8)
                wd = ring[:, sl, 6144:6144 + G * 1024].rearrange("p (c f) -> p c f", c=G)
                for (t0, tn, mi) in blocks:
                    bi = t0 // 512
                    hi, hbuf = hr.next()
                    for c in range(G):
                        _, pa = k.bank()
                        for dc in range(DC):
                            k.mm((pa[0][:, :tn], pa[1]), (wa[:, dc, c * 128:(c + 1) * 128], [rb[sl]]),
                                 (nT[:, dc, t0:t0 + tn], [nb[bi]]), start=(dc == 0), stop=(dc == DC - 1), inc=(dc == DC - 1))
                        _, pb = k.bank()
                        for dc in range(DC):
                            k.mm((pb[0][:, :tn], pb[1]), (wb[:, dc, c * 128:(c + 1) * 128], [rb[sl]]),
                                 (nT[:, dc, t0:t0 + tn], [nb[bi]]), start=(dc == 0), stop=(dc == DC - 1), inc=(dc == DC - 1))
                        si, sbuf = sar.next()
                        s_ = (sa[:, si, :tn], [sbuf])
                        k.actf(s_, (pa[0][:, :tn], pa[1]), AF.Silu)
                        k.tt((hid[:, hi, c, :tn], [hbuf]), s_, (pb[0][:, :tn], pb[1]), ALU.mult)
                    for j in range(DC):
                        _, pd = k.bank()
                        for c in range(G):
                            k.mm((pd[0][:, :tn], pd[1]), (wd[:, c, j * 128:(j + 1) * 128], [rb[sl]]),
                                 (hid[:, hi, c, :tn], [hbuf]), start=(c == 0), stop=(c == G - 1), inc=(c == G - 1))
                        k.stt(self.H(j, bi), (pd[0][:, :tn], pd[1]), self.DER(sub, 1, j, mi), self.H(j, bi), ALU.mult, ALU.add)
                if gi + 2 < len(FFN_GENS):
                    load(gi + 2)
            k.barrier()

    def proj_phase(self, l, need_ctx):
        k = self.k
        k.barrier()
        win = self.inp(f"win{l}", [D, 25 * 128]).rearrange("(dc p) f -> p dc f", p=128)
        wout = self.inp(f"wout{l}", [D, D])
        rope = self.inp("rope", [128, 4, NT])
        wsT = self.inp(f"wsT{l}", [128, 512])
        with contextlib.ExitStack() as es:
            nT = self.sb(es, "nT", [128, DC, NT], BF16)
            nb = [Buf() for _ in range(5)]
            with contextlib.ExitStack() as es2:
                self.modulate(l, 1, BLOCKS, nT, nb, es2)
                k.barrier()
            wr = self.sb(es, "pw", [128, 2, DC, 512], BF16)
            wrr = Ring(2)

            def loadw(c0, n):
                i, b = wrr.next()
                k.dma(k.pool, wr[:, i, :, 0:n * 128], win[:, :, c0 * 128:(c0 + n) * 128], [], [b], k.dsem(f"wr{i}"))
                return i, b

            def proj(wi, wbuf, ci, bi):
                t0, tn, mi = BLOCKS[bi]
                _, bk = k.bank()
                for dc in range(DC):
                    k.mm((bk[0][:, :tn], bk[1]), (wr[:, wi, dc, ci * 128:(ci + 1) * 128], [wbuf]),
                         (nT[:, dc, t0:t0 + tn], [nb[bi]]), start=(dc == 0), stop=(dc == DC - 1), inc=(dc == DC - 1))
                return (bk[0][:, :tn], bk[1])

            with contextlib.ExitStack() as es3:
                wi, wbuf = loadw(21, 4)
                wo = self.sb(es3, "gwo", [128, 2, D], BF16)
                wob = Buf()
                k.dma(k.pool, wo[:], wout[768:1024, :].rearrange("(c p) f -> p c f", p=128), [], [wob], k.dsem("wo"))
                ws = self.sb(es3, "gws", [128, 4, 128], BF16)
                wsb = Buf()
                k.dma(k.pool, ws[:].rearrange("p a b -> p (a b)"), wsT, [], [wsb], k.dsem("wo"))
                u = self.sb(es3, "gu", [128, 2, 512], F32)
                ub = Buf()
                gv = self.sb(es3, "gv", [128, 2, 512], F32)
                gvb = Buf()
                gq = self.sb(es3, "gq", [128, 2, 512], F32)
                gqb = Buf()
                st = self.sb(es3, "gst", [128, 4, 512], F32)
                stb = [Buf() for _ in range(4)]
                vn = self.sb(es3, "gvn", [128, 2, 512], BF16)
                vnb = Buf()
                vp = self.sb(es3, "gvp", [128, 2, 4, 128], BF16)
                vpr = Ring(2)
                go = self.sb(es3, "ggo", [128, 2, 512], BF16)
                gob = Buf()
                gt = self.sb(es3, "ggt", [128, 2, 128], F32)
                gtr = Ring(2)
                k.memset((vp[:], vpr.bufs), 0.0)
                for bi, (t0, tn, mi) in enumerate(BLOCKS):
                    if mi == 1 and not need_ctx:
                        continue
                    def gelu(dst, P):
                        a_ = (st[:, 0, :tn], [stb[0]])
                        b_ = (st[:, 1, :tn], [stb[1]])
                        k.actf(a_, P, AF.Square)
                        k.ts(a_, a_, 0.044715, ALU.mult, 1.0, ALU.add)
                        k.tt(a_, a_, P, ALU.mult)
                        k.actf(b_, a_, AF.Sigmoid, scale=1.5957691216057308)
                        k.tt(dst, b_, P, ALU.mult)

                    for c in range(2):
                        gelu((u[:, c, :tn], [ub]), proj(wi, wbuf, c, bi))
                    for c in range(2):
                        gelu((gv[:, c, :tn], [gvb]), proj(wi, wbuf, 2 + c, bi))
                        k.actf((gq[:, c, :tn], [gqb]), (gv[:, c, :tn], [gvb]), AF.Square)
                    _, b1 = k.bank()
                    for c in range(2):
                        k.mm((b1[0][:, :tn], b1[1]), self.CF_("ones"), (gv[:, c, :tn], [gvb]), start=(c == 0), stop=(c == 1), inc=(c == 1))
                    _, b2 = k.bank()
                    for c in range(2):
                        k.mm((b2[0][:, :tn], b2[1]), self.CF_("ones"), (gq[:, c, :tn], [gqb]), start=(c == 0), stop=(c == 1), inc=(c == 1))
                    mean = (st[:, 0, :tn], [stb[0]])
                    msq = (st[:, 1, :tn], [stb[1]])
                    var = (st[:, 2, :tn], [stb[2]])
                    dd = (st[:, 3, :tn], [stb[3]])
                    k.actf(mean, (b1[0][:, :tn], b1[1]), AF.Identity, scale=1.0 / 256)
                    k.tt(msq, mean, mean, ALU.mult)
                    k.stt(var, (b2[0][:, :tn], b2[1]), 1.0 / 256, msq, ALU.mult, ALU.subtract)
                    k.actf(var, var, AF.Sqrt, bias=self.epsap)
                    k.recip(var, var)
                    for c in range(2):
                        k.tt(dd, (gv[:, c, :tn], [gvb]), mean, ALU.subtract)
                        k.stt((vn[:, c, :tn], [vnb]), dd, self.C_(f"gmn{l}", c, 1), var, ALU.mult, ALU.mult)
                    for tl in range(tn // 128):
                        vi, vb = vpr.next()
                        for c in range(2):
                            pt = k.bbank()
                            k.transpose((pt[0][:, 0:128], pt[1]), (vn[:, c, tl * 128:(tl + 1) * 128], [vnb]), self.CB_("ident"))
                            k.copy((vp[:, vi, 2 * c, 0:64], [vb]), (pt[0][:, 0:64], pt[1]))
                            k.copy((vp[:, vi, 2 * c + 1, 64:128], [vb]), (pt[0][:, 64:128], pt[1]))
                        for c in range(2):
                            _, mb = k.bank()
                            for gg in range(2):
                                k.mm((mb[0][:, 0:128], mb[1]), (vp[:, vi, 2 * c + gg, :], [vb]), (ws[:, 2 * c + gg, :], [wsb]),
                                     start=(gg == 0), stop=(gg == 1), inc=(gg == 1))
                            gi_, gb_ = gtr.next()
                            g_ = (gt[:, gi_, :], [gb_])
                            o_, w_ = CST[f"bt{l}"]
                            k.tt(g_, (mb[0][:, 0:128], mb[1]), (self.cst[:, o_ + c * 128:o_ + (c + 1) * 128], [self.cbuf]), ALU.add)
                            k.tt((go[:, c, tl * 128:(tl + 1) * 128], [gob]), g_, (u[:, c, tl * 128:(tl + 1) * 128], [ub]), ALU.mult)
                    for j in range(DC):
                        _, pd = k.bank()
                        for c in range(2):
                            k.mm((pd[0][:, :tn], pd[1]), (wo[:, c, j * 128:(j + 1) * 128], [wob]), (go[:, c, :tn], [gob]),
                                 start=(c == 0), stop=(c == 1), inc=(c == 1))
                        k.stt(self.H(j, bi), (pd[0][:, :tn], pd[1]), self.DER(1, 1, j, mi), self.H(j, bi), ALU.mult, ALU.add)
                k.barrier()

            with contextlib.ExitStack() as es3:
                rr = self.sb(es3, "rope", [128, 2, 2, 512], F32)
                rrr = Ring(2)
                tmp = self.sb(es3, "ptmp", [128, 4, 512], F32)
                tr = Ring(4)
                stg = self.sb(es3, "pstg", [128, 3, 512], BF16)
                sr = Ring(3)
                sqt = self.sb(es3, "psq", [128, 2, 512], BF16)
                sqr = Ring(2)
                rst = self.sb(es3, "prs", [128, 2, 512], F32)
                rsr = Ring(2)

                def T_(tn):
                    i, b = tr.next()
                    return (tmp[:, i, :tn], [b])

                def store(src, row, bi):
                    t0, tn, mi = BLOCKS[bi]
                    if mi == 0:
                        k.dma(k.sp, self.kvl[row // 4][(row % 4) * 128:(row % 4 + 1) * 128, t0:t0 + tn], src[0], src[1], [self.kvlb], k.dsem("kvst"))
                    else:
                        k.dma(k.sp, self.kvc[row * 128:(row + 1) * 128, 0:tn], src[0], src[1], [self.kvcb], k.dsem("kvst"))

                def rope_unit(c0, nch, tab, kind, dst):
                    wi, wbuf = loadw(c0, 2 * nch)
                    for bi, (t0, tn, mi) in enumerate(BLOCKS):
                        if mi == 1 and not need_ctx and kind in ("retq", "attq"):
                            continue
                        ri, rbuf = rrr.next()
                        k.dma(k.sp, rr[:, ri, :, :tn], rope[:, tab:tab + 2, t0:t0 + tn], [], [rbuf], k.dsem(f"rope{ri}"))
                        cosT = (rr[:, ri, 0, :tn], [rbuf])
                        sinT = (rr[:, ri, 1, :tn], [rbuf])
                        for ci in range(nch):
                            P = proj(wi, wbuf, ci, bi)
                            Ps = proj(wi, wbuf, nch + ci, bi)
                            t1, t2 = T_(tn), T_(tn)
                            if kind in ("retq", "retk"):
                                if kind == "retq":
                                    k.stt(t1, P, 0.125, cosT, ALU.mult, ALU.mult)
                                    k.stt(t2, Ps, 0.125, sinT, ALU.mult, ALU.mult)
                                else:
                                    k.tt(t1, P, cosT, ALU.mult)
                                    k.tt(t2, Ps, sinT, ALU.mult)
                                if kind == "retq":
                                    k.tt(self.Q(1, ci, bi), t1, t2, ALU.add)
                                else:
                                    si, sbuf = sr.next()
                                    o = (stg[:, si, :tn], [sbuf])
                                    k.tt(o, t1, t2, ALU.add)
                                    store(o, dst + ci, bi)
                            else:
                                gname = "aq" if kind == "attq" else "ak"
                                qi, qbuf = sqr.next()
                                sq = (sqt[:, qi, :tn], [qbuf])
                                k.actf(sq, P, AF.Square)
                                _, sb_ = k.bank()
                                k.mm((sb_[0][:, :tn], sb_[1]), self.CB_("bones"), sq)
                                ri2, rb2 = rsr.next()
                                r = (rst[:, ri2, :tn], [rb2])
                                k.actf(r, (sb_[0][:, :tn], sb_[1]), AF.Sqrt, scale=1.0 / 64, bias=self.epsap)
                                k.recip(r, r)
                                k.stt(t1, P, self.C_(f"{gname}n{l}"), cosT, ALU.mult, ALU.mult)
                                k.stt(t2, Ps, self.C_(f"{gname}s{l}"), sinT, ALU.mult, ALU.mult)
                                k.tt(t1, t1, t2, ALU.add)
                                if kind == "attq":
                                    k.tt(self.Q(0, ci, bi), t1, r, ALU.mult)
                                else:
                                    si, sbuf = sr.next()
                                    o = (stg[:, si, :tn], [sbuf])
                                    k.tt(o, t1, r, ALU.mult)
                                    store(o, dst + ci, bi)

                def plain_unit(c0, nch, kind, dst):
                    wi, wbuf = loadw(c0, nch)
                    for bi, (t0, tn, mi) in enumerate(BLOCKS):
                        if mi == 1 and not need_ctx and kind in ("retg", "fnet"):
                            continue
                        for ci in range(nch):
                            P = proj(wi, wbuf, ci, bi)
                            if kind == "retg":
                                k.actf(self.Q(2, ci, bi), P, AF.Silu)
                            else:
                                si, sbuf = sr.next()
                                o = (stg[:, si, :tn], [sbuf])
                                k.actf(o, P, AF.Identity)
                                store(o, dst + ci, bi)

                rope_unit(0, 2, 2, "retq", None)
                rope_unit(4, 2, 2, "retk", 2)
                plain_unit(8, 2, "retv", 4)
                plain_unit(10, 2, "retg", None)
                plain_unit(12, 2, "fnet", 6)
                rope_unit(14, 2, 0, "attq", None)
                rope_unit(18, 1, 0, "attk", 0)
                plain_unit(20, 1, "attv", 1)
                k.barrier()

    def kv_src(self, row, c0, n):
        out = []
        segs = [(0, C, self.kvc[row * 128:(row + 1) * 128, :]),
                (C, SL, self.kva[row // 4][(row % 4) * 128:(row % 4 + 1) * 128, :]),
                (C + SL, SL, self.kva[row // 4][(4 + row % 4) * 128:(5 + row % 4) * 128, :])]
        for (s0, sn, t) in segs:
            a = max(c0, s0)
            b = min(c0 + n, s0 + sn)
            if a < b:
                out.append((t[:, a - s0:b - s0], a - c0, b - a))
        return out

    def attn_phase(self, l, need_ctx):
        k = self.k
        k.barrier()
        wout = self.inp(f"wout{l}", [D, D])
        with contextlib.ExitStack() as es:
            Kt = self.sb(es, "aK", [128, NKC * 128], BF16)
            Kb = Buf()
            Va = self.sb(es, "aV", [128, NKC, 2, 128], BF16)
            Vb = Buf()
            vst = self.sb(es, "avst", [128, 2, 512], BF16)
            vsr = Ring(2)
            E = self.sb(es, "aE", [128, 4, 512], BF16)
            Er = Ring(4)
            accs = self.sb(es, "aacc", [128, 2, 512], F32)
            acr = Ring(2)
            rden = self.sb(es, "arden", [64, 2, 512], F32)
            rdr = Ring(2)
            aout = self.sb(es, "aout", [64, 4, 512], BF16)
            aob = Buf()
            wo = self.sb(es, "awo", [64, 4, D], BF16)
            wob = Buf()
            for (src, off, n) in self.kv_src(0, 0, NKC * 128):
                k.dma(k.sp, Kt[:, off:off + n], src, self.kvrd, [Kb], k.dsem("kload"))
            k.dma(k.pool, wo[:], wout[512:768, :].rearrange("(h p) f -> p h f", p=64), [], [wob], k.dsem("wo"))
            dbg = int(os.environ.get("KDBG", "99"))
            if dbg <= 0:
                k.barrier()
                return
            k.memset((Va[:, :, :, 64:128], [Vb]), 1.0)
            if dbg <= 1:
                k.barrier()
                return
            for pc in range(0, NKC * 128, 512):
                n = min(512, NKC * 128 - pc)
                vi, vb = vsr.next()
                for (src, off, nn) in self.kv_src(1, pc, n):
                    k.dma(k.sp, vst[:, vi, off:off + nn], src, self.kvrd, [vb], k.dsem(f"vst{vi}"))
                for tl in range(n // 128):
                    kc = pc // 128 + tl
                    if dbg == 12:
                        continue
                    pt = k.bbank()
                    k.transpose((pt[0][:, 0:128], pt[1]), (vst[:, vi, tl * 128:(tl + 1) * 128], [vb]), self.CB_("ident"))
                    if dbg == 13:
                        continue
                    if dbg == 14:
                        k.copy((accs[:, 0, 0:64], [acr.bufs[0]]), (pt[0][:, 0:64], pt[1]))
                        continue
                    if dbg == 15:
                        k.copy((Va[:, kc, 0, 0:64], [Vb]), (accs[:, 0, 0:64], [acr.bufs[0]]))
                        continue
                    k.copy((Va[:, kc, 0, 0:64], [Vb]), (pt[0][:, 0:64], pt[1]))
                    k.copy((Va[:, kc, 1, 0:64], [Vb]), (pt[0][:, 64:128], pt[1]))
            if dbg <= 2 or dbg in (12, 13, 14, 15):
                k.barrier()
                return
            for bi, (t0, tn, mi) in enumerate(BLOCKS):
                if mi == 1 and not need_ctx:
                    continue
                if dbg <= 6 and bi > 0:
                    continue
                kcs = list(range(NKC)) if mi == 0 else [0, 1]
                if dbg <= 3:
                    kcs = kcs[:3]
                for h in range(4):
                    kvh = h // 2
                    pr = slice(kvh * 64, kvh * 64 + 64)
                    q = self.Q(0, h % 2, bi, parts=pr)
                    ai, acc = k.bank(hold=True)
                    pend = []

                    def score(kc):
                        _, sb_ = k.bank()
                        k.mm((sb_[0][:, :tn], sb_[1]), (Kt[pr, kc * 128:(kc + 1) * 128], [Kb]), q)
                        return sb_

                    def finish(kc, sb_, first, last):
                        ei, eb = Er.next()
                        e = (E[:, ei, :tn], [eb])
                        k.actf(e, (sb_[0][:, :tn], sb_[1]), AF.Exp, scale=0.125)
                        k.mm((acc[0][:, :tn], acc[1]), (Va[:, kc, kvh, :], [Vb]), e, start=first, stop=last)

                    for idx, kc in enumerate(kcs):
                        pend.append((kc, score(kc)))
                        if len(pend) > 3:
                            kc0, s0 = pend.pop(0)
                            finish(kc0, s0, kc0 == kcs[0], False)
                    while pend:
                        kc0, s0 = pend.pop(0)
                        finish(kc0, s0, kc0 == kcs[0], len(pend) == 0)
                    ci, cbf = acr.next()
                    a_ = (accs[:, ci, :tn], [cbf])
                    k.actf(a_, (acc[0][:, :tn], acc[1]), AF.Identity)
                    k.release(ai)
                    if dbg <= 4:
                        continue
                    _, dn = k.bank()
                    k.mm((dn[0][0:64, :tn], dn[1]), self.CF_("shift"), a_)
                    di, dbf = rdr.next()
                    rd = (rden[:, di, :tn], [dbf])
                    k.recip(rd, (dn[0][0:64, :tn], dn[1]))
                    k.tt((aout[:, h, :tn], [aob]), (accs[0:64, ci, :tn], [cbf]), rd, ALU.mult)
                if dbg <= 5:
                    continue
                for j in range(DC):
                    _, pd = k.bank()
                    for h in range(4):
                        k.mm((pd[0][:, :tn], pd[1]), (wo[:, h, j * 128:(j + 1) * 128], [wob]), (aout[:, h, :tn], [aob]),
                             start=(h == 0), stop=(h == 3), inc=(h == 3))
                    k.stt(self.H(j, bi), (pd[0][:, :tn], pd[1]), self.DER(1, 1, j, mi), self.H(j, bi), ALU.mult, ALU.add)
            k.barrier()

    def ret_phase(self, l, need_ctx):
        k = self.k
        k.barrier()
        wout = self.inp(f"wout{l}", [D, D])
        with contextlib.ExitStack() as es:
            Kr = self.sb(es, "rK", [128, NKC * 128], BF16)
            Kb = Buf()
            Vr = self.sb(es, "rV", [128, NKC, 128], BF16)
            Vb = Buf()
            vst = self.sb(es, "rvst", [128, 2, 256], BF16)
            vsr = Ring(2)
            PM = self.sb(es, "rPM", [128, 2, 2, 4, 512], BF16)
            PMb = Buf()
            PMc = self.sb(es, "rPMc", [128, 2, 2, 256], BF16) if need_ctx else None
            EE = self.sb(es, "rEE", [128, 2, 2, 2, 512], BF16)
            EEb = Buf()
            mg = self.sb(es, "rmg", [128, 2, 512], BF16)
            mgr = Ring(2)
            mg2 = self.sb(es, "rmg2", [128, 2, 512], BF16)
            mg2r = Ring(2)
            Pm = self.sb(es, "rPm", [128, 4, 512], BF16)
            Pmr = Ring(4)
            st = self.sb(es, "rst", [128, 5, 512], F32)
            stb = [Buf() for _ in range(5)]
            rout = self.sb(es, "rout", [128, 2, 512], BF16)
            ror = Ring(2)
            wo = self.sb(es, "rwo", [128, 2, D], BF16)
            wob = Buf()
            ct = self.sb(es, "rct", [128, 4, 132], F32)
            ctb = Buf()
            sx = self.sb(es, "rsx", [128, 4, 72], F32)
            k.dma(k.pool, wo[:], wout[0:256, :].rearrange("(c p) f -> p c f", p=128), [], [wob], k.dsem("wo"))
            T = self.CF_("iota")
            for h in range(4):
                lgf = self.C_(f"lgf{l}", h, 1)
                lgb = self.C_(f"lgb{l}", h, 1)
                c1 = (ct[:, h, 0:56], [ctb])
                k.ts(c1, self.C_("sgf"), lgf, ALU.mult)
                k.stt(c1, self.C_("sgb"), lgb, c1, ALU.mult, ALU.add)
                k.tt((ct[:, h, 56:112], [ctb]), c1, self.C_("dlt"), ALU.mult)
                k.ts((ct[:, h, 112:120], [ctb]), self.C_("d1"), lgf, ALU.mult)
                k.ts((ct[:, h, 120:128], [ctb]), self.C_("d2"), lgb, ALU.mult)
                k.ts((ct[:, h, 128:129], [ctb]), lgb, -1.0, ALU.mult)
                k.actf((sx[:, h, :], [ctb]), (ct[:, h, 56:128], [ctb]), AF.Exp)

            def CT(h, a):
                return (ct[:, h, a:a + 1], [ctb])

            def SX(h, a):
                return (sx[:, h, a:a + 1], [ctb])

            for hp in range(2):
                for (src, off, n) in self.kv_src(2 + hp, 0, NKC * 128):
                    k.dma(k.sp, Kr[:, off:off + n], src, self.kvrd, [Kb], k.dsem("kload"))
                for pc in range(0, NKC * 128, 256):
                    vi, vb = vsr.next()
                    for (src, off, nn) in self.kv_src(4 + hp, pc, 256):
                        k.dma(k.sp, vst[:, vi, off:off + nn], src, self.kvrd, [vb], k.dsem(f"vst{vi}"))
                    for tl in range(2):
                        kc = pc // 128 + tl
                        pt = k.bbank()
                        k.transpose((pt[0][:, 0:128], pt[1]), (vst[:, vi, tl * 128:(tl + 1) * 128], [vb]), self.CB_("ident"))
                        k.copy((Vr[:, kc, 0:64], [Vb]), (pt[0][:, 0:64], pt[1]))
                        k.copy((Vr[:, kc, 64:128], [Vb]), (pt[0][:, 64:128], pt[1]))
                r1 = (st[:, 0, :], [stb[0]])
                r2 = (st[:, 1, :], [stb[1]])
                e1 = (st[:, 2, :], [stb[2]])
                dg = (st[:, 3, :], [stb[3]])
                for r in range(2):
                    for m in range(4):
                        dcol = self.C_("pmd", r * 4 + m, 1)
                        ncol = self.C_("pmdn", r * 4 + m, 1)
                        k.actf(r1, T, AF.Relu, scale=1.0, bias=dcol)
                        k.actf(r2, T, AF.Relu, scale=-1.0, bias=ncol)
                        k.ts(dg, T, ncol, ALU.is_equal)
                        for hh in range(2):
                            h = 2 * hp + hh
                            k.ts(e1, r1, self.C_(f"lgf{l}", h, 1), ALU.mult)
                            k.stt(e1, r2, self.C_(f"lgb{l}", h, 1), e1, ALU.mult, ALU.add)
                            k.actf(e1, e1, AF.Exp)
                            k.tt((PM[:, hh, r, m, :], [PMb]), e1, dg, ALU.add)
                    for hh in range(2):
                        h = 2 * hp + hh
                        for cls in range(2):
                            k.actf((EE[:, hh, r, cls, :], [EEb]), T, AF.Exp, scale=CT(h, r * 28 + (16 if cls == 0 else 11)))
                if need_ctx:
                    for m in range(2):
                        dl = -128.0 * m
                        r1c = (st[:, 0, 0:C], [stb[0]])
                        r2c = (st[:, 1, 0:C], [stb[1]])
                        e1c = (st[:, 2, 0:C], [stb[2]])
                        dgc = (st[:, 3, 0:C], [stb[3]])
                        Tc = (T[0][:, 0:C], T[1])
                        k.ts(r1c, Tc, dl, ALU.add, 0.0, ALU.max)
                        k.ts(r2c, Tc, -1.0, ALU.mult, -dl, ALU.add)
                        k.ts(r2c, r2c, 0.0, ALU.max)
                        k.ts(dgc, Tc, -dl, ALU.is_equal)
                        for hh in range(2):
                            h = 2 * hp + hh
                            k.ts(e1c, r1c, self.C_(f"lgf{l}", h, 1), ALU.mult)
                            k.stt(e1c, r2c, self.C_(f"lgb{l}", h, 1), e1c, ALU.mult, ALU.add)
                            k.actf(e1c, e1c, AF.Exp)
                            k.tt((PMc[:, hh, m, :], [PMb]), e1c, dgc, ALU.add)

                for bi, (t0, tn, mi) in enumerate(BLOCKS):
                    if mi == 1 and not need_ctx:
                        continue
                    kcs = list(range(NKC)) if mi == 0 else [0, 1]
                    qb_ = bi
                    q0 = t0
                    ai, acc = k.bank(hold=True)
                    for hh in range(2):
                        h = 2 * hp + hh
                        pr = slice(hh * 64, hh * 64 + 64)
                        q = self.Q(1, hp, bi, parts=pr)
                        pend = []

                        def score(kc):
                            _, sb_ = k.bank()
                            k.mm((sb_[0][:, :tn], sb_[1]), (Kr[pr, kc * 128:(kc + 1) * 128], [Kb]), q)
                            return sb_

                        def finish(kc, sb_, first, last):
                            pi, pb = Pmr.next()
                            p_ = (Pm[:, pi, :tn], [pb])
                            S_ = (sb_[0][:, :tn], sb_[1])
                            if mi == 1:
                                k.tt(p_, S_, (PMc[:, hh, kc, :tn], [PMb]), ALU.mult)
                            elif kc < 2:
                                i1, b1 = mgr.next()
                                m1 = (mg[:, i1, :tn], [b1])
                                i2, b2 = mg2r.next()
                                m2 = (mg2[:, i2, :tn], [b2])
                                k.actf(m1, (T[0][:, :tn], T[1]), AF.Exp, scale=self.C_(f"lgf{l}", h, 1), bias=CT(h, 112 + qb_ * 2 + kc))
                                k.actf(m2, (T[0][:, :tn], T[1]), AF.Exp, scale=CT(h, 128), bias=CT(h, 120 + qb_ * 2 + kc))
                                k.tt(m1, m1, m2, ALU.add)
                                k.tt(p_, S_, m1, ALU.mult)
                            else:
                                r = (kc - 2) // 16
                                dl = q0 - ((kc - 2) % 16) * 128
                                if -384 <= dl <= 0:
                                    k.tt(p_, S_, (PM[:, hh, r, (-dl) // 128, :tn], [PMb]), ALU.mult)
                                else:
                                    didx = dl // 128 + 15
                                    cls = 0 if dl >= 128 else 1
                                    k.stt(p_, S_, SX(h, r * 28 + didx), (EE[:, hh, r, cls, :tn], [EEb]), ALU.mult, ALU.mult)
                            k.mm((acc[0][pr, :tn], acc[1]), (Vr[:, kc, hh * 64:(hh + 1) * 64], [Vb]), p_, start=first, stop=last)

                        for kc in kcs:
                            pend.append((kc, score(kc)))
                            if len(pend) > 3:
                                kc0, s0 = pend.pop(0)
                                finish(kc0, s0, kc0 == kcs[0], False)
                        while pend:
                            kc0, s0 = pend.pop(0)
                            finish(kc0, s0, kc0 == kcs[0], len(pend) == 0)
                    o = (st[:, 0, :tn], [stb[0]])
                    sq = (st[:, 1, :tn], [stb[1]])
                    mean = (st[:, 2, :tn], [stb[2]])
                    msq = (st[:, 3, :tn], [stb[3]])
                    var = (st[:, 4, :tn], [stb[4]])
                    dd = (st[:, 1, :tn], [stb[1]])
                    k.actf(o, (acc[0][:, :tn], acc[1]), AF.Identity)
                    k.actf(sq, (acc[0][:, :tn], acc[1]), AF.Square)
                    k.release(ai)
                    _, b1 = k.bank()
                    k.mm((b1[0][:, :tn], b1[1]), self.CF_("bones"), o)
                    _, b2 = k.bank()
                    k.mm((b2[0][:, :tn], b2[1]), self.CF_("bones"), sq)
                    k.actf(mean, (b1[0][:, :tn], b1[1]), AF.Identity, scale=1.0 / 64)
                    k.tt(msq, mean, mean, ALU.mult)
                    k.stt(var, (b2[0][:, :tn], b2[1]), 1.0 / 64, msq, ALU.mult, ALU.subtract)
                    k.actf(var, var, AF.Sqrt, bias=self.epsap)
                    k.recip(var, var)
                    k.tt(dd, o, mean, ALU.subtract)
                    k.stt(dd, dd, self.C_(f"retn{l}", hp, 1), var, ALU.mult, ALU.mult)
                    ri_, rb_ = ror.next()
                    ro = (rout[:, ri_, :tn], [rb_])
                    k.tt(ro, dd, self.Q(2, hp, bi), ALU.mult)
                    for j in range(DC):
                        _, pd = k.bank()
                        k.mm((pd[0][:, :tn], pd[1]), (wo[:, hp, j * 128:(j + 1) * 128], [wob]), ro)
                        k.stt(self.H(j, bi), (pd[0][:, :tn], pd[1]), self.DER(1, 1, j, mi), self.H(j, bi), ALU.mult, ALU.add)
            k.barrier()

    def mix_phase(self, l, need_ctx):
        k = self.k
        k.barrier()
        wout = self.inp(f"wout{l}", [D, D])
        with contextlib.ExitStack() as es:
            Kt = self.sb(es, "aK", [128, NKC * 128], BF16)
            Ktb = Buf()
            Va = self.sb(es, "aV", [128, NKC, 128], BF16)
            Vab = Buf()
            E = self.sb(es, "aE", [128, 3, 512], BF16)
            Er = Ring(3)
            accs = self.sb(es, "aacc", [128, 512], F32)
            acb = Buf()
            aout = self.sb(es, "aout", [64, 2, 512], BF16)
            aob = Buf()
            woa = self.sb(es, "awo", [64, 2, D], BF16)
            woab = Buf()
            Kr = self.sb(es, "rK", [128, NKC * 128], BF16)
            Krb = Buf()
            Vr = self.sb(es, "rV", [128, NKC, 128], BF16)
            Vrb = Buf()
            vst = self.sb(es, "rvst", [128, 2, 256], BF16)
            vsr = Ring(2)
            PM = self.sb(es, "rPM", [128, 2, 2, 4, 512], BF16)
            PMb = Buf()
            PMc = self.sb(es, "rPMc", [128, 2, 2, 256], BF16) if need_ctx else None
            AP_ = self.sb(es, "rAp", [128, 6, 512], BF16)
            APb = Buf()
            QA = self.sb(es, "rQA", [128, 2, 512], BF16)
            QAr = Ring(2)
            Kbt = self.sb(es, "rKb", [128, 2, 128], BF16)
            Kbr = Ring(2)
            Wsum = self.sb(es, "rWs", [128, 4, 6, 64], F32)
            Wsb = Buf()
            Wsq = [[Buf() for _ in range(6)] for _ in range(4)]
            Wb = self.sb(es, "rWb", [128, 1, 6, 64], BF16)
            Wbr = Ring(1)
            c1p = self.sb(es, "rc1p", [128, 6, 4], F32)
            c1b = Buf()
            bcol = self.sb(es, "rbcol", [128, 6, 2], F32)
            sxp = self.sb(es, "rsxp", [128, 72], F32)
            sxb = Buf()
            Pm = self.sb(es, "rPm", [128, 3, 512], BF16)
            Pmr = Ring(3)
            st = self.sb(es, "rst", [128, 5, 512], F32)
            stb = [Buf() for _ in range(5)]
            rout = self.sb(es, "rout", [128, 512], BF16)
            rob = Buf()
            wor = self.sb(es, "rwo", [128, D], BF16)
            worb = Buf()
            ct = self.sb(es, "rct", [128, 4, 132], F32)
            ctb = Buf()
            sx = self.sb(es, "rsx", [128, 4, 72], F32)
            for (src, off, n) in self.kv_src(0, 0, NKC * 128):
                k.dma(k.sp, Kt[:, off:off + n], src, self.kvrd, [Ktb], k.dsem("kload"))
            k.memset((Va[:, :, 64:128], [Vab]), 1.0)
            T = self.CF_("iota")
            for h in range(4):
                lgf = self.C_(f"lgf{l}", h, 1)
                lgb = self.C_(f"lgb{l}", h, 1)
                c1 = (ct[:, h, 0:56], [ctb])
                k.ts(c1, self.C_("sgf"), lgf, ALU.mult)
                k.stt(c1, self.C_("sgb"), lgb, c1, ALU.mult, ALU.add)
                k.tt((ct[:, h, 56:112], [ctb]), c1, self.C_("dlt"), ALU.mult)
                k.ts((ct[:, h, 112:120], [ctb]), self.C_("d1"), lgf, ALU.mult)
                k.ts((ct[:, h, 120:128], [ctb]), self.C_("d2"), lgb, ALU.mult)
                k.ts((ct[:, h, 128:129], [ctb]), lgb, -1.0, ALU.mult)
                k.actf((sx[:, h, :], [ctb]), (ct[:, h, 56:128], [ctb]), AF.Exp)

            def CT(h, a):
                return (ct[:, h, a:a + 1], [ctb])

            def SX(h, a):
                return (sx[:, h, a:a + 1], [ctb])

            for p in range(2):
                k.dma(k.pool, woa[:], wout[512 + p * 128:512 + (p + 1) * 128, :].rearrange("(h p) f -> p h f", p=64), [], [woab], k.dsem("wo"))
                k.dma(k.pool, wor[:], wout[p * 128:(p + 1) * 128, :], [], [worb], k.dsem("wo"))
                for (src, off, n) in self.kv_src(2 + p, 0, NKC * 128):
                    k.dma(k.sp, Kr[:, off:off + n], src, self.kvrd, [Krb], k.dsem("kload"))
                for which in range(2):
                    for pc in range(0, NKC * 128, 256):
                        vi, vb = vsr.next()
                        for (src, off, nn) in self.kv_src(1 if which == 0 else 4 + p, pc, 256):
                            k.dma(k.sp, vst[:, vi, off:off + nn], src, self.kvrd, [vb], k.dsem(f"vst{vi}"))
                        for tl in range(2):
                            kc = pc // 128 + tl
                            pt = k.bbank()
                            k.transpose((pt[0][:, 0:128], pt[1]), (vst[:, vi, tl * 128:(tl + 1) * 128], [vb]), self.CB_("ident"))
                            if which == 0:
                                k.copy((Va[:, kc, 0:64], [Vab]), (pt[0][:, p * 64:(p + 1) * 64], pt[1]))
                            else:
                                k.copy((Vr[:, kc, 0:64], [Vrb]), (pt[0][:, 0:64], pt[1]))
                                k.copy((Vr[:, kc, 64:128], [Vrb]), (pt[0][:, 64:128], pt[1]))
                r1 = (st[:, 0, :], [stb[0]])
                r2 = (st[:, 1, :], [stb[1]])
                e1 = (st[:, 2, :], [stb[2]])
                dg = (st[:, 3, :], [stb[3]])
                for r in range(2):
                    for m in range(4):
                        dcol = self.C_("pmd", r * 4 + m, 1)
                        ncol = self.C_("pmdn", r * 4 + m, 1)
                        k.actf(r1, T, AF.Relu, scale=1.0, bias=dcol)
                        k.actf(r2, T, AF.Relu, scale=-1.0, bias=ncol)
                        k.ts(dg, T, ncol, ALU.is_equal)
                        for hh in range(2):
                            h = 2 * p + hh
                            k.ts(e1, r1, self.C_(f"lgf{l}", h, 1), ALU.mult)
                            k.stt(e1, r2, self.C_(f"lgb{l}", h, 1), e1, ALU.mult, ALU.add)
                            k.actf(e1, e1, AF.Exp)
                            k.tt((PM[:, hh, r, m, :], [PMb]), e1, dg, ALU.add)
                if need_ctx:
                    for m in range(2):
                        dl = -128.0 * m
                        r1c = (st[:, 0, 0:C], [stb[0]])
                        r2c = (st[:, 1, 0:C], [stb[1]])
                        e1c = (st[:, 2, 0:C], [stb[2]])
                        dgc = (st[:, 3, 0:C], [stb[3]])
                        Tc = (T[0][:, 0:C], T[1])
                        k.ts(r1c, Tc, dl, ALU.add, 0.0, ALU.max)
                        k.ts(r2c, Tc, -1.0, ALU.mult, -dl, ALU.add)
                        k.ts(r2c, r2c, 0.0, ALU.max)
                        k.ts(dgc, Tc, -dl, ALU.is_equal)
                        for hh in range(2):
                            h = 2 * p + hh
                            k.ts(e1c, r1c, self.C_(f"lgf{l}", h, 1), ALU.mult)
                            k.stt(e1c, r2c, self.C_(f"lgb{l}", h, 1), e1c, ALU.mult, ALU.add)
                            k.actf(e1c, e1c, AF.Exp)
                            k.tt((PMc[:, hh, m, :], [PMb]), e1c, dgc, ALU.add)

                T0 = (self.cf[:, CF["iota"][0]:CF["iota"][0] + 1], [self.cbuf])
                for hh in range(2):
                    h = 2 * p + hh
                    prs = slice(hh * 64, hh * 64 + 64)
                    k.copy((sxp[prs, :], [sxb]), (sx[prs, h, :], [ctb]))
                    for t in range(6):
                        if t < 4:
                            ci_ = (t // 2) * 28 + (16 if t % 2 == 0 else 11)
                            src = (ct[prs, h, ci_:ci_ + 1], [ctb])
                            srcf = (ct[:, h, ci_:ci_ + 1], [ctb])
                        elif t == 4:
                            o_ = CST[f"lgf{l}"][0] + h
                            src = (self.cst[prs, o_:o_ + 1], [self.cbuf])
                            srcf = (self.cst[:, o_:o_ + 1], [self.cbuf])
                        else:
                            src = (ct[prs, h, 128:129], [ctb])
                            srcf = (ct[:, h, 128:129], [ctb])
                        k.copy((c1p[prs, t, 0:1], [c1b]), src)
                        k.actf((bcol[:, t, hh:hh + 1], [c1b]), T0, AF.Exp, scale=srcf)
                for t in range(6):
                    k.ts((c1p[:, t, 1:2], [c1b]), (c1p[:, t, 0:1], [c1b]), -1.0, ALU.mult)
                    k.tt((c1p[:, t, 2:3], [c1b]), T0, (c1p[:, t, 1:2], [c1b]), ALU.mult)
                    k.actf((AP_[:, t, :], [APb]), T, AF.Exp, scale=(c1p[:, t, 0:1], [c1b]), bias=(c1p[:, t, 2:3], [c1b]))
                k.memset((Wsum[:], [Wsb] + [b_ for r_ in Wsq for b_ in r_]), 0.0)
                for kc in range(NKC):
                    pt = k.bbank()
                    k.transpose((pt[0][:, 0:128], pt[1]), (Kr[:, kc * 128:(kc + 1) * 128], [Krb]), self.CB_("ident"))
                    terms = [4, 5] if kc < 2 else [((kc - 2) // 16) * 2, ((kc - 2) // 16) * 2 + 1]
                    for t in terms:
                        uses = []
                        for qb in range(4):
                            if kc < 2:
                                uses.append((qb, (56 if t == 4 else 64) + qb * 2 + kc))
                            else:
                                r = (kc - 2) // 16
                                dl = qb * 512 - ((kc - 2) % 16) * 128
                                if -384 <= dl <= 0:
                                    continue
                                if (0 if dl >= 128 else 1) == t % 2:
                                    uses.append((qb, r * 28 + dl // 128 + 15))
                        if not uses:
                            continue
                        ki, kb_ = Kbr.next()
                        for hh in range(2):
                            k.ts((Kbt[:, ki, hh * 64:(hh + 1) * 64], [kb_]), (pt[0][:, hh * 64:(hh + 1) * 64], pt[1]),
                                 (bcol[:, t, hh:hh + 1], [c1b]), ALU.mult)
                        _, wb_ = k.bank()
                        for hh in range(2):
                            k.mm((wb_[0][hh * 64:(hh + 1) * 64, 0:64], wb_[1]), (Kbt[:, ki, hh * 64:(hh + 1) * 64], [kb_]),
                                 (Vr[:, kc, hh * 64:(hh + 1) * 64], [Vrb]), inc=(hh == 1))
                        for (qb, sidx) in uses:
                            k.stt((Wsum[:, qb, t, :], [Wsq[qb][t]]), (wb_[0][:, 0:64], wb_[1]), (sxp[:, sidx:sidx + 1], [sxb]),
                                  (Wsum[:, qb, t, :], [Wsq[qb][t]]), ALU.mult, ALU.add)

                def att_gen(bi):
                    t0, tn, mi = BLOCKS[bi]
                    kcs = list(range(NKC)) if mi == 0 else [0, 1]
                    pr = slice(p * 64, p * 64 + 64)
                    for j in range(2):
                        q = self.Q(0, j, bi, parts=pr)
                        ai, acc = k.bank(hold=True)
                        pend = []

                        def score(kc):
                            bi_, sb_ = k.bank(hold=True)
                            k.mm((sb_[0][:, :tn], sb_[1]), (Kt[pr, kc * 128:(kc + 1) * 128], [Ktb]), q)
                            return (bi_, sb_)

                        def finish(kc, sbh, first, last):
                            bi_, sb_ = sbh
                            ei, eb = Er.next()
                            e = (E[:, ei, :tn], [eb])
                            k.actf(e, (sb_[0][:, :tn], sb_[1]), AF.Exp, scale=0.125)
                            k.release(bi_)
                            k.mm((acc[0][:, :tn], acc[1]), (Va[:, kc, :], [Vab]), e, start=first, stop=last)

                        groups = [kcs[i_:i_ + 2] for i_ in range(0, len(kcs), 2)]
                        pendg = []
                        seen = [0]

                        def flush(g0, sc0, last_group):
                            es = []
                            for kc0, sbh in zip(g0, sc0):
                                bi_, sb_ = sbh
                                ei, eb = Er.next()
                                e = (E[:, ei, :tn], [eb])
                                k.actf(e, (sb_[0][:, :tn], sb_[1]), AF.Exp, scale=0.125)
                                k.release(bi_)
                                es.append(e)
                            order = list(range(len(g0)))[::-1]
                            for n_, idx in enumerate(order):
                                first = (seen[0] == 0)
                                seen[0] += 1
                                last = last_group and (n_ == len(order) - 1)
                                k.mm((acc[0][:, :tn], acc[1]), (Va[:, g0[idx], :], [Vab]), es[idx], start=first, stop=last)

                        for gi_, g in enumerate(groups):
                            pendg.append((g, [score(kc) for kc in g]))
                            if len(pendg) > 1:
                                g0, sc0 = pendg.pop(0)
                                flush(g0, sc0, False)
                                yield
                        while pendg:
                            g0, sc0 = pendg.pop(0)
                            flush(g0, sc0, len(pendg) == 0)
                            yield
                        a_ = (accs[:, :tn], [acb])
                        k.actf(a_, (acc[0][:, :tn], acc[1]), AF.Identity)
                        k.release(ai)
                        _, dn = k.bank()
                        k.mm((dn[0][0:64, :tn], dn[1]), self.CF_("shift"), a_)
                        rd = (st[0:64, 4, :tn], [stb[4]])
                        k.recip(rd, (dn[0][0:64, :tn], dn[1]))
                        k.tt((aout[:, j, :tn], [aob]), (accs[0:64, :tn], [acb]), rd, ALU.mult)
                        yield
                    for jj in range(DC):
                        _, pd = k.bank()
                        for j in range(2):
                            k.mm((pd[0][:, :tn], pd[1]), (woa[:, j, jj * 128:(jj + 1) * 128], [woab]), (aout[:, j, :tn], [aob]),
                                 start=(j == 0), stop=(j == 1), inc=(j == 1))
                        k.stt(self.H(jj, bi), (pd[0][:, :tn], pd[1]), self.DER(1, 1, jj, mi), self.H(jj, bi), ALU.mult, ALU.add)
                        yield

                def ret_gen(bi):
                    t0, tn, mi = BLOCKS[bi]
                    q0 = t0
                    ai, acc = k.bank(hold=True)
                    started = [False, False]
                    work = []
                    if mi == 1:
                        for hh in range(2):
                            for kc in (0, 1):
                                work.append((hh, kc))
                    else:
                        wi_, wbb = Wbr.next()
                        k.copy((Wb[:, wi_].rearrange("p a b -> p (a b)"), [wbb]), (Wsum[:, bi].rearrange("p a b -> p (a b)"), [Wsb] + Wsq[bi]))
                        for t in range(6):
                            qi_, qab = QAr.next()
                            qa = (QA[:, qi_, :tn], [qab])
                            k.tt(qa, self.Q(1, p, bi), (AP_[:, t, :tn], [APb]), ALU.mult)
                            for hh in range(2):
                                pr = slice(hh * 64, hh * 64 + 64)
                                k.mm((acc[0][pr, :tn], acc[1]), (Wb[pr, wi_, t, :], [wbb]), (QA[pr, qi_, :tn], [qab]),
                                     start=(not started[hh]), stop=False)
                                started[hh] = True
                            yield
                        for hh in range(2):
                            for kc in range(2, NKC):
                                dl = q0 - ((kc - 2) % 16) * 128
                                if -384 <= dl <= 0:
                                    work.append((hh, kc))
                    last_of = {}
                    for (hh, kc) in work:
                        last_of[hh] = kc
                    def rscore(hh, kc):
                        pr = slice(hh * 64, hh * 64 + 64)
                        sbi_, sb_ = k.bank(hold=True)
                        k.mm((sb_[0][:, :tn], sb_[1]), (Kr[pr, kc * 128:(kc + 1) * 128], [Krb]), self.Q(1, p, bi, parts=pr))
                        return (sbi_, sb_)

                    def rfinish(hh, kc, sbh):
                        sbi_, sb_ = sbh
                        pr = slice(hh * 64, hh * 64 + 64)
                        pi, pb = Pmr.next()
                        p_ = (Pm[:, pi, :tn], [pb])
                        S_ = (sb_[0][:, :tn], sb_[1])
                        if mi == 1:
                            k.tt(p_, S_, (PMc[:, hh, kc, :tn], [PMb]), ALU.mult)
                        else:
                            r = (kc - 2) // 16
                            k.tt(p_, S_, (PM[:, hh, r, (-(q0 - ((kc - 2) % 16) * 128)) // 128, :tn], [PMb]), ALU.mult)
                        k.release(sbi_)
                        k.mm((acc[0][pr, :tn], acc[1]), (Vr[:, kc, hh * 64:(hh + 1) * 64], [Vrb]), p_,
                             start=(not started[hh]), stop=(last_of[hh] == kc))
                        started[hh] = True

                    rp = []
                    for (hh, kc) in work:
                        rp.append((hh, kc, rscore(hh, kc)))
                        if len(rp) > 2:
                            a_, b_, c_ = rp.pop(0)
                            rfinish(a_, b_, c_)
                            yield
                    while rp:
                        a_, b_, c_ = rp.pop(0)
                        rfinish(a_, b_, c_)
                        yield
                    o = (st[:, 0, :tn], [stb[0]])
                    sq = (st[:, 1, :tn], [stb[1]])
                    mean = (st[:, 2, :tn], [stb[2]])
                    msq = (st[:, 3, :tn], [stb[3]])
                    var = (st[:, 4, :tn], [stb[4]])
                    dd = (st[:, 1, :tn], [stb[1]])
                    k.actf(o, (acc[0][:, :tn], acc[1]), AF.Identity)
                    k.actf(sq, (acc[0][:, :tn], acc[1]), AF.Square)
                    k.release(ai)
                    _, b1 = k.bank()
                    k.mm((b1[0][:, :tn], b1[1]), self.CF_("bones"), o)
                    k.actf(mean, (b1[0][:, :tn], b1[1]), AF.Identity, scale=1.0 / 64)
                    _, b2 = k.bank()
                    k.mm((b2[0][:, :tn], b2[1]), self.CF_("bones"), sq)
                    k.tt(msq, mean, mean, ALU.mult)
                    k.stt(var, (b2[0][:, :tn], b2[1]), 1.0 / 64, msq, ALU.mult, ALU.subtract)
                    k.actf(var, var, AF.Sqrt, bias=self.epsap)
                    k.recip(var, var)
                    k.tt(dd, o, mean, ALU.subtract)
                    k.stt(dd, dd, self.C_(f"retn{l}", p, 1), var, ALU.mult, ALU.mult)
                    ro = (rout[:, :tn], [rob])
                    k.tt(ro, dd, self.Q(2, p, bi), ALU.mult)
                    yield
                    for jj in range(DC):
                        _, pd = k.bank()
                        k.mm((pd[0][:, :tn], pd[1]), (wor[:, jj * 128:(jj + 1) * 128], [worb]), ro)
                        k.stt(self.H(jj, bi), (pd[0][:, :tn], pd[1]), self.DER(1, 1, jj, mi), self.H(jj, bi), ALU.mult, ALU.add)
                        yield

                for bi, (t0, tn, mi) in enumerate(BLOCKS):
                    if mi == 1 and not need_ctx:
                        continue
                    for g in (ret_gen(bi), att_gen(bi)):
                        for _ in g:
                            pass
            k.barrier()

    def fft_phase(self, l, need_ctx):
        k = self.k
        k.barrier()
        wout = self.inp(f"wout{l}", [D, D])
        dft = self.inp("dft", [S, 2, SL], BF16).rearrange("(nt p) c k -> p nt c k", p=128)
        with contextlib.ExitStack() as es:
            AB = self.sb(es, "fAB", [128, 32, 2, 256], BF16)
            ABb = Buf()
            fst = self.sb(es, "fst", [128, 2, 2, 512], BF16)
            fsr = Ring(2)
            tr = self.sb(es, "ftr", [128, 6, 2, 512], BF16)
            trr = Ring(6)
            fo = self.sb(es, "fo", [128, 2, 512], BF16)
            fob = Buf()
            wo = self.sb(es, "fwo", [128, 2, D], BF16)
            wob = Buf()
            k.dma(k.pool, wo[:], wout[256:512, :].rearrange("(c p) f -> p c f", p=128), [], [wob], k.dsem("wo"))

            def build_ab(dst, dbuf, srcs, ntok):
                for pc in range(0, ntok, 512):
                    n = min(512, ntok - pc)
                    fi, fb = fsr.next()
                    for ci in range(2):
                        for (src, off, nn) in srcs(ci, pc, n):
                            k.dma(k.sp, fst[:, fi, ci, off:off + nn], src, self.kvrd, [fb], k.dsem(f"vst{fi}"))
                    for tl in range(n // 128):
                        tt_ = pc // 128 + tl
                        for ci in range(2):
                            _, bk = k.bank()
                            k.mm((bk[0][:, 0:256], bk[1]), (fst[:, fi, ci, tl * 128:(tl + 1) * 128], [fb]), self.CB_("fd"))
                            k.actf((dst[:, tt_, ci, :], [dbuf]), (bk[0][:, 0:256], bk[1]), AF.Identity)

            def lat_src(ci, c0, n):
                out = []
                for (s0, t) in ((0, self.kva[1][(2 + ci) * 128:(3 + ci) * 128, :]), (SL, self.kva[1][(6 + ci) * 128:(7 + ci) * 128, :])):
                    a, b = max(c0, s0), min(c0 + n, s0 + SL)
                    if a < b:
                        out.append((t[:, a - s0:b - s0], a - c0, b - a))
                return out

            build_ab(AB, ABb, lat_src, S)
            for kb in range(4):
                t0, tn, mi = BLOCKS[kb]
                a0, acc0 = k.bank(hold=True)
                a1, acc1 = k.bank(hold=True)
                accs = [acc0, acc1]
                for nt_ in range(32):
                    ti, tb = trr.next()
                    k.dma(k.sp, tr[:, ti], dft[:, nt_, :, kb * 512:(kb + 1) * 512], [], [tb], k.dsem(f"dft{ti}"))
                    for c in range(2):
                        k.mm((accs[c][0], accs[c][1]), (AB[:, nt_, c, 0:128], [ABb]), (tr[:, ti, 0, :], [tb]),
                             start=(nt_ == 0), stop=False, inc=False)
                        k.mm((accs[c][0], accs[c][1]), (AB[:, nt_, c, 128:256], [ABb]), (tr[:, ti, 1, :], [tb]),
                             start=False, stop=(nt_ == 31), inc=(c == 1))
                for c in range(2):
                    k.actf((fo[:, c, :], [fob]), accs[c], AF.Identity)
                k.release(a0)
                k.release(a1)
                for j in range(DC):
                    _, pd = k.bank()
                    for c in range(2):
                        k.mm((pd[0][:, :tn], pd[1]), (wo[:, c, j * 128:(j + 1) * 128], [wob]), (fo[:, c, :tn], [fob]),
                             start=(c == 0), stop=(c == 1), inc=(c == 1))
                    k.stt(self.H(j, kb), (pd[0][:, :tn], pd[1]), self.DER(1, 1, j, mi), self.H(j, kb), ALU.mult, ALU.add)
            if need_ctx:
                dftc = self.inp("dftc", [C, 2, C], BF16).rearrange("(nt p) c k -> p nt c k", p=128)
                ABc = self.sb(es, "fABc", [128, 2, 2, 256], BF16)
                ABcb = Buf()
                trc = self.sb(es, "ftrc", [128, 2, 2, C], BF16)
                trcb = Buf()
                k.dma(k.sp, trc[:], dftc, [], [trcb], k.dsem("dft0"))
                build_ab(ABc, ABcb, lambda ci, c0, n: [(self.kvc[(6 + ci) * 128:(7 + ci) * 128, c0:c0 + n], 0, n)], C)
                t0, tn, mi = BLOCKS[4]
                a0, acc0 = k.bank(hold=True)
                a1, acc1 = k.bank(hold=True)
                accs = [acc0, acc1]
                for nt_ in range(2):
                    for c in range(2):
                        k.mm((accs[c][0][:, :tn], accs[c][1]), (ABc[:, nt_, c, 0:128], [ABcb]), (trc[:, nt_, 0, :], [trcb]),
                             start=(nt_ == 0), stop=False, inc=False)
                        k.mm((accs[c][0][:, :tn], accs[c][1]), (ABc[:, nt_, c, 128:256], [ABcb]), (trc[:, nt_, 1, :], [trcb]),
                             start=False, stop=(nt_ == 1), inc=(c == 1))
                for c in range(2):
                    k.actf((fo[:, c, :tn], [fob]), (accs[c][0][:, :tn], accs[c][1]), AF.Identity)
                k.release(a0)
                k.release(a1)
                for j in range(DC):
                    _, pd = k.bank()
                    for c in range(2):
                        k.mm((pd[0][:, :tn], pd[1]), (wo[:, c, j * 128:(j + 1) * 128], [wob]), (fo[:, c, :tn], [fob]),
                             start=(c == 0), stop=(c == 1), inc=(c == 1))
                    k.stt(self.H(j, 4), (pd[0][:, :tn], pd[1]), self.DER(1, 1, j, mi), self.H(j, 4), ALU.mult, ALU.add)
            k.barrier()

    def final_norm(self):
        k = self.k
        k.barrier()
        y = self.outp("yT", [D, SL])
        with contextlib.ExitStack() as es:
            sq8 = self.sb(es, "sq8", [128, 3, 512], BF16)
            sqr = Ring(3)
            rs = self.sb(es, "rs", [128, 2, 512], F32)
            rsr = Ring(2)
            tt = self.sb(es, "mtt", [128, 2, 512], F32)
            ttr = Ring(2)
            yo = self.sb(es, "yo", [128, 3, 512], F32)
            yr = Ring(3)
            for bi, (t0, tn, mi) in enumerate(LAT):
                _, bk = k.bank()
                for dc in range(DC):
                    qi, qbf = sqr.next()
                    k.actf((sq8[:, qi, :tn], [qbf]), self.H(dc, bi), AF.Square)
                    k.mm((bk[0][:, :tn], bk[1]), self.CB_("ones"), (sq8[:, qi, :tn], [qbf]), start=(dc == 0),
                         stop=(dc == DC - 1))
                ri, rbuf = rsr.next()
                r = (rs[:, ri, :tn], [rbuf])
                k.actf(r, (bk[0][:, :tn], bk[1]), AF.Sqrt, scale=1.0 / D, bias=self.epsap)
                k.recip(r, r)
                for dc in range(DC):
                    yi, yb = yr.next()
                    o = (yo[:, yi, :tn], [yb])
                    k.stt(o, self.H(dc, bi), self.C_("fng", dc, 1), r, ALU.mult, ALU.mult)
                    k.dma(k.sp, y[dc * 128:(dc + 1) * 128, t0:t0 + tn], o[0], o[1], [], k.dsem("yout"))
            k.barrier()


def build_prog(seg, fused=False, phases=None):
    p = Prog(seg, fused)
    if phases is not None:
        p.phases = phases
    p.epsap = EPS
    nc = p.build()
    return p, nc


_PROGS = {}


def get_prog(seg):
    if seg not in _PROGS:
        _PROGS[seg] = build_prog(seg)
    return _PROGS[seg]


def kernel(**inp):
    inp = {k_: np.asarray(v) for k_, v in inp.items()}
    cf, cb = host_static()
    cores = list(range(8))
    csts = [host_consts(inp, c // 2, c % 2) for c in cores]
    ropes = [host_rope(s) for s in range(2)]
    wext = [host_w_in_ext(inp["w_in"][l]) for l in range(2)]
    wsT = [np.ascontiguousarray(np.transpose(inp["gmlp_w_s"][l], (2, 0, 1)).reshape(128, 512), np.float32) for l in range(2)]
    adaw = [np.ascontiguousarray(inp["ada_w"][l]) for l in range(2)]

    def base(c):
        return {"cst": csts[c], "cf": cf, "cb": cb}

    def full(name, c, state):
        b, s = c // 2, c % 2
        if name in ("cst", "cf", "cb"):
            return base(c)[name]
        if name == "h0":
            return np.ascontiguousarray(np.concatenate([inp["x"][b, s * SL:(s + 1) * SL].T, inp["ctx"][b].T], axis=1), np.float32)
        if name == "rope":
            return ropes[s]
        if name == "dft":
            return host_dft(s)[0]
        if name == "dftc":
            return host_dft(s)[1]
        for l in range(2):
            if name == f"ada_w{l}":
                return adaw[l]
            if name == f"win{l}":
                return wext[l]
            if name == f"wout{l}":
                return np.ascontiguousarray(inp["w_out"][l])
            if name == f"wsT{l}":
                return wsT[l]
            for w in (1, 2):
                if name == f"wgu{w}_{l}":
                    return np.ascontiguousarray(inp[f"ffn{w}_w_gu"][l])
                if name == f"wd{w}_{l}":
                    return np.ascontiguousarray(inp[f"ffn{w}_w_down"][l])
        if name.startswith("i_st_"):
            return state[c]["o_st_" + name[5:]]
        if name == "kvf_own":
            return state[c]["kvf_loc"]
        if name == "kvf_oth":
            return state[c ^ 1]["kvf_loc"]
        if name == "kvf_ctx_in":
            return state[c]["kvf_ctx"]
        raise KeyError(name)

    p, nc = get_prog(0)
    in_maps = [{n: full(n, c, None) for n in p.in_names} for c in cores]
    res = run_bass_kernel_spmd(nc, in_maps, core_ids=cores)
    state = res.results
    out = np.empty((4, S, D), np.float32)
    for c in cores:
        b, s = c // 2, c % 2
        out[b, s * SL:(s + 1) * SL, :] = state[c]["yT"].T
    return out
```

```python
import contextlib
import os
import math
import numpy as np
import ml_dtypes
import concourse.bass as bass
import concourse.mybir as mybir
from concourse.bass_utils import run_bass_kernel_spmd

F32 = mybir.dt.float32
BF16 = mybir.dt.bfloat16
AF = mybir.ActivationFunctionType
ALU = mybir.AluOpType
NPBF = ml_dtypes.bfloat16

D = 1024
DC = 8
S = 4096
SL = 2048
C = 256
NT = SL + C
DFF = 2816
EPS = 1e-6
BLOCKS = [(0, 512, 0), (512, 512, 0), (1024, 512, 0), (1536, 512, 0), (2048, 256, 1)]
LAT = BLOCKS[:4]
NKC = 34
FFN_GENS = [(0, 3), (3, 3), (6, 3), (9, 3), (12, 3), (15, 3), (18, 3), (21, 1)]
SAME_ENG_SYNC = True

CST = {}
_off = 0


def _c(name, w):
    global _off
    CST[name] = (_off, w)
    _off += w


for _l in range(2):
    _c(f"adab{_l}", 72)
    _c(f"adabh{_l}", 36)
    _c(f"g1_{_l}", 8)
    _c(f"gm_{_l}", 8)
    _c(f"g2_{_l}", 8)
    _c(f"retn{_l}", 2)
    _c(f"gmn{_l}", 2)
    _c(f"aqn{_l}", 1)
    _c(f"aqs{_l}", 1)
    _c(f"akn{_l}", 1)
    _c(f"aks{_l}", 1)
    _c(f"lgf{_l}", 4)
    _c(f"lgb{_l}", 4)
    _c(f"bt{_l}", 256)
_c("fng", 8)
_c("cc", 16)
_c("pmd", 8)
_c("pmdn", 8)
_c("sgf", 56)
_c("sgb", 56)
_c("dlt", 56)
_c("d1", 8)
_c("d2", 8)
NCST = _off

CF = {"ones": (0, 128), "bones": (128, 128), "shift": (256, 64), "iota": (320, 512)}
NCF = 832
CB = {"ident": (0, 128), "ones": (128, 128), "bones": (256, 128), "fd": (384, 256)}
NCB = 640


def _fm(v):
    v = np.asarray(v, np.float32).reshape(-1, 128)
    return np.ascontiguousarray(v.T)


def _swap32(a, axis=-1):
    a = np.moveaxis(a, axis, -1)
    sh = a.shape
    b = a.reshape(sh[:-1] + (sh[-1] // 64, 2, 32))[..., ::-1, :].reshape(sh)
    return np.moveaxis(b, -1, axis)


def host_consts(inp, b, s):
    cst = np.zeros((128, NCST), np.float32)

    def put(name, arr):
        o, w = CST[name]
        arr = np.asarray(arr, np.float32)
        assert arr.shape == (128, w), (name, arr.shape)
        cst[:, o:o + w] = arr

    for l in range(2):
        put(f"adab{l}", _fm(inp["ada_b"][l]))
        put(f"adabh{l}", _fm(inp["ada_b"][l])[:, s * 36:(s + 1) * 36])
        put(f"g1_{l}", _fm(inp["norm_ffn1"][l]))
        put(f"gm_{l}", _fm(inp["norm_mix"][l]))
        put(f"g2_{l}", _fm(inp["norm_ffn2"][l]))
        put(f"retn{l}", _fm(inp["ret_norm"][l]))
        put(f"gmn{l}", _fm(inp["gmlp_norm"][l]))
        qn = np.asarray(inp["att_q_norm"][l], np.float32)
        kn = np.asarray(inp["att_k_norm"][l], np.float32)
        put(f"aqn{l}", np.tile(qn, 2)[:, None])
        put(f"aqs{l}", np.tile(_swap32(qn), 2)[:, None])
        put(f"akn{l}", np.tile(kn, 2)[:, None])
        put(f"aks{l}", np.tile(_swap32(kn), 2)[:, None])
        put(f"lgf{l}", np.broadcast_to(np.asarray(inp["ret_log_decay_fwd"][l], np.float32)[None, :], (128, 4)))
        put(f"lgb{l}", np.broadcast_to(np.asarray(inp["ret_log_decay_bwd"][l], np.float32)[None, :], (128, 4)))
        bs = np.asarray(inp["gmlp_b_s"][l], np.float32)
        bt = np.zeros((128, 2, 128), np.float32)
        for g in range(4):
            bt[(g % 2) * 64:(g % 2) * 64 + 64, g // 2, :] = bs[g][None, :]
        put(f"bt{l}", bt.reshape(128, 256))
    put("fng", _fm(inp["final_norm"]))
    cc = np.zeros((128, 8, 2), np.float32)
    cc[:, :, 0] = _fm(inp["c"][b])
    cc[:, :, 1] = _fm(inp["c_ctx"])
    put("cc", cc.reshape(128, 16))
    pmd = np.array([(s - r) * 2048 - 128 * m for r in range(2) for m in range(4)], np.float32)
    put("pmd", np.broadcast_to(pmd[None, :], (128, 8)))
    put("pmdn", np.broadcast_to(-pmd[None, :], (128, 8)))
    dlt = np.array([(s - r) * 2048 + 128 * (di - 15) for r in range(2) for di in range(28)], np.float32)
    sgf = (dlt > 0).astype(np.float32)
    put("dlt", np.broadcast_to(dlt[None, :], (128, 56)))
    put("sgf", np.broadcast_to(sgf[None, :], (128, 56)))
    put("sgb", np.broadcast_to((sgf - 1.0)[None, :], (128, 56)))
    d1 = np.zeros(8, np.float32)
    d2 = np.zeros(8, np.float32)
    for qb in range(4):
        for kc in range(2):
            d1[qb * 2 + kc] = s * 2048 + qb * 512 + 256 - kc * 128
            d2[qb * 2 + kc] = 4096 - s * 2048 - qb * 512 + kc * 128
    put("d1", np.broadcast_to(d1[None, :], (128, 8)))
    put("d2", np.broadcast_to(d2[None, :], (128, 8)))
    return cst


def host_static():
    cf = np.zeros((128, NCF), np.float32)
    cf[:, 0:128] = 1.0
    bo = np.zeros((128, 128), np.float32)
    bo[0:64, 0:64] = 1.0
    bo[64:128, 64:128] = 1.0
    cf[:, 128:256] = bo
    sh = np.zeros((128, 64), np.float32)
    sh[64 + np.arange(64), np.arange(64)] = 1.0
    cf[:, 256:320] = sh
    cf[:, 320:832] = np.arange(512, dtype=np.float32)[None, :] - np.arange(128, dtype=np.float32)[:, None]
    cb = np.zeros((128, NCB), np.float32)
    cb[:, 0:128] = np.eye(128)
    cb[:, 128:256] = 1.0
    cb[:, 256:384] = bo
    de = np.outer(np.arange(64), np.arange(64)).astype(np.float64) * (2 * np.pi / 64)
    c64, s64 = np.cos(de), np.sin(de)
    fd = np.zeros((128, 256))
    fd[0:64, 0:64] = c64
    fd[64:128, 64:128] = c64
    fd[0:64, 128:192] = s64
    fd[64:128, 192:256] = s64
    cb[:, 384:640] = fd
    return cf, cb.astype(NPBF)


def host_rope(s):
    p = np.arange(128)
    f = p % 32
    sign = np.where((p % 64) < 32, -1.0, 1.0)[:, None]
    idx = s * SL + np.arange(SL)
    row = (idx // 64).astype(np.float64)
    col = (idx % 64).astype(np.float64)
    ax_freq = 10000.0 ** (-np.arange(16, dtype=np.float64) / 16)
    ang = np.concatenate([row[:, None] * ax_freq, col[:, None] * ax_freq], -1)
    angA = ang[:, f].T
    ret_freq = 1.0 / (10000.0 ** np.linspace(0.0, 1.0, 32))
    angR = ((C + idx)[:, None] * ret_freq)[:, f].T
    angRc = (np.arange(C)[:, None] * ret_freq)[:, f].T
    t = np.zeros((128, 4, NT), np.float32)
    t[:, 0, :SL] = np.cos(angA)
    t[:, 1, :SL] = np.sin(angA) * sign
    t[:, 0, SL:] = 1.0
    t[:, 2, :SL] = np.cos(angR)
    t[:, 3, :SL] = np.sin(angR) * sign
    t[:, 2, SL:] = np.cos(angRc)
    t[:, 3, SL:] = np.sin(angRc) * sign
    return t


_DFT_CACHE = {}


def host_dft(s):
    if s in _DFT_CACHE:
        return _DFT_CACHE[s]
    n = np.arange(S).astype(np.int64)
    kk = (s * SL + np.arange(SL)).astype(np.int64)
    ph = (np.outer(n, kk) % S).astype(np.float64) * (2 * np.pi / S)
    t = np.empty((S, 2, SL), NPBF)
    t[:, 0, :] = (np.cos(ph) / 512.0).astype(NPBF)
    t[:, 1, :] = (-np.sin(ph) / 512.0).astype(NPBF)
    ph = np.outer(np.arange(C), np.arange(C)).astype(np.float64) * (2 * np.pi / C)
    tc = np.empty((C, 2, C), NPBF)
    tc[:, 0, :] = (np.cos(ph) / 128.0).astype(NPBF)
    tc[:, 1, :] = (-np.sin(ph) / 128.0).astype(NPBF)
    _DFT_CACHE[s] = (t, tc)
    return t, tc


def host_w_in_ext(w):
    def sw(x):
        return _swap32(x, axis=1)
    retq, retk, retv, retg = w[:, 0:256], w[:, 256:512], w[:, 512:768], w[:, 768:1024]
    fnet = w[:, 1024:1280]
    aq = w[:, 1280:1536].reshape(1024, 4, 64)
    aq = np.concatenate([aq[:, 0], aq[:, 2], aq[:, 1], aq[:, 3]], axis=1)
    ak, av = w[:, 1536:1664], w[:, 1664:1792]
    gu, gv = w[:, 1792:2048], w[:, 2048:2304]
    return np.ascontiguousarray(np.concatenate(
        [retq, sw(retq), retk, sw(retk), retv, retg, fnet, aq, sw(aq), ak, sw(ak), av, gu, gv], axis=1), np.float32)


class Sem:
    def __init__(self, h, dma=False):
        self.h = h
        self.n = 0
        self.dma = dma


class Eng:
    def __init__(self, name, e, sem):
        self.name = name
        self.e = e
        self.sem = sem
        self.waited = {}
        self.pending = False


class Buf:
    __slots__ = ("w", "r", "name", "small")

    def __init__(self, name=""):
        self.w = {}
        self.r = {}
        self.name = name
        self.small = {}


class KB:
    def __init__(self, nc, es):
        self.nc = nc
        self.es = es
        self.sems = []
        self.pe = Eng("pe", nc.tensor, self.newsem("pe"))
        self.act = Eng("act", nc.scalar, self.newsem("act"))
        self.dve = Eng("dve", nc.vector, self.newsem("dve"))
        self.pool = Eng("pool", nc.gpsimd, None)
        self.sp = Eng("sp", nc.sync, None)
        self.engs = [self.pe, self.act, self.dve, self.pool, self.sp]
        self.ps = es.enter_context(nc.psum_tensor("ps", [128, 6, 512], F32))
        self.psb = es.enter_context(nc.psum_tensor("psb", [128, 2, 1024], BF16))
        self.banks = [Buf(f"bank{i}") for i in range(6)]
        self.bbanks = [Buf(f"bbank{i}") for i in range(2)]
        self.bptr = 0
        self.bbptr = 0
        self.held = set()
        self.dsems = {}

    def newsem(self, name, dma=False):
        s = Sem(self.es.enter_context(self.nc.semaphore(name)), dma)
        self.sems.append(s)
        return s

    def dsem(self, name):
        if name not in self.dsems or self.dsems[name].n > 2400:
            self.nds = getattr(self, "nds", 0) + 1
            self.dsems[name] = self.newsem(f"d{self.nds}_" + name, dma=True)
        return self.dsems[name]

    def bank(self, hold=False):
        for _ in range(8):
            i = self.bptr
            self.bptr = (self.bptr + 1) % 6
            if i not in self.held:
                if hold:
                    self.held.add(i)
                return i, (self.ps[:, i, :], [self.banks[i]])
        raise RuntimeError("no bank")

    def release(self, i):
        self.held.discard(i)

    def bbank(self):
        i = self.bbptr
        self.bbptr = (self.bbptr + 1) % 2
        return (self.psb[:, i, :], [self.bbanks[i]])

    def _wait(self, E, reads, writes):
        deps = {}
        for b in reads:
            for s, v in b.w.items():
                if deps.get(s, 0) < v:
                    deps[s] = v
        for b in writes:
            for s, v in b.w.items():
                if deps.get(s, 0) < v:
                    deps[s] = v
            for s, v in b.r.items():
                if deps.get(s, 0) < v:
                    deps[s] = v
        for s, v in deps.items():
            if s is E.sem:
                if E is self.pe or not SAME_ENG_SYNC:
                    continue
                if v > s.n:
                    continue
                if E is self.dve:
                    sm = 0
                    for b in list(reads) + list(writes):
                        sm = max(sm, b.small.get(s, 0))
                    if sm == 0:
                        continue
                    v = min(v, sm)
            if s.dma:
                v = s.n
            elif s is not E.sem:
                sm = False
                for b in list(reads) + list(writes):
                    if b.small.get(s, 0) == v:
                        sm = True
                        break
                if sm:
                    v = max(v, min(v + 1, s.n))
            if E.waited.get(s, 0) < v:
                E.e.wait_ge(s.h, v)
                E.waited[s] = v

    def op(self, E, fn, reads, writes, inc=True, small=True):
        if E is not self.pe:
            assert not self.pe.pending
        self._wait(E, reads, writes)
        ins = fn()
        if inc:
            E.sem.n += 1
            ins.then_inc(E.sem.h, 1)
            ev = E.sem.n
            E.pending = False
        else:
            assert E is self.pe
            ev = E.sem.n + 1
            E.pending = True
        s = E.sem
        for b in reads:
            if b.r.get(s, 0) < ev:
                b.r[s] = ev
        for b in writes:
            if b.w.get(s, 0) < ev:
                b.w[s] = ev
            if small:
                b.small[s] = ev
        return ins

    def dma(self, E, out_ap, in_ap, reads, writes, sem):
        assert not self.pe.pending
        self._wait(E, reads, writes)
        ins = E.e.dma_start(out=out_ap, in_=in_ap)
        sem.n += 16
        ins.then_inc(sem.h, 16)
        for b in reads:
            b.r[sem] = sem.n
        for b in writes:
            b.w[sem] = sem.n
        return ins

    def barrier(self):
        assert not self.pe.pending
        for E in self.engs:
            for s in self.sems:
                if s is E.sem:
                    continue
                if E.waited.get(s, 0) < s.n:
                    E.e.wait_ge(s.h, s.n)
                    E.waited[s] = s.n
        for E in (self.pe, self.act, self.dve):
            if E.sem.n > 1500:
                self.nes = getattr(self, "nes", 0) + 1
                E.sem = self.newsem(f"{E.name}{self.nes}")

    def mm(self, out, lhsT, rhs, start=True, stop=True, inc=True):
        return self.op(self.pe, lambda: self.nc.tensor.matmul(out[0], lhsT[0], rhs[0], start=start, stop=stop),
                       lhsT[1] + rhs[1], out[1], inc=inc, small=self._small(out[0]))

    def transpose(self, out, in_, ident):
        return self.op(self.pe, lambda: self.nc.tensor.transpose(out[0], in_[0], ident[0]),
                       in_[1] + ident[1], out[1], small=True)

    def actf(self, out, in_, func, scale=1.0, bias=0.0):
        rd = list(in_[1])
        sc, bi = scale, bias
        if isinstance(scale, tuple):
            rd += scale[1]
            sc = scale[0]
        if isinstance(bias, tuple):
            rd += bias[1]
            bi = bias[0]
        return self.op(self.act, lambda: self.nc.scalar.activation(out=out[0], in_=in_[0], func=func, bias=bi, scale=sc),
                       rd, out[1], small=self._small(out[0]))

    @staticmethod
    def _small(ap):
        n = 1
        for d_ in list(ap.shape)[1:]:
            n *= int(d_)
        return n < 256

    def tt(self, out, a, b, op):
        return self.op(self.dve, lambda: self.nc.vector.tensor_tensor(out=out[0], in0=a[0], in1=b[0], op=op),
                       a[1] + b[1], out[1], small=self._small(out[0]))

    def ts(self, out, a, s1, op0, s2=None, op1=None):
        rd = list(a[1])
        v1, v2 = s1, s2
        if isinstance(s1, tuple):
            rd += s1[1]
            v1 = s1[0]
        if isinstance(s2, tuple):
            rd += s2[1]
            v2 = s2[0]
        if op1 is None:
            return self.op(self.dve, lambda: self.nc.vector.tensor_scalar(out=out[0], in0=a[0], scalar1=v1, scalar2=None, op0=op0),
                           rd, out[1], small=self._small(out[0]))
        return self.op(self.dve, lambda: self.nc.vector.tensor_scalar(out=out[0], in0=a[0], scalar1=v1, scalar2=v2, op0=op0, op1=op1),
                       rd, out[1], small=self._small(out[0]))

    def stt(self, out, a, sc, b, op0, op1):
        rd = list(a[1]) + list(b[1])
        v = sc
        if isinstance(sc, tuple):
            rd += sc[1]
            v = sc[0]
        return self.op(self.dve, lambda: self.nc.vector.scalar_tensor_tensor(out=out[0], in0=a[0], scalar=v, in1=b[0], op0=op0, op1=op1),
                       rd, out[1], small=self._small(out[0]))

    def recip(self, out, a):
        return self.op(self.dve, lambda: self.nc.vector.reciprocal(out=out[0], in_=a[0]), a[1], out[1], small=self._small(out[0]))

    def copy(self, out, a):
        return self.op(self.dve, lambda: self.nc.vector.tensor_copy(out=out[0], in_=a[0]), a[1], out[1], small=self._small(out[0]))

    def memset(self, out, val):
        return self.op(self.dve, lambda: self.nc.vector.memset(out[0], val), [], out[1])


class Ring:
    def __init__(self, n):
        self.n = n
        self.i = 0
        self.bufs = [Buf() for _ in range(n)]

    def next(self):
        i = self.i
        self.i = (self.i + 1) % self.n
        return i, self.bufs[i]


class Prog:
    def __init__(self, seg, fused):
        self.phases = ("L0", "L1", "mix", "fft", "ffn2", "ffn1")
        self.seg = seg
        self.fused = fused
        self.in_names = {}
        self.out_names = {}

    def inp(self, name, shape, dt=F32):
        if name not in self.in_names:
            self.in_names[name] = self.nc.dram_tensor(name, list(shape), dt, kind="ExternalInput").ap()
        return self.in_names[name]

    def outp(self, name, shape, dt=F32):
        if name not in self.out_names:
            self.out_names[name] = self.nc.dram_tensor(name, list(shape), dt, kind="ExternalOutput").ap()
        return self.out_names[name]

    def sb(self, es, name, shape, dt):
        self._sbn = getattr(self, "_sbn", 0) + 1
        return es.enter_context(self.nc.sbuf_tensor(f"s{self._sbn}_{name}", list(shape), dt))

    def build(self):
        nc = bass.Bass("TRN2", target_bir_lowering=False)
        self.nc = nc
        seg = self.seg
        with contextlib.ExitStack() as es:
            k = KB(nc, es)
            self.k = k
            self.hT = self.sb(es, "hT", [128, DC, NT], F32)
            self.hb = [[Buf(f"h{dc}_{bi}") for bi in range(5)] for dc in range(DC)]
            self.cst = self.sb(es, "cst", [128, NCST], F32)
            self.cf = self.sb(es, "cf", [128, NCF], F32)
            self.cb = self.sb(es, "cb", [128, NCB], BF16)
            self.modv = self.sb(es, "modv", [128, 72, 2], F32)
            self.der = self.sb(es, "der", [128, 3, 2, 8, 2], F32)
            self.qs = self.sb(es, "qs", [128, 3, 2, NT], BF16)
            self.cbuf = Buf("const")
            self.modb = Buf("modv")
            self.derb = Buf("der")
            self.qb = [[Buf(f"q{i}_{bi}") for bi in range(5)] for i in range(3)]
            cs = k.dsem("const")
            k.dma(k.sp, self.cst[:], self.inp("cst", [128, NCST]), [], [self.cbuf], cs)
            k.dma(k.sp, self.cf[:], self.inp("cf", [128, NCF]), [], [self.cbuf], cs)
            k.dma(k.sp, self.cb[:], self.inp("cb", [128, NCB], BF16), [], [self.cbuf], cs)
            h0 = self.inp("h0", [D, NT])
            for dc in range(DC):
                k.dma(k.sp, self.hT[:, dc, :], h0[dc * 128:(dc + 1) * 128, :], [],
                      [self.hb[dc][bi] for bi in range(5)], k.dsem("hload"))
            self.ccsem = k.newsem("cc")
            for l in range(2):
                need_ctx = (l == 0)
                self.kvl = [nc.dram_tensor(f"kvloc{l}_{p_}", [4 * 128, SL], BF16).ap() for p_ in range(2)]
                self.kva = [nc.dram_tensor(f"kvall{l}_{p_}", [8 * 128, SL], BF16).ap() for p_ in range(2)]
                self.kvc = nc.dram_tensor(f"kvctx{l}", [8 * 128, C], BF16).ap()
                self.kvlb, self.kvab, self.kvcb = Buf("kvl"), Buf("kva"), Buf("kvc")
                self.kvrd = [self.kvab, self.kvcb]
                ph = self.phases
                if f"L{l}" not in ph:
                    continue
                self.ada(l)
                self.derive(l)
                if "ffn1" in ph:
                    self.ffn(l, 1, 0, BLOCKS)
                self.proj_phase(l, need_ctx)
                k._wait(k.pool, [self.kvlb], [self.kvab])
                for p_ in range(2):
                    ins = nc.gpsimd.collective_compute("AllGather", ALU.bypass, replica_groups=[[0, 1], [2, 3], [4, 5], [6, 7]],
                                                       ins=[self.kvl[p_].opt()], outs=[self.kva[p_].opt()])
                    ins.then_inc(self.ccsem.h)
                    self.ccsem.n += 1
                self.kvab.w[self.ccsem] = self.ccsem.n
                self.kvlb.r[self.ccsem] = self.ccsem.n
                if "mix" in ph:
                    self.mix_phase(l, need_ctx)
                if "attn" in ph:
                    self.attn_phase(l, need_ctx)
                if "ret" in ph:
                    self.ret_phase(l, need_ctx)
                if "fft" in ph:
                    self.fft_phase(l, need_ctx)
                if "ffn2" in ph:
                    self.ffn(l, 2, 2, BLOCKS if need_ctx else LAT)
            self.final_norm()
            k.barrier()
        return nc

    def C_(self, name, a=0, w=None):
        o, ww = CST[name]
        if w is None:
            w = ww - a
        return (self.cst[:, o + a:o + a + w], [self.cbuf])

    def CF_(self, name, parts=slice(0, 128)):
        o, w = CF[name]
        return (self.cf[parts, o:o + w], [self.cbuf])

    def CB_(self, name, a=0, w=None, parts=slice(0, 128)):
        o, ww = CB[name]
        if w is None:
            w = ww - a
        return (self.cb[parts, o + a:o + a + w], [self.cbuf])

    def H(self, dc, bi):
        t0, tn, _ = BLOCKS[bi]
        return (self.hT[:, dc, t0:t0 + tn], [self.hb[dc][bi]])

    def Q(self, i, ci, bi, parts=slice(0, 128)):
        t0, tn, _ = BLOCKS[bi]
        return (self.qs[parts, i, ci, t0:t0 + tn], [self.qb[i][bi]])

    def DER(self, sub, kind, dc, mi):
        return (self.der[:, sub, kind, dc, mi:mi + 1], [self.derb])

    def MODV(self, j, mi):
        return (self.modv[:, j, mi:mi + 1], [self.modb])

    def save_state(self):
        k = self.k
        k.barrier()
        ds = k.dsem("state")
        o = self.outp("o_st_h", [128, DC * NT])
        k.dma(k.sp, o, self.hT[:].rearrange("p a b -> p (a b)"), [b for r in self.hb for b in r], [], ds)
        o = self.outp("o_st_mod", [128, 144])
        k.dma(k.sp, o, self.modv[:].rearrange("p a b -> p (a b)"), [self.modb], [], ds)
        o = self.outp("o_st_q", [128, 6 * NT], BF16)
        k.dma(k.sp, o, self.qs[:].rearrange("p a b c -> p (a b c)"), [b for r in self.qb for b in r], [], ds)

    def load_state(self):
        k = self.k
        ds = k.dsem("state")
        i = self.inp("i_st_h", [128, DC * NT])
        k.dma(k.sp, self.hT[:].rearrange("p a b -> p (a b)"), i, [], [b for r in self.hb for b in r], ds)
        i = self.inp("i_st_mod", [128, 144])
        k.dma(k.sp, self.modv[:].rearrange("p a b -> p (a b)"), i, [], [self.modb], ds)
        i = self.inp("i_st_q", [128, 6 * NT], BF16)
        k.dma(k.sp, self.qs[:].rearrange("p a b c -> p (a b c)"), i, [], [b for r in self.qb for b in r], ds)

    def ada(self, l):
        k, nc = self.k, self.nc
        k.barrier()
        w = self.inp(f"ada_wh{l}", [D, 36 * 128])
        wv = w.rearrange("(dc p) f -> p dc f", p=128)
        modl = nc.dram_tensor(f"modl{l}", [128, 72], F32).ap()
        moda = nc.dram_tensor(f"moda{l}", [256, 72], F32).ap()
        mlb, mab = Buf(), Buf()
        with contextlib.ExitStack() as es:
            ring = self.sb(es, "adaw", [128, 2, DC, 512], BF16)
            rb = [Buf(), Buf()]
            sT = self.sb(es, "adas", [128, 16], BF16)
            sb_ = Buf()
            modh = self.sb(es, "modh", [128, 36, 2], F32)
            mhb = Buf()
            k.actf((sT[:], [sb_]), self.C_("cc"), AF.Silu)
            for km in range(2):
                k.dma(k.pool, ring[:, km % 2], wv[:, :, km * 512:(km + 1) * 512], [], [rb[km % 2]], k.dsem(f"wr{km % 2}"))
            for km in range(9):
                sl = km % 2
                bi_, bk = k.bank()
                for jc in range(4):
                    for dc in range(DC):
                        k.mm((bk[0][:, jc * 2:jc * 2 + 2], bk[1]), (ring[:, sl, dc, jc * 128:(jc + 1) * 128], [rb[sl]]),
                             (sT[:, dc * 2:dc * 2 + 2], [sb_]), start=(dc == 0), stop=(dc == DC - 1),
                             inc=(dc == DC - 1))
                for jc in range(4):
                    j = km * 4 + jc
                    k.ts((modh[:, j, :], [mhb]), (bk[0][:, jc * 2:jc * 2 + 2], bk[1]),
                         self.C_(f"adabh{l}", j, 1), ALU.add)
                if km + 2 < 9:
                    k.dma(k.pool, ring[:, sl], wv[:, :, (km + 2) * 512:(km + 3) * 512], [], [rb[sl]], k.dsem(f"wr{sl}"))
            k.dma(k.sp, modl, modh[:].rearrange("p a b -> p (a b)"), [mhb], [mlb], k.dsem("modx"))
            k._wait(k.pool, [mlb], [mab])
            ins = nc.gpsimd.collective_compute("AllGather", ALU.bypass, replica_groups=[[0, 1], [2, 3], [4, 5], [6, 7]],
                                               ins=[modl.opt()], outs=[moda.opt()])
            ins.then_inc(self.ccsem.h)
            self.ccsem.n += 1
            mab.w[self.ccsem] = self.ccsem.n
            mlb.r[self.ccsem] = self.ccsem.n
            for r in range(2):
                k.dma(k.sp, self.modv[:, r * 36:(r + 1) * 36, :].rearrange("p a b -> p (a b)"), moda[r * 128:(r + 1) * 128, :],
                      [mab], [self.modb], k.dsem("modx"))
            k.barrier()

    def derive(self, l):
        k = self.k
        gn = [f"g1_{l}", f"gm_{l}", f"g2_{l}"]
        for sub in range(3):
            k0 = 3 * sub
            for mi in range(2):
                k.stt((self.der[:, sub, 0, :, mi], [self.derb]), (self.modv[:, (k0 + 1) * 8:(k0 + 2) * 8, mi], [self.modb]),
                      1.0, self.C_(gn[sub]), ALU.add, ALU.mult)
                k.ts((self.der[:, sub, 1, :, mi], [self.derb]), (self.modv[:, (k0 + 2) * 8:(k0 + 3) * 8, mi], [self.modb]),
                     1.0 if sub == 1 else 0.5, ALU.mult)

    def modulate(self, l, sub, blocks, nT, nb, es):
        k = self.k
        sq8 = self.sb(es, "sq8", [128, 3, 512], BF16)
        sqr = Ring(3)
        rs = self.sb(es, "rs", [128, 2, 512], F32)
        rsr = Ring(2)
        tt = self.sb(es, "mtt", [128, 2, 512], F32)
        ttr = Ring(2)
        for (t0, tn, mi) in blocks:
            bi = t0 // 512
            _, bk = k.bank()
            for dc in range(DC):
                qi, qbf = sqr.next()
                k.actf((sq8[:, qi, :tn], [qbf]), self.H(dc, bi), AF.Square)
                k.mm((bk[0][:, :tn], bk[1]), self.CB_("ones"), (sq8[:, qi, :tn], [qbf]), start=(dc == 0),
                     stop=(dc == DC - 1))
            ri, rbuf = rsr.next()
            r = (rs[:, ri, :tn], [rbuf])
            k.actf(r, (bk[0][:, :tn], bk[1]), AF.Sqrt, scale=1.0 / D, bias=self.epsap)
            k.recip(r, r)
            for dc in range(DC):
                ti, tb = ttr.next()
                t = (tt[:, ti, :tn], [tb])
                k.tt(t, self.H(dc, bi), r, ALU.mult)
                k.actf((nT[:, dc, t0:t0 + tn], [nb[bi]]), t, AF.Identity, scale=self.DER(sub, 0, dc, mi),
                       bias=self.MODV(3 * sub * 8 + dc, mi))

    def ffn(self, l, which, sub, blocks):
        k = self.k
        k.barrier()
        wgu = self.inp(f"wgu{which}_{l}", [D, 2 * DFF]).rearrange("(dc p) f -> p dc f", p=128)
        wdn = self.inp(f"wd{which}_{l}", [DFF, D]).rearrange("(c p) f -> p c f", p=128)
        with contextlib.ExitStack() as es:
            nT = self.sb(es, "nT", [128, DC, NT], BF16)
            nb = [Buf() for _ in range(5)]
            ring = self.sb(es, "wring", [128, 2, 9216], BF16)
            rb = [Buf(), Buf()]
            hid = self.sb(es, "hid", [128, 2, 3, 512], BF16)
            hr = Ring(2)
            sa = self.sb(es, "sa", [128, 2, 512], F32)
            sar = Ring(2)

            def load(gi):
                c0, G = FFN_GENS[gi]
                sl = gi % 2
                wa = ring[:, sl, 0:8 * G * 128].rearrange("p (c f) -> p c f", c=8)
                wb = ring[:, sl, 3072:3072 + 8 * G * 128].rearrange("p (c f) -> p c f", c=8)
                wd = ring[:, sl, 6144:6144 + G * 1024].rearrange("p (c f) -> p c f", c=G)
                ds = k.dsem(f"wr{sl}")
                k.dma(k.pool, wa, wgu[:, :, c0 * 128:(c0 + G) * 128], [], [rb[sl]], ds)
                k.dma(k.pool, wb, wgu[:, :, DFF + c0 * 128:DFF + (c0 + G) * 128], [], [rb[sl]], ds)
                k.dma(k.pool, wd, wdn[:, c0:c0 + G, :], [], [rb[sl]], ds)

            load(0)
            load(1)
            with contextlib.ExitStack() as es2:
                self.modulate(l, sub, blocks, nT, nb, es2)
            for gi, (c0, G) in enumerate(FFN_GENS):
                sl = gi % 2
                wa = ring[:, sl, 0:8 * G * 128].rearrange("p (c f) -> p c f", c=8)
                wb = ring[:, sl, 3072:3072 + 8 * G * 128].rearrange("p (c f) -> p c f", c=8)
                wd = ring[:, sl, 6144:6144 + G * 1024].rearrange("p (c f) -> p c f", c=G)
                for (t0, tn, mi) in blocks:
                    bi = t0 // 512
                    hi, hbuf = hr.next()
                    for c in range(G):
                        _, pa = k.bank()
                        for dc in range(DC):
                            k.mm((pa[0][:, :tn], pa[1]), (wa[:, dc, c * 128:(c + 1) * 128], [rb[sl]]),
                                 (nT[:, dc, t0:t0 + tn], [nb[bi]]), start=(dc == 0), stop=(dc == DC - 1), inc=(dc == DC - 1))
                        _, pb = k.bank()
                        for dc in range(DC):
                            k.mm((pb[0][:, :tn], pb[1]), (wb[:, dc, c * 128:(c + 1) * 128], [rb[sl]]),
                                 (nT[:, dc, t0:t0 + tn], [nb[bi]]), start=(dc == 0), stop=(dc == DC - 1), inc=(dc == DC - 1))
                        si, sbuf = sar.next()
                        s_ = (sa[:, si, :tn], [sbuf])
                        k.actf(s_, (pa[0][:, :tn], pa[1]), AF.Silu)
                        k.tt((hid[:, hi, c, :tn], [hbuf]), s_, (pb[0][:, :tn], pb[1]), ALU.mult)
                    for j in range(DC):
                        _, pd = k.bank()
                        for c in range(G):
                            k.mm((pd[0][:, :tn], pd[1]), (wd[:, c, j * 128:(j + 1) * 128], [rb[sl]]),
                                 (hid[:, hi, c, :tn], [hbuf]), start=(c == 0), stop=(c == G - 1), inc=(c == G - 1))
                        k.stt(self.H(j, bi), (pd[0][:, :tn], pd[1]), self.DER(sub, 1, j, mi), self.H(j, bi), ALU.mult, ALU.add)
                if gi + 2 < len(FFN_GENS):
                    load(gi + 2)
            k.barrier()

    def proj_phase(self, l, need_ctx):
        k = self.k
        k.barrier()
        win = self.inp(f"win{l}", [D, 25 * 128]).rearrange("(dc p) f -> p dc f", p=128)
        wout = self.inp(f"wout{l}", [D, D])
        rope = self.inp("rope", [128, 4, NT])
        wsT = self.inp(f"wsT{l}", [128, 512])
        with contextlib.ExitStack() as es:
            nT = self.sb(es, "nT", [128, DC, NT], BF16)
            nb = [Buf() for _ in range(5)]
            with contextlib.ExitStack() as es2:
                self.modulate(l, 1, BLOCKS, nT, nb, es2)
                k.barrier()
            wr = self.sb(es, "pw", [128, 2, DC, 512], BF16)
            wrr = Ring(2)

            def loadw(c0, n):
                i, b = wrr.next()
                k.dma(k.pool, wr[:, i, :, 0:n * 128], win[:, :, c0 * 128:(c0 + n) * 128], [], [b], k.dsem(f"wr{i}"))
                return i, b

            def proj(wi, wbuf, ci, bi):
                t0, tn, mi = BLOCKS[bi]
                _, bk = k.bank()
                for dc in range(DC):
                    k.mm((bk[0][:, :tn], bk[1]), (wr[:, wi, dc, ci * 128:(ci + 1) * 128], [wbuf]),
                         (nT[:, dc, t0:t0 + tn], [nb[bi]]), start=(dc == 0), stop=(dc == DC - 1), inc=(dc == DC - 1))
                return (bk[0][:, :tn], bk[1])

            with contextlib.ExitStack() as es3:
                wi, wbuf = loadw(21, 4)
                wo = self.sb(es3, "gwo", [128, 2, D], BF16)
                wob = Buf()
                k.dma(k.pool, wo[:], wout[768:1024, :].rearrange("(c p) f -> p c f", p=128), [], [wob], k.dsem("wo"))
                ws = self.sb(es3, "gws", [128, 4, 128], BF16)
                wsb = Buf()
                k.dma(k.pool, ws[:].rearrange("p a b -> p (a b)"), wsT, [], [wsb], k.dsem("wo"))
                u = self.sb(es3, "gu", [128, 2, 512], F32)
                ub = Buf()
                gv = self.sb(es3, "gv", [128, 2, 512], F32)
                gvb = Buf()
                gq = self.sb(es3, "gq", [128, 2, 512], F32)
                gqb = Buf()
                st = self.sb(es3, "gst", [128, 4, 512], F32)
                stb = [Buf() for _ in range(4)]
                vn = self.sb(es3, "gvn", [128, 2, 512], BF16)
                vnb = Buf()
                vp = self.sb(es3, "gvp", [128, 2, 4, 128], BF16)
                vpr = Ring(2)
                go = self.sb(es3, "ggo", [128, 2, 512], BF16)
                gob = Buf()
                gt = self.sb(es3, "ggt", [128, 2, 128], F32)
                gtr = Ring(2)
                k.memset((vp[:], vpr.bufs), 0.0)
                for bi, (t0, tn, mi) in enumerate(BLOCKS):
                    if mi == 1 and not need_ctx:
                        continue
                    def gelu(dst, P):
                        a_ = (st[:, 0, :tn], [stb[0]])
                        b_ = (st[:, 1, :tn], [stb[1]])
                        k.actf(a_, P, AF.Square)
                        k.ts(a_, a_, 0.044715, ALU.mult, 1.0, ALU.add)
                        k.tt(a_, a_, P, ALU.mult)
                        k.actf(b_, a_, AF.Sigmoid, scale=1.5957691216057308)
                        k.tt(dst, b_, P, ALU.mult)

                    for c in range(2):
                        gelu((u[:, c, :tn], [ub]), proj(wi, wbuf, c, bi))
                    for c in range(2):
                        gelu((gv[:, c, :tn], [gvb]), proj(wi, wbuf, 2 + c, bi))
                        k.actf((gq[:, c, :tn], [gqb]), (gv[:, c, :tn], [gvb]), AF.Square)
                    _, b1 = k.bank()
                    for c in range(2):
                        k.mm((b1[0][:, :tn], b1[1]), self.CF_("ones"), (gv[:, c, :tn], [gvb]), start=(c == 0), stop=(c == 1), inc=(c == 1))
                    _, b2 = k.bank()
                    for c in range(2):
                        k.mm((b2[0][:, :tn], b2[1]), self.CF_("ones"), (gq[:, c, :tn], [gqb]), start=(c == 0), stop=(c == 1), inc=(c == 1))
                    mean = (st[:, 0, :tn], [stb[0]])
                    msq = (st[:, 1, :tn], [stb[1]])
                    var = (st[:, 2, :tn], [stb[2]])
                    dd = (st[:, 3, :tn], [stb[3]])
                    k.actf(mean, (b1[0][:, :tn], b1[1]), AF.Identity, scale=1.0 / 256)
                    k.tt(msq, mean, mean, ALU.mult)
                    k.stt(var, (b2[0][:, :tn], b2[1]), 1.0 / 256, msq, ALU.mult, ALU.subtract)
                    k.actf(var, var, AF.Sqrt, bias=self.epsap)
                    k.recip(var, var)
                    for c in range(2):
                        k.tt(dd, (gv[:, c, :tn], [gvb]), mean, ALU.subtract)
                        k.stt((vn[:, c, :tn], [vnb]), dd, self.C_(f"gmn{l}", c, 1), var, ALU.mult, ALU.mult)
                    for tl in range(tn // 128):
                        vi, vb = vpr.next()
                        for c in range(2):
                            pt = k.bbank()
                            k.transpose((pt[0][:, 0:128], pt[1]), (vn[:, c, tl * 128:(tl + 1) * 128], [vnb]), self.CB_("ident"))
                            k.copy((vp[:, vi, 2 * c, 0:64], [vb]), (pt[0][:, 0:64], pt[1]))
                            k.copy((vp[:, vi, 2 * c + 1, 64:128], [vb]), (pt[0][:, 64:128], pt[1]))
                        for c in range(2):
                            _, mb = k.bank()
                            for gg in range(2):
                                k.mm((mb[0][:, 0:128], mb[1]), (vp[:, vi, 2 * c + gg, :], [vb]), (ws[:, 2 * c + gg, :], [wsb]),
                                     start=(gg == 0), stop=(gg == 1), inc=(gg == 1))
                            gi_, gb_ = gtr.next()
                            g_ = (gt[:, gi_, :], [gb_])
                            o_, w_ = CST[f"bt{l}"]
                            k.tt(g_, (mb[0][:, 0:128], mb[1]), (self.cst[:, o_ + c * 128:o_ + (c + 1) * 128], [self.cbuf]), ALU.add)
                            k.tt((go[:, c, tl * 128:(tl + 1) * 128], [gob]), g_, (u[:, c, tl * 128:(tl + 1) * 128], [ub]), ALU.mult)
                    for j in range(DC):
                        _, pd = k.bank()
                        for c in range(2):
                            k.mm((pd[0][:, :tn], pd[1]), (wo[:, c, j * 128:(j + 1) * 128], [wob]), (go[:, c, :tn], [gob]),
                                 start=(c == 0), stop=(c == 1), inc=(c == 1))
                        k.stt(self.H(j, bi), (pd[0][:, :tn], pd[1]), self.DER(1, 1, j, mi), self.H(j, bi), ALU.mult, ALU.add)
                k.barrier()

            with contextlib.ExitStack() as es3:
                rr = self.sb(es3, "rope", [128, 2, 2, 512], F32)
                rrr = Ring(2)
                tmp = self.sb(es3, "ptmp", [128, 4, 512], F32)
                tr = Ring(4)
                stg = self.sb(es3, "pstg", [128, 3, 512], BF16)
                sr = Ring(3)
                sqt = self.sb(es3, "psq", [128, 2, 512], BF16)
                sqr = Ring(2)
                rst = self.sb(es3, "prs", [128, 2, 512], F32)
                rsr = Ring(2)

                def T_(tn):
                    i, b = tr.next()
                    return (tmp[:, i, :tn], [b])

                def store(src, row, bi):
                    t0, tn, mi = BLOCKS[bi]
                    if mi == 0:
                        k.dma(k.sp, self.kvl[row // 4][(row % 4) * 128:(row % 4 + 1) * 128, t0:t0 + tn], src[0], src[1], [self.kvlb], k.dsem("kvst"))
                    else:
                        k.dma(k.sp, self.kvc[row * 128:(row + 1) * 128, 0:tn], src[0], src[1], [self.kvcb], k.dsem("kvst"))

                def rope_unit(c0, nch, tab, kind, dst):
                    wi, wbuf = loadw(c0, 2 * nch)
                    for bi, (t0, tn, mi) in enumerate(BLOCKS):
                        if mi == 1 and not need_ctx and kind in ("retq", "attq"):
                            continue
                        ri, rbuf = rrr.next()
                        k.dma(k.sp, rr[:, ri, :, :tn], rope[:, tab:tab + 2, t0:t0 + tn], [], [rbuf], k.dsem(f"rope{ri}"))
                        cosT = (rr[:, ri, 0, :tn], [rbuf])
                        sinT = (rr[:, ri, 1, :tn], [rbuf])
                        for ci in range(nch):
                            P = proj(wi, wbuf, ci, bi)
                            Ps = proj(wi, wbuf, nch + ci, bi)
                            t1, t2 = T_(tn), T_(tn)
                            if kind in ("retq", "retk"):
                                if kind == "retq":
                                    k.stt(t1, P, 0.125, cosT, ALU.mult, ALU.mult)
                                    k.stt(t2, Ps, 0.125, sinT, ALU.mult, ALU.mult)
                                else:
                                    k.tt(t1, P, cosT, ALU.mult)
                                    k.tt(t2, Ps, sinT, ALU.mult)
                                if kind == "retq":
                                    k.tt(self.Q(1, ci, bi), t1, t2, ALU.add)
                                else:
                                    si, sbuf = sr.next()
                                    o = (stg[:, si, :tn], [sbuf])
                                    k.tt(o, t1, t2, ALU.add)
                                    store(o, dst + ci, bi)
                            else:
                                gname = "aq" if kind == "attq" else "ak"
                                qi, qbuf = sqr.next()
                                sq = (sqt[:, qi, :tn], [qbuf])
                                k.actf(sq, P, AF.Square)
                                _, sb_ = k.bank()
                                k.mm((sb_[0][:, :tn], sb_[1]), self.CB_("bones"), sq)
                                ri2, rb2 = rsr.next()
                                r = (rst[:, ri2, :tn], [rb2])
                                k.actf(r, (sb_[0][:, :tn], sb_[1]), AF.Sqrt, scale=1.0 / 64, bias=self.epsap)
                                k.recip(r, r)
                                k.stt(t1, P, self.C_(f"{gname}n{l}"), cosT, ALU.mult, ALU.mult)
                                k.stt(t2, Ps, self.C_(f"{gname}s{l}"), sinT, ALU.mult, ALU.mult)
                                k.tt(t1, t1, t2, ALU.add)
                                if kind == "attq":
                                    k.tt(self.Q(0, ci, bi), t1, r, ALU.mult)
                                else:
                                    si, sbuf = sr.next()
                                    o = (stg[:, si, :tn], [sbuf])
                                    k.tt(o, t1, r, ALU.mult)
                                    store(o, dst + ci, bi)

                def plain_unit(c0, nch, kind, dst):
                    wi, wbuf = loadw(c0, nch)
                    for bi, (t0, tn, mi) in enumerate(BLOCKS):
                        if mi == 1 and not need_ctx and kind in ("retg", "fnet"):
                            continue
                        for ci in range(nch):
                            P = proj(wi, wbuf, ci, bi)
                            if kind == "retg":
                                k.actf(self.Q(2, ci, bi), P, AF.Silu)
                            else:
                                si, sbuf = sr.next()
                                o = (stg[:, si, :tn], [sbuf])
                                k.actf(o, P, AF.Identity)
                                store(o, dst + ci, bi)

                rope_unit(0, 2, 2, "retq", None)
                rope_unit(4, 2, 2, "retk", 2)
                plain_unit(8, 2, "retv", 4)
                plain_unit(10, 2, "retg", None)
                plain_unit(12, 2, "fnet", 6)
                rope_unit(14, 2, 0, "attq", None)
                rope_unit(18, 1, 0, "attk", 0)
                plain_unit(20, 1, "attv", 1)
                k.barrier()

    def kv_src(self, row, c0, n):
        out = []
        segs = [(0, C, self.kvc[row * 128:(row + 1) * 128, :]),
                (C, SL, self.kva[row // 4][(row % 4) * 128:(row % 4 + 1) * 128, :]),
                (C + SL, SL, self.kva[row // 4][(4 + row % 4) * 128:(5 + row % 4) * 128, :])]
        for (s0, sn, t) in segs:
            a = max(c0, s0)
            b = min(c0 + n, s0 + sn)
            if a < b:
                out.append((t[:, a - s0:b - s0], a - c0, b - a))
        return out

    def attn_phase(self, l, need_ctx):
        k = self.k
        k.barrier()
        wout = self.inp(f"wout{l}", [D, D])
        with contextlib.ExitStack() as es:
            Kt = self.sb(es, "aK", [128, NKC * 128], BF16)
            Kb = Buf()
            Va = self.sb(es, "aV", [128, NKC, 2, 128], BF16)
            Vb = Buf()
            vst = self.sb(es, "avst", [128, 2, 512], BF16)
            vsr = Ring(2)
            E = self.sb(es, "aE", [128, 4, 512], BF16)
            Er = Ring(4)
            accs = self.sb(es, "aacc", [128, 2, 512], F32)
            acr = Ring(2)
            rden = self.sb(es, "arden", [64, 2, 512], F32)
            rdr = Ring(2)
            aout = self.sb(es, "aout", [64, 4, 512], BF16)
            aob = Buf()
            wo = self.sb(es, "awo", [64, 4, D], BF16)
            wob = Buf()
            for (src, off, n) in self.kv_src(0, 0, NKC * 128):
                k.dma(k.sp, Kt[:, off:off + n], src, self.kvrd, [Kb], k.dsem("kload"))
            k.dma(k.pool, wo[:], wout[512:768, :].rearrange("(h p) f -> p h f", p=64), [], [wob], k.dsem("wo"))
            dbg = int(os.environ.get("KDBG", "99"))
            if dbg <= 0:
                k.barrier()
                return
            k.memset((Va[:, :, :, 64:128], [Vb]), 1.0)
            if dbg <= 1:
                k.barrier()
                return
            for pc in range(0, NKC * 128, 512):
                n = min(512, NKC * 128 - pc)
                vi, vb = vsr.next()
                for (src, off, nn) in self.kv_src(1, pc, n):
                    k.dma(k.sp, vst[:, vi, off:off + nn], src, self.kvrd, [vb], k.dsem(f"vst{vi}"))
                for tl in range(n // 128):
                    kc = pc // 128 + tl
                    if dbg == 12:
                        continue
                    pt = k.bbank()
                    k.transpose((pt[0][:, 0:128], pt[1]), (vst[:, vi, tl * 128:(tl + 1) * 128], [vb]), self.CB_("ident"))
                    if dbg == 13:
                        continue
                    if dbg == 14:
                        k.copy((accs[:, 0, 0:64], [acr.bufs[0]]), (pt[0][:, 0:64], pt[1]))
                        continue
                    if dbg == 15:
                        k.copy((Va[:, kc, 0, 0:64], [Vb]), (accs[:, 0, 0:64], [acr.bufs[0]]))
                        continue
                    k.copy((Va[:, kc, 0, 0:64], [Vb]), (pt[0][:, 0:64], pt[1]))
                    k.copy((Va[:, kc, 1, 0:64], [Vb]), (pt[0][:, 64:128], pt[1]))
            if dbg <= 2 or dbg in (12, 13, 14, 15):
                k.barrier()
                return
            for bi, (t0, tn, mi) in enumerate(BLOCKS):
                if mi == 1 and not need_ctx:
                    continue
                if dbg <= 6 and bi > 0:
                    continue
                kcs = list(range(NKC)) if mi == 0 else [0, 1]
                if dbg <= 3:
                    kcs = kcs[:3]
                for h in range(4):
                    kvh = h // 2
                    pr = slice(kvh * 64, kvh * 64 + 64)
                    q = self.Q(0, h % 2, bi, parts=pr)
                    ai, acc = k.bank(hold=True)
                    pend = []

                    def score(kc):
                        _, sb_ = k.bank()
                        k.mm((sb_[0][:, :tn], sb_[1]), (Kt[pr, kc * 128:(kc + 1) * 128], [Kb]), q)
                        return sb_

                    def finish(kc, sb_, first, last):
                        ei, eb = Er.next()
                        e = (E[:, ei, :tn], [eb])
                        k.actf(e, (sb_[0][:, :tn], sb_[1]), AF.Exp, scale=0.125)
                        k.mm((acc[0][:, :tn], acc[1]), (Va[:, kc, kvh, :], [Vb]), e, start=first, stop=last)

                    for idx, kc in enumerate(kcs):
                        pend.append((kc, score(kc)))
                        if len(pend) > 3:
                            kc0, s0 = pend.pop(0)
                            finish(kc0, s0, kc0 == kcs[0], False)
                    while pend:
                        kc0, s0 = pend.pop(0)
                        finish(kc0, s0, kc0 == kcs[0], len(pend) == 0)
                    ci, cbf = acr.next()
                    a_ = (accs[:, ci, :tn], [cbf])
                    k.actf(a_, (acc[0][:, :tn], acc[1]), AF.Identity)
                    k.release(ai)
                    if dbg <= 4:
                        continue
                    _, dn = k.bank()
                    k.mm((dn[0][0:64, :tn], dn[1]), self.CF_("shift"), a_)
                    di, dbf = rdr.next()
                    rd = (rden[:, di, :tn], [dbf])
                    k.recip(rd, (dn[0][0:64, :tn], dn[1]))
                    k.tt((aout[:, h, :tn], [aob]), (accs[0:64, ci, :tn], [cbf]), rd, ALU.mult)
                if dbg <= 5:
                    continue
                for j in range(DC):
                    _, pd = k.bank()
                    for h in range(4):
                        k.mm((pd[0][:, :tn], pd[1]), (wo[:, h, j * 128:(j + 1) * 128], [wob]), (aout[:, h, :tn], [aob]),
                             start=(h == 0), stop=(h == 3), inc=(h == 3))
                    k.stt(self.H(j, bi), (pd[0][:, :tn], pd[1]), self.DER(1, 1, j, mi), self.H(j, bi), ALU.mult, ALU.add)
            k.barrier()

    def ret_phase(self, l, need_ctx):
        k = self.k
        k.barrier()
        wout = self.inp(f"wout{l}", [D, D])
        with contextlib.ExitStack() as es:
            Kr = self.sb(es, "rK", [128, NKC * 128], BF16)
            Kb = Buf()
            Vr = self.sb(es, "rV", [128, NKC, 128], BF16)
            Vb = Buf()
            vst = self.sb(es, "rvst", [128, 2, 256], BF16)
            vsr = Ring(2)
            PM = self.sb(es, "rPM", [128, 2, 2, 4, 512], BF16)
            PMb = Buf()
            PMc = self.sb(es, "rPMc", [128, 2, 2, 256], BF16) if need_ctx else None
            EE = self.sb(es, "rEE", [128, 2, 2, 2, 512], BF16)
            EEb = Buf()
            mg = self.sb(es, "rmg", [128, 2, 512], BF16)
            mgr = Ring(2)
            mg2 = self.sb(es, "rmg2", [128, 2, 512], BF16)
            mg2r = Ring(2)
            Pm = self.sb(es, "rPm", [128, 4, 512], BF16)
            Pmr = Ring(4)
            st = self.sb(es, "rst", [128, 5, 512], F32)
            stb = [Buf() for _ in range(5)]
            rout = self.sb(es, "rout", [128, 2, 512], BF16)
            ror = Ring(2)
            wo = self.sb(es, "rwo", [128, 2, D], BF16)
            wob = Buf()
            ct = self.sb(es, "rct", [128, 4, 132], F32)
            ctb = Buf()
            sx = self.sb(es, "rsx", [128, 4, 72], F32)
            k.dma(k.pool, wo[:], wout[0:256, :].rearrange("(c p) f -> p c f", p=128), [], [wob], k.dsem("wo"))
            T = self.CF_("iota")
            for h in range(4):
                lgf = self.C_(f"lgf{l}", h, 1)
                lgb = self.C_(f"lgb{l}", h, 1)
                c1 = (ct[:, h, 0:56], [ctb])
                k.ts(c1, self.C_("sgf"), lgf, ALU.mult)
                k.stt(c1, self.C_("sgb"), lgb, c1, ALU.mult, ALU.add)
                k.tt((ct[:, h, 56:112], [ctb]), c1, self.C_("dlt"), ALU.mult)
                k.ts((ct[:, h, 112:120], [ctb]), self.C_("d1"), lgf, ALU.mult)
                k.ts((ct[:, h, 120:128], [ctb]), self.C_("d2"), lgb, ALU.mult)
                k.ts((ct[:, h, 128:129], [ctb]), lgb, -1.0, ALU.mult)
                k.actf((sx[:, h, :], [ctb]), (ct[:, h, 56:128], [ctb]), AF.Exp)

            def CT(h, a):
                return (ct[:, h, a:a + 1], [ctb])

            def SX(h, a):
                return (sx[:, h, a:a + 1], [ctb])

            for hp in range(2):
                for (src, off, n) in self.kv_src(2 + hp, 0, NKC * 128):
                    k.dma(k.sp, Kr[:, off:off + n], src, self.kvrd, [Kb], k.dsem("kload"))
                for pc in range(0, NKC * 128, 256):
                    vi, vb = vsr.next()
                    for (src, off, nn) in self.kv_src(4 + hp, pc, 256):
                        k.dma(k.sp, vst[:, vi, off:off + nn], src, self.kvrd, [vb], k.dsem(f"vst{vi}"))
                    for tl in range(2):
                        kc = pc // 128 + tl
                        pt = k.bbank()
                        k.transpose((pt[0][:, 0:128], pt[1]), (vst[:, vi, tl * 128:(tl + 1) * 128], [vb]), self.CB_("ident"))
                        k.copy((Vr[:, kc, 0:64], [Vb]), (pt[0][:, 0:64], pt[1]))
                        k.copy((Vr[:, kc, 64:128], [Vb]), (pt[0][:, 64:128], pt[1]))
                r1 = (st[:, 0, :], [stb[0]])
                r2 = (st[:, 1, :], [stb[1]])
                e1 = (st[:, 2, :], [stb[2]])
                dg = (st[:, 3, :], [stb[3]])
                for r in range(2):
                    for m in range(4):
                        dcol = self.C_("pmd", r * 4 + m, 1)
                        ncol = self.C_("pmdn", r * 4 + m, 1)
                        k.actf(r1, T, AF.Relu, scale=1.0, bias=dcol)
                        k.actf(r2, T, AF.Relu, scale=-1.0, bias=ncol)
                        k.ts(dg, T, ncol, ALU.is_equal)
                        for hh in range(2):
                            h = 2 * hp + hh
                            k.ts(e1, r1, self.C_(f"lgf{l}", h, 1), ALU.mult)
                            k.stt(e1, r2, self.C_(f"lgb{l}", h, 1), e1, ALU.mult, ALU.add)
                            k.actf(e1, e1, AF.Exp)
                            k.tt((PM[:, hh, r, m, :], [PMb]), e1, dg, ALU.add)
                    for hh in range(2):
                        h = 2 * hp + hh
                        for cls in range(2):
                            k.actf((EE[:, hh, r, cls, :], [EEb]), T, AF.Exp, scale=CT(h, r * 28 + (16 if cls == 0 else 11)))
                if need_ctx:
                    for m in range(2):
                        dl = -128.0 * m
                        r1c = (st[:, 0, 0:C], [stb[0]])
                        r2c = (st[:, 1, 0:C], [stb[1]])
                        e1c = (st[:, 2, 0:C], [stb[2]])
                        dgc = (st[:, 3, 0:C], [stb[3]])
                        Tc = (T[0][:, 0:C], T[1])
                        k.ts(r1c, Tc, dl, ALU.add, 0.0, ALU.max)
                        k.ts(r2c, Tc, -1.0, ALU.mult, -dl, ALU.add)
                        k.ts(r2c, r2c, 0.0, ALU.max)
                        k.ts(dgc, Tc, -dl, ALU.is_equal)
                        for hh in range(2):
                            h = 2 * hp + hh
                            k.ts(e1c, r1c, self.C_(f"lgf{l}", h, 1), ALU.mult)
                            k.stt(e1c, r2c, self.C_(f"lgb{l}", h, 1), e1c, ALU.mult, ALU.add)
                            k.actf(e1c, e1c, AF.Exp)
                            k.tt((PMc[:, hh, m, :], [PMb]), e1c, dgc, ALU.add)

                for bi, (t0, tn, mi) in enumerate(BLOCKS):
                    if mi == 1 and not need_ctx:
                        continue
                    kcs = list(range(NKC)) if mi == 0 else [0, 1]
                    qb_ = bi
                    q0 = t0
                    ai, acc = k.bank(hold=True)
                    for hh in range(2):
                        h = 2 * hp + hh
                        pr = slice(hh * 64, hh * 64 + 64)
                        q = self.Q(1, hp, bi, parts=pr)
                        pend = []

                        def score(kc):
                            _, sb_ = k.bank()
                            k.mm((sb_[0][:, :tn], sb_[1]), (Kr[pr, kc * 128:(kc + 1) * 128], [Kb]), q)
                            return sb_

                        def finish(kc, sb_, first, last):
                            pi, pb = Pmr.next()
                            p_ = (Pm[:, pi, :tn], [pb])
                            S_ = (sb_[0][:, :tn], sb_[1])
                            if mi == 1:
                                k.tt(p_, S_, (PMc[:, hh, kc, :tn], [PMb]), ALU.mult)
                            elif kc < 2:
                                i1, b1 = mgr.next()
                                m1 = (mg[:, i1, :tn], [b1])
                                i2, b2 = mg2r.next()
                                m2 = (mg2[:, i2, :tn], [b2])
                                k.actf(m1, (T[0][:, :tn], T[1]), AF.Exp, scale=self.C_(f"lgf{l}", h, 1), bias=CT(h, 112 + qb_ * 2 + kc))
                                k.actf(m2, (T[0][:, :tn], T[1]), AF.Exp, scale=CT(h, 128), bias=CT(h, 120 + qb_ * 2 + kc))
                                k.tt(m1, m1, m2, ALU.add)
                                k.tt(p_, S_, m1, ALU.mult)
                            else:
                                r = (kc - 2) // 16
                                dl = q0 - ((kc - 2) % 16) * 128
                                if -384 <= dl <= 0:
                                    k.tt(p_, S_, (PM[:, hh, r, (-dl) // 128, :tn], [PMb]), ALU.mult)
                                else:
                                    didx = dl // 128 + 15
                                    cls = 0 if dl >= 128 else 1
                                    k.stt(p_, S_, SX(h, r * 28 + didx), (EE[:, hh, r, cls, :tn], [EEb]), ALU.mult, ALU.mult)
                            k.mm((acc[0][pr, :tn], acc[1]), (Vr[:, kc, hh * 64:(hh + 1) * 64], [Vb]), p_, start=first, stop=last)

                        for kc in kcs:
                            pend.append((kc, score(kc)))
                            if len(pend) > 3:
                                kc0, s0 = pend.pop(0)
                                finish(kc0, s0, kc0 == kcs[0], False)
                        while pend:
                            kc0, s0 = pend.pop(0)
                            finish(kc0, s0, kc0 == kcs[0], len(pend) == 0)
                    o = (st[:, 0, :tn], [stb[0]])
                    sq = (st[:, 1, :tn], [stb[1]])
                    mean = (st[:, 2, :tn], [stb[2]])
                    msq = (st[:, 3, :tn], [stb[3]])
                    var = (st[:, 4, :tn], [stb[4]])
                    dd = (st[:, 1, :tn], [stb[1]])
                    k.actf(o, (acc[0][:, :tn], acc[1]), AF.Identity)
                    k.actf(sq, (acc[0][:, :tn], acc[1]), AF.Square)
                    k.release(ai)
                    _, b1 = k.bank()
                    k.mm((b1[0][:, :tn], b1[1]), self.CF_("bones"), o)
                    _, b2 = k.bank()
                    k.mm((b2[0][:, :tn], b2[1]), self.CF_("bones"), sq)
                    k.actf(mean, (b1[0][:, :tn], b1[1]), AF.Identity, scale=1.0 / 64)
                    k.tt(msq, mean, mean, ALU.mult)
                    k.stt(var, (b2[0][:, :tn], b2[1]), 1.0 / 64, msq, ALU.mult, ALU.subtract)
                    k.actf(var, var, AF.Sqrt, bias=self.epsap)
                    k.recip(var, var)
                    k.tt(dd, o, mean, ALU.subtract)
                    k.stt(dd, dd, self.C_(f"retn{l}", hp, 1), var, ALU.mult, ALU.mult)
                    ri_, rb_ = ror.next()
                    ro = (rout[:, ri_, :tn], [rb_])
                    k.tt(ro, dd, self.Q(2, hp, bi), ALU.mult)
                    for j in range(DC):
                        _, pd = k.bank()
                        k.mm((pd[0][:, :tn], pd[1]), (wo[:, hp, j * 128:(j + 1) * 128], [wob]), ro)
                        k.stt(self.H(j, bi), (pd[0][:, :tn], pd[1]), self.DER(1, 1, j, mi), self.H(j, bi), ALU.mult, ALU.add)
            k.barrier()

    def mix_phase(self, l, need_ctx):
        k = self.k
        k.barrier()
        wout = self.inp(f"wout{l}", [D, D])
        with contextlib.ExitStack() as es:
            Kt = self.sb(es, "aK", [128, NKC * 128], BF16)
            Ktb = Buf()
            Va = self.sb(es, "aV", [128, NKC, 128], BF16)
            Vab = Buf()
            E = self.sb(es, "aE", [128, 3, 512], BF16)
            Er = Ring(3)
            accs = self.sb(es, "aacc", [128, 512], F32)
            acb = Buf()
            aout = self.sb(es, "aout", [64, 2, 512], BF16)
            aob = Buf()
            woa = self.sb(es, "awo", [64, 2, D], BF16)
            woab = Buf()
            Kr = self.sb(es, "rK", [128, NKC * 128], BF16)
            Krb = Buf()
            Vr = self.sb(es, "rV", [128, NKC, 128], BF16)
            Vrb = Buf()
            vst = self.sb(es, "rvst", [128, 2, 256], BF16)
            vsr = Ring(2)
            PM = self.sb(es, "rPM", [128, 2, 2, 4, 512], BF16)
            PMb = Buf()
            PMc = self.sb(es, "rPMc", [128, 2, 2, 256], BF16) if need_ctx else None
            AP_ = self.sb(es, "rAp", [128, 6, 512], BF16)
            APb = Buf()
            QA = self.sb(es, "rQA", [128, 2, 512], BF16)
            QAr = Ring(2)
            Kbt = self.sb(es, "rKb", [128, 2, 128], BF16)
            Kbr = Ring(2)
            Wsum = self.sb(es, "rWs", [128, 4, 6, 64], F32)
            Wsb = Buf()
            Wsq = [[Buf() for _ in range(6)] for _ in range(4)]
            Wb = self.sb(es, "rWb", [128, 1, 6, 64], BF16)
            Wbr = Ring(1)
            c1p = self.sb(es, "rc1p", [128, 6, 4], F32)
            c1b = Buf()
            bcol = self.sb(es, "rbcol", [128, 6, 2], F32)
            sxp = self.sb(es, "rsxp", [128, 72], F32)
            sxb = Buf()
            Pm = self.sb(es, "rPm", [128, 3, 512], BF16)
            Pmr = Ring(3)
            st = self.sb(es, "rst", [128, 5, 512], F32)
            stb = [Buf() for _ in range(5)]
            rout = self.sb(es, "rout", [128, 512], BF16)
            rob = Buf()
            wor = self.sb(es, "rwo", [128, D], BF16)
            worb = Buf()
            ct = self.sb(es, "rct", [128, 4, 132], F32)
            ctb = Buf()
            sx = self.sb(es, "rsx", [128, 4, 72], F32)
            for (src, off, n) in self.kv_src(0, 0, NKC * 128):
                k.dma(k.sp, Kt[:, off:off + n], src, self.kvrd, [Ktb], k.dsem("kload"))
            k.memset((Va[:, :, 64:128], [Vab]), 1.0)
            T = self.CF_("iota")
            for h in range(4):
                lgf = self.C_(f"lgf{l}", h, 1)
                lgb = self.C_(f"lgb{l}", h, 1)
                c1 = (ct[:, h, 0:56], [ctb])
                k.ts(c1, self.C_("sgf"), lgf, ALU.mult)
                k.stt(c1, self.C_("sgb"), lgb, c1, ALU.mult, ALU.add)
                k.tt((ct[:, h, 56:112], [ctb]), c1, self.C_("dlt"), ALU.mult)
                k.ts((ct[:, h, 112:120], [ctb]), self.C_("d1"), lgf, ALU.mult)
                k.ts((ct[:, h, 120:128], [ctb]), self.C_("d2"), lgb, ALU.mult)
                k.ts((ct[:, h, 128:129], [ctb]), lgb, -1.0, ALU.mult)
                k.actf((sx[:, h, :], [ctb]), (ct[:, h, 56:128], [ctb]), AF.Exp)

            def CT(h, a):
                return (ct[:, h, a:a + 1], [ctb])

            def SX(h, a):
                return (sx[:, h, a:a + 1], [ctb])

            for p in range(2):
                k.dma(k.pool, woa[:], wout[512 + p * 128:512 + (p + 1) * 128, :].rearrange("(h p) f -> p h f", p=64), [], [woab], k.dsem("wo"))
                k.dma(k.pool, wor[:], wout[p * 128:(p + 1) * 128, :], [], [worb], k.dsem("wo"))
                for (src, off, n) in self.kv_src(2 + p, 0, NKC * 128):
                    k.dma(k.sp, Kr[:, off:off + n], src, self.kvrd, [Krb], k.dsem("kload"))
                for which in range(2):
                    for pc in range(0, NKC * 128, 256):
                        vi, vb = vsr.next()
                        for (src, off, nn) in self.kv_src(1 if which == 0 else 4 + p, pc, 256):
                            k.dma(k.sp, vst[:, vi, off:off + nn], src, self.kvrd, [vb], k.dsem(f"vst{vi}"))
                        for tl in range(2):
                            kc = pc // 128 + tl
                            pt = k.bbank()
                            k.transpose((pt[0][:, 0:128], pt[1]), (vst[:, vi, tl * 128:(tl + 1) * 128], [vb]), self.CB_("ident"))
                            if which == 0:
                                k.copy((Va[:, kc, 0:64], [Vab]), (pt[0][:, p * 64:(p + 1) * 64], pt[1]))
                            else:
                                k.copy((Vr[:, kc, 0:64], [Vrb]), (pt[0][:, 0:64], pt[1]))
                                k.copy((Vr[:, kc, 64:128], [Vrb]), (pt[0][:, 64:128], pt[1]))
                r1 = (st[:, 0, :], [stb[0]])
                r2 = (st[:, 1, :], [stb[1]])
                e1 = (st[:, 2, :], [stb[2]])
                dg = (st[:, 3, :], [stb[3]])
                for r in range(2):
                    for m in range(4):
                        dcol = self.C_("pmd", r * 4 + m, 1)
                        ncol = self.C_("pmdn", r * 4 + m, 1)
                        k.actf(r1, T, AF.Relu, scale=1.0, bias=dcol)
                        k.actf(r2, T, AF.Relu, scale=-1.0, bias=ncol)
                        k.ts(dg, T, ncol, ALU.is_equal)
                        for hh in range(2):
                            h = 2 * p + hh
                            k.ts(e1, r1, self.C_(f"lgf{l}", h, 1), ALU.mult)
                            k.stt(e1, r2, self.C_(f"lgb{l}", h, 1), e1, ALU.mult, ALU.add)
                            k.actf(e1, e1, AF.Exp)
                            k.tt((PM[:, hh, r, m, :], [PMb]), e1, dg, ALU.add)
                if need_ctx:
                    for m in range(2):
                        dl = -128.0 * m
                        r1c = (st[:, 0, 0:C], [stb[0]])
                        r2c = (st[:, 1, 0:C], [stb[1]])
                        e1c = (st[:, 2, 0:C], [stb[2]])
                        dgc = (st[:, 3, 0:C], [stb[3]])
                        Tc = (T[0][:, 0:C], T[1])
                        k.ts(r1c, Tc, dl, ALU.add, 0.0, ALU.max)
                        k.ts(r2c, Tc, -1.0, ALU.mult, -dl, ALU.add)
                        k.ts(r2c, r2c, 0.0, ALU.max)
                        k.ts(dgc, Tc, -dl, ALU.is_equal)
                        for hh in range(2):
                            h = 2 * p + hh
                            k.ts(e1c, r1c, self.C_(f"lgf{l}", h, 1), ALU.mult)
                            k.stt(e1c, r2c, self.C_(f"lgb{l}", h, 1), e1c, ALU.mult, ALU.add)
                            k.actf(e1c, e1c, AF.Exp)
                            k.tt((PMc[:, hh, m, :], [PMb]), e1c, dgc, ALU.add)

                T0 = (self.cf[:, CF["iota"][0]:CF["iota"][0] + 1], [self.cbuf])
                for hh in range(2):
                    h = 2 * p + hh
                    prs = slice(hh * 64, hh * 64 + 64)
                    k.copy((sxp[prs, :], [sxb]), (sx[prs, h, :], [ctb]))
                    for t in range(6):
                        if t < 4:
                            ci_ = (t // 2) * 28 + (16 if t % 2 == 0 else 11)
                            src = (ct[prs, h, ci_:ci_ + 1], [ctb])
                            srcf = (ct[:, h, ci_:ci_ + 1], [ctb])
                        elif t == 4:
                            o_ = CST[f"lgf{l}"][0] + h
                            src = (self.cst[prs, o_:o_ + 1], [self.cbuf])
                            srcf = (self.cst[:, o_:o_ + 1], [self.cbuf])
                        else:
                            src = (ct[prs, h, 128:129], [ctb])
                            srcf = (ct[:, h, 128:129], [ctb])
                        k.copy((c1p[prs, t, 0:1], [c1b]), src)
                        k.actf((bcol[:, t, hh:hh + 1], [c1b]), T0, AF.Exp, scale=srcf)
                for t in range(6):
                    k.ts((c1p[:, t, 1:2], [c1b]), (c1p[:, t, 0:1], [c1b]), -1.0, ALU.mult)
                    k.tt((c1p[:, t, 2:3], [c1b]), T0, (c1p[:, t, 1:2], [c1b]), ALU.mult)
                    k.actf((AP_[:, t, :], [APb]), T, AF.Exp, scale=(c1p[:, t, 0:1], [c1b]), bias=(c1p[:, t, 2:3], [c1b]))
                k.memset((Wsum[:], [Wsb] + [b_ for r_ in Wsq for b_ in r_]), 0.0)
                for kc in range(NKC):
                    pt = k.bbank()
                    k.transpose((pt[0][:, 0:128], pt[1]), (Kr[:, kc * 128:(kc + 1) * 128], [Krb]), self.CB_("ident"))
                    terms = [4, 5] if kc < 2 else [((kc - 2) // 16) * 2, ((kc - 2) // 16) * 2 + 1]
                    for t in terms:
                        uses = []
                        for qb in range(4):
                            if kc < 2:
                                uses.append((qb, (56 if t == 4 else 64) + qb * 2 + kc))
                            else:
                                r = (kc - 2) // 16
                                dl = qb * 512 - ((kc - 2) % 16) * 128
                                if -384 <= dl <= 0:
                                    continue
                                if (0 if dl >= 128 else 1) == t % 2:
                                    uses.append((qb, r * 28 + dl // 128 + 15))
                        if not uses:
                            continue
                        ki, kb_ = Kbr.next()
                        for hh in range(2):
                            k.ts((Kbt[:, ki, hh * 64:(hh + 1) * 64], [kb_]), (pt[0][:, hh * 64:(hh + 1) * 64], pt[1]),
                                 (bcol[:, t, hh:hh + 1], [c1b]), ALU.mult)
                        _, wb_ = k.bank()
                        for hh in range(2):
                            k.mm((wb_[0][hh * 64:(hh + 1) * 64, 0:64], wb_[1]), (Kbt[:, ki, hh * 64:(hh + 1) * 64], [kb_]),
                                 (Vr[:, kc, hh * 64:(hh + 1) * 64], [Vrb]), inc=(hh == 1))
                        for (qb, sidx) in uses:
                            k.stt((Wsum[:, qb, t, :], [Wsq[qb][t]]), (wb_[0][:, 0:64], wb_[1]), (sxp[:, sidx:sidx + 1], [sxb]),
                                  (Wsum[:, qb, t, :], [Wsq[qb][t]]), ALU.mult, ALU.add)

                def att_gen(bi):
                    t0, tn, mi = BLOCKS[bi]
                    kcs = list(range(NKC)) if mi == 0 else [0, 1]
                    pr = slice(p * 64, p * 64 + 64)
                    for j in range(2):
                        q = self.Q(0, j, bi, parts=pr)
                        ai, acc = k.bank(hold=True)
                        pend = []

                        def score(kc):
                            bi_, sb_ = k.bank(hold=True)
                            k.mm((sb_[0][:, :tn], sb_[1]), (Kt[pr, kc * 128:(kc + 1) * 128], [Ktb]), q)
                            return (bi_, sb_)

                        def finish(kc, sbh, first, last):
                            bi_, sb_ = sbh
                            ei, eb = Er.next()
                            e = (E[:, ei, :tn], [eb])
                            k.actf(e, (sb_[0][:, :tn], sb_[1]), AF.Exp, scale=0.125)
                            k.release(bi_)
                            k.mm((acc[0][:, :tn], acc[1]), (Va[:, kc, :], [Vab]), e, start=first, stop=last)

                        groups = [kcs[i_:i_ + 2] for i_ in range(0, len(kcs), 2)]
                        pendg = []
                        seen = [0]

                        def flush(g0, sc0, last_group):
                            es = []
                            for kc0, sbh in zip(g0, sc0):
                                bi_, sb_ = sbh
                                ei, eb = Er.next()
                                e = (E[:, ei, :tn], [eb])
                                k.actf(e, (sb_[0][:, :tn], sb_[1]), AF.Exp, scale=0.125)
                                k.release(bi_)
                                es.append(e)
                            order = list(range(len(g0)))[::-1]
                            for n_, idx in enumerate(order):
                                first = (seen[0] == 0)
                                seen[0] += 1
                                last = last_group and (n_ == len(order) - 1)
                                k.mm((acc[0][:, :tn], acc[1]), (Va[:, g0[idx], :], [Vab]), es[idx], start=first, stop=last)

                        for gi_, g in enumerate(groups):
                            pendg.append((g, [score(kc) for kc in g]))
                            if len(pendg) > 1:
                                g0, sc0 = pendg.pop(0)
                                flush(g0, sc0, False)
                                yield
                        while pendg:
                            g0, sc0 = pendg.pop(0)
                            flush(g0, sc0, len(pendg) == 0)
                            yield
                        a_ = (accs[:, :tn], [acb])
                        k.actf(a_, (acc[0][:, :tn], acc[1]), AF.Identity)
                        k.release(ai)
                        _, dn = k.bank()
                        k.mm((dn[0][0:64, :tn], dn[1]), self.CF_("shift"), a_)
                        rd = (st[0:64, 4, :tn], [stb[4]])
                        k.recip(rd, (dn[0][0:64, :tn], dn[1]))
                        k.tt((aout[:, j, :tn], [aob]), (accs[0:64, :tn], [acb]), rd, ALU.mult)
                        yield
                    for jj in range(DC):
                        _, pd = k.bank()
                        for j in range(2):
                            k.mm((pd[0][:, :tn], pd[1]), (woa[:, j, jj * 128:(jj + 1) * 128], [woab]), (aout[:, j, :tn], [aob]),
                                 start=(j == 0), stop=(j == 1), inc=(j == 1))
                        k.stt(self.H(jj, bi), (pd[0][:, :tn], pd[1]), self.DER(1, 1, jj, mi), self.H(jj, bi), ALU.mult, ALU.add)
                        yield

                def ret_gen(bi):
                    t0, tn, mi = BLOCKS[bi]
                    q0 = t0
                    ai, acc = k.bank(hold=True)
                    started = [False, False]
                    work = []
                    if mi == 1:
                        for hh in range(2):
                            for kc in (0, 1):
                                work.append((hh, kc))
                    else:
                        wi_, wbb = Wbr.next()
                        k.copy((Wb[:, wi_].rearrange("p a b -> p (a b)"), [wbb]), (Wsum[:, bi].rearrange("p a b -> p (a b)"), [Wsb] + Wsq[bi]))
                        for t in range(6):
                            qi_, qab = QAr.next()
                            qa = (QA[:, qi_, :tn], [qab])
                            k.tt(qa, self.Q(1, p, bi), (AP_[:, t, :tn], [APb]), ALU.mult)
                            for hh in range(2):
                                pr = slice(hh * 64, hh * 64 + 64)
                                k.mm((acc[0][pr, :tn], acc[1]), (Wb[pr, wi_, t, :], [wbb]), (QA[pr, qi_, :tn], [qab]),
                                     start=(not started[hh]), stop=False)
                                started[hh] = True
                            yield
                        for hh in range(2):
                            for kc in range(2, NKC):
                                dl = q0 - ((kc - 2) % 16) * 128
                                if -384 <= dl <= 0:
                                    work.append((hh, kc))
                    last_of = {}
                    for (hh, kc) in work:
                        last_of[hh] = kc
                    def rscore(hh, kc):
                        pr = slice(hh * 64, hh * 64 + 64)
                        sbi_, sb_ = k.bank(hold=True)
                        k.mm((sb_[0][:, :tn], sb_[1]), (Kr[pr, kc * 128:(kc + 1) * 128], [Krb]), self.Q(1, p, bi, parts=pr))
                        return (sbi_, sb_)

                    def rfinish(hh, kc, sbh):
                        sbi_, sb_ = sbh
                        pr = slice(hh * 64, hh * 64 + 64)
                        pi, pb = Pmr.next()
                        p_ = (Pm[:, pi, :tn], [pb])
                        S_ = (sb_[0][:, :tn], sb_[1])
                        if mi == 1:
                            k.tt(p_, S_, (PMc[:, hh, kc, :tn], [PMb]), ALU.mult)
                        else:
                            r = (kc - 2) // 16
                            k.tt(p_, S_, (PM[:, hh, r, (-(q0 - ((kc - 2) % 16) * 128)) // 128, :tn], [PMb]), ALU.mult)
                        k.release(sbi_)
                        k.mm((acc[0][pr, :tn], acc[1]), (Vr[:, kc, hh * 64:(hh + 1) * 64], [Vrb]), p_,
                             start=(not started[hh]), stop=(last_of[hh] == kc))
                        started[hh] = True

                    rp = []
                    for (hh, kc) in work:
                        rp.append((hh, kc, rscore(hh, kc)))
                        if len(rp) > 2:
                            a_, b_, c_ = rp.pop(0)
                            rfinish(a_, b_, c_)
                            yield
                    while rp:
                        a_, b_, c_ = rp.pop(0)
                        rfinish(a_, b_, c_)
                        yield
                    o = (st[:, 0, :tn], [stb[0]])
                    sq = (st[:, 1, :tn], [stb[1]])
                    mean = (st[:, 2, :tn], [stb[2]])
                    msq = (st[:, 3, :tn], [stb[3]])
                    var = (st[:, 4, :tn], [stb[4]])
                    dd = (st[:, 1, :tn], [stb[1]])
                    k.actf(o, (acc[0][:, :tn], acc[1]), AF.Identity)
                    k.actf(sq, (acc[0][:, :tn], acc[1]), AF.Square)
                    k.release(ai)
                    _, b1 = k.bank()
                    k.mm((b1[0][:, :tn], b1[1]), self.CF_("bones"), o)
                    k.actf(mean, (b1[0][:, :tn], b1[1]), AF.Identity, scale=1.0 / 64)
                    _, b2 = k.bank()
                    k.mm((b2[0][:, :tn], b2[1]), self.CF_("bones"), sq)
                    k.tt(msq, mean, mean, ALU.mult)
                    k.stt(var, (b2[0][:, :tn], b2[1]), 1.0 / 64, msq, ALU.mult, ALU.subtract)
                    k.actf(var, var, AF.Sqrt, bias=self.epsap)
                    k.recip(var, var)
                    k.tt(dd, o, mean, ALU.subtract)
                    k.stt(dd, dd, self.C_(f"retn{l}", p, 1), var, ALU.mult, ALU.mult)
                    ro = (rout[:, :tn], [rob])
                    k.tt(ro, dd, self.Q(2, p, bi), ALU.mult)
                    yield
                    for jj in range(DC):
                        _, pd = k.bank()
                        k.mm((pd[0][:, :tn], pd[1]), (wor[:, jj * 128:(jj + 1) * 128], [worb]), ro)
                        k.stt(self.H(jj, bi), (pd[0][:, :tn], pd[1]), self.DER(1, 1, jj, mi), self.H(jj, bi), ALU.mult, ALU.add)
                        yield

                for bi, (t0, tn, mi) in enumerate(BLOCKS):
                    if mi == 1 and not need_ctx:
                        continue
                    for g in (ret_gen(bi), att_gen(bi)):
                        for _ in g:
                            pass
            k.barrier()

    def fft_phase(self, l, need_ctx):
        k = self.k
        k.barrier()
        wout = self.inp(f"wout{l}", [D, D])
        dft = self.inp("dft", [S, 2, SL], BF16).rearrange("(nt p) c k -> p nt c k", p=128)
        with contextlib.ExitStack() as es:
            AB = self.sb(es, "fAB", [128, 32, 2, 256], BF16)
            ABb = Buf()
            fst = self.sb(es, "fst", [128, 2, 2, 512], BF16)
            fsr = Ring(2)
            tr = self.sb(es, "ftr", [128, 6, 2, 512], BF16)
            trr = Ring(6)
            fo = self.sb(es, "fo", [128, 2, 512], BF16)
            fob = Buf()
            wo = self.sb(es, "fwo", [128, 2, D], BF16)
            wob = Buf()
            k.dma(k.pool, wo[:], wout[256:512, :].rearrange("(c p) f -> p c f", p=128), [], [wob], k.dsem("wo"))

            def build_ab(dst, dbuf, srcs, ntok):
                for pc in range(0, ntok, 512):
                    n = min(512, ntok - pc)
                    fi, fb = fsr.next()
                    for ci in range(2):
                        for (src, off, nn) in srcs(ci, pc, n):
                            k.dma(k.sp, fst[:, fi, ci, off:off + nn], src, self.kvrd, [fb], k.dsem(f"vst{fi}"))
                    for tl in range(n // 128):
                        tt_ = pc // 128 + tl
                        for ci in range(2):
                            _, bk = k.bank()
                            k.mm((bk[0][:, 0:256], bk[1]), (fst[:, fi, ci, tl * 128:(tl + 1) * 128], [fb]), self.CB_("fd"))
                            k.actf((dst[:, tt_, ci, :], [dbuf]), (bk[0][:, 0:256], bk[1]), AF.Identity)

            def lat_src(ci, c0, n):
                out = []
                for (s0, t) in ((0, self.kva[1][(2 + ci) * 128:(3 + ci) * 128, :]), (SL, self.kva[1][(6 + ci) * 128:(7 + ci) * 128, :])):
                    a, b = max(c0, s0), min(c0 + n, s0 + SL)
                    if a < b:
                        out.append((t[:, a - s0:b - s0], a - c0, b - a))
                return out

            build_ab(AB, ABb, lat_src, S)
            for kb in range(4):
                t0, tn, mi = BLOCKS[kb]
                a0, acc0 = k.bank(hold=True)
                a1, acc1 = k.bank(hold=True)
                accs = [acc0, acc1]
                for nt_ in range(32):
                    ti, tb = trr.next()
                    k.dma(k.sp, tr[:, ti], dft[:, nt_, :, kb * 512:(kb + 1) * 512], [], [tb], k.dsem(f"dft{ti}"))
                    for c in range(2):
                        k.mm((accs[c][0], accs[c][1]), (AB[:, nt_, c, 0:128], [ABb]), (tr[:, ti, 0, :], [tb]),
                             start=(nt_ == 0), stop=False, inc=False)
                        k.mm((accs[c][0], accs[c][1]), (AB[:, nt_, c, 128:256], [ABb]), (tr[:, ti, 1, :], [tb]),
                             start=False, stop=(nt_ == 31), inc=(c == 1))
                for c in range(2):
                    k.actf((fo[:, c, :], [fob]), accs[c], AF.Identity)
                k.release(a0)
                k.release(a1)
                for j in range(DC):
                    _, pd = k.bank()
                    for c in range(2):
                        k.mm((pd[0][:, :tn], pd[1]), (wo[:, c, j * 128:(j + 1) * 128], [wob]), (fo[:, c, :tn], [fob]),
                             start=(c == 0), stop=(c == 1), inc=(c == 1))
                    k.stt(self.H(j, kb), (pd[0][:, :tn], pd[1]), self.DER(1, 1, j, mi), self.H(j, kb), ALU.mult, ALU.add)
            if need_ctx:
                dftc = self.inp("dftc", [C, 2, C], BF16).rearrange("(nt p) c k -> p nt c k", p=128)
                ABc = self.sb(es, "fABc", [128, 2, 2, 256], BF16)
                ABcb = Buf()
                trc = self.sb(es, "ftrc", [128, 2, 2, C], BF16)
                trcb = Buf()
                k.dma(k.sp, trc[:], dftc, [], [trcb], k.dsem("dft0"))
                build_ab(ABc, ABcb, lambda ci, c0, n: [(self.kvc[(6 + ci) * 128:(7 + ci) * 128, c0:c0 + n], 0, n)], C)
                t0, tn, mi = BLOCKS[4]
                a0, acc0 = k.bank(hold=True)
                a1, acc1 = k.bank(hold=True)
                accs = [acc0, acc1]
                for nt_ in range(2):
                    for c in range(2):
                        k.mm((accs[c][0][:, :tn], accs[c][1]), (ABc[:, nt_, c, 0:128], [ABcb]), (trc[:, nt_, 0, :], [trcb]),
                             start=(nt_ == 0), stop=False, inc=False)
                        k.mm((accs[c][0][:, :tn], accs[c][1]), (ABc[:, nt_, c, 128:256], [ABcb]), (trc[:, nt_, 1, :], [trcb]),
                             start=False, stop=(nt_ == 1), inc=(c == 1))
                for c in range(2):
                    k.actf((fo[:, c, :tn], [fob]), (accs[c][0][:, :tn], accs[c][1]), AF.Identity)
                k.release(a0)
                k.release(a1)
                for j in range(DC):
                    _, pd = k.bank()
                    for c in range(2):
                        k.mm((pd[0][:, :tn], pd[1]), (wo[:, c, j * 128:(j + 1) * 128], [wob]), (fo[:, c, :tn], [fob]),
                             start=(c == 0), stop=(c == 1), inc=(c == 1))
                    k.stt(self.H(j, 4), (pd[0][:, :tn], pd[1]), self.DER(1, 1, j, mi), self.H(j, 4), ALU.mult, ALU.add)
            k.barrier()

    def final_norm(self):
        k = self.k
        k.barrier()
        y = self.outp("yT", [D, SL])
        with contextlib.ExitStack() as es:
            sq8 = self.sb(es, "sq8", [128, 3, 512], BF16)
            sqr = Ring(3)
            rs = self.sb(es, "rs", [128, 2, 512], F32)
            rsr = Ring(2)
            tt = self.sb(es, "mtt", [128, 2, 512], F32)
            ttr = Ring(2)
            yo = self.sb(es, "yo", [128, 3, 512], F32)
            yr = Ring(3)
            for bi, (t0, tn, mi) in enumerate(LAT):
                _, bk = k.bank()
                for dc in range(DC):
                    qi, qbf = sqr.next()
                    k.actf((sq8[:, qi, :tn], [qbf]), self.H(dc, bi), AF.Square)
                    k.mm((bk[0][:, :tn], bk[1]), self.CB_("ones"), (sq8[:, qi, :tn], [qbf]), start=(dc == 0),
                         stop=(dc == DC - 1))
                ri, rbuf = rsr.next()
                r = (rs[:, ri, :tn], [rbuf])
                k.actf(r, (bk[0][:, :tn], bk[1]), AF.Sqrt, scale=1.0 / D, bias=self.epsap)
                k.recip(r, r)
                for dc in range(DC):
                    yi, yb = yr.next()
                    o = (yo[:, yi, :tn], [yb])
                    k.stt(o, self.H(dc, bi), self.C_("fng", dc, 1), r, ALU.mult, ALU.mult)
                    k.dma(k.sp, y[dc * 128:(dc + 1) * 128, t0:t0 + tn], o[0], o[1], [], k.dsem("yout"))
            k.barrier()


def build_prog(seg, fused=False, phases=None):
    p = Prog(seg, fused)
    if phases is not None:
        p.phases = phases
    p.epsap = EPS
    nc = p.build()
    return p, nc


_PROGS = {}


def get_prog(seg):
    if seg not in _PROGS:
        _PROGS[seg] = build_prog(seg)
    return _PROGS[seg]


def kernel(**inp):
    inp = {k_: np.asarray(v) for k_, v in inp.items()}
    cf, cb = host_static()
    cores = list(range(8))
    csts = [host_consts(inp, c // 2, c % 2) for c in cores]
    ropes = [host_rope(s) for s in range(2)]
    wext = [host_w_in_ext(inp["w_in"][l]) for l in range(2)]
    wsT = [np.ascontiguousarray(np.transpose(inp["gmlp_w_s"][l], (2, 0, 1)).reshape(128, 512), np.float32) for l in range(2)]
    adawh = [[np.ascontiguousarray(inp["ada_w"][l][:, r * 4608:(r + 1) * 4608]) for r in range(2)] for l in range(2)]

    def base(c):
        return {"cst": csts[c], "cf": cf, "cb": cb}

    def full(name, c, state):
        b, s = c // 2, c % 2
        if name in ("cst", "cf", "cb"):
            return base(c)[name]
        if name == "h0":
            return np.ascontiguousarray(np.concatenate([inp["x"][b, s * SL:(s + 1) * SL].T, inp["ctx"][b].T], axis=1), np.float32)
        if name == "rope":
            return ropes[s]
        if name == "dft":
            return host_dft(s)[0]
        if name == "dftc":
            return host_dft(s)[1]
        for l in range(2):
            if name == f"ada_wh{l}":
                return adawh[l][s]
            if name == f"win{l}":
                return wext[l]
            if name == f"wout{l}":
                return np.ascontiguousarray(inp["w_out"][l])
            if name == f"wsT{l}":
                return wsT[l]
            for w in (1, 2):
                if name == f"wgu{w}_{l}":
                    return np.ascontiguousarray(inp[f"ffn{w}_w_gu"][l])
                if name == f"wd{w}_{l}":
                    return np.ascontiguousarray(inp[f"ffn{w}_w_down"][l])
        if name.startswith("i_st_"):
            return state[c]["o_st_" + name[5:]]
        if name == "kvf_own":
            return state[c]["kvf_loc"]
        if name == "kvf_oth":
            return state[c ^ 1]["kvf_loc"]
        if name == "kvf_ctx_in":
            return state[c]["kvf_ctx"]
        raise KeyError(name)

    p, nc = get_prog(0)
    in_maps = [{n: full(n, c, None) for n in p.in_names} for c in cores]
    res = run_bass_kernel_spmd(nc, in_maps, core_ids=cores)
    state = res.results
    out = np.empty((4, S, D), np.float32)
    for c in cores:
        b, s = c // 2, c % 2
        out[b, s * SL:(s + 1) * SL, :] = state[c]["yT"].T
    return out
```

```python
import contextlib
import os
import math
import numpy as np
import ml_dtypes
import concourse.bass as bass
import concourse.mybir as mybir
from concourse.bass_utils import run_bass_kernel_spmd

F32 = mybir.dt.float32
BF16 = mybir.dt.bfloat16
AF = mybir.ActivationFunctionType
ALU = mybir.AluOpType
NPBF = ml_dtypes.bfloat16

D = 1024
DC = 8
S = 4096
SL = 2048
C = 256
NT = SL + C
DFF = 2816
EPS = 1e-6
BLOCKS = [(0, 512, 0), (512, 512, 0), (1024, 512, 0), (1536, 512, 0), (2048, 256, 1)]
LAT = BLOCKS[:4]
NKC = 34
FFN_GENS = [(0, 3), (3, 3), (6, 3), (9, 3), (12, 3), (15, 3), (18, 3), (21, 1)]
SAME_ENG_SYNC = True

CST = {}
_off = 0


def _c(name, w):
    global _off
    CST[name] = (_off, w)
    _off += w


for _l in range(2):
    _c(f"adab{_l}", 72)
    _c(f"adabh{_l}", 36)
    _c(f"g1_{_l}", 8)
    _c(f"gm_{_l}", 8)
    _c(f"g2_{_l}", 8)
    _c(f"retn{_l}", 2)
    _c(f"gmn{_l}", 2)
    _c(f"aqn{_l}", 1)
    _c(f"aqs{_l}", 1)
    _c(f"akn{_l}", 1)
    _c(f"aks{_l}", 1)
    _c(f"lgf{_l}", 4)
    _c(f"lgb{_l}", 4)
    _c(f"bt{_l}", 256)
_c("fng", 8)
_c("cc", 16)
_c("pmd", 8)
_c("pmdn", 8)
_c("sgf", 56)
_c("sgb", 56)
_c("dlt", 56)
_c("d1", 8)
_c("d2", 8)
NCST = _off

CF = {"ones": (0, 128), "bones": (128, 128), "shift": (256, 64), "iota": (320, 512)}
NCF = 832
CB = {"ident": (0, 128), "ones": (128, 128), "bones": (256, 128), "fd": (384, 256)}
NCB = 640


def _fm(v):
    v = np.asarray(v, np.float32).reshape(-1, 128)
    return np.ascontiguousarray(v.T)


def _swap32(a, axis=-1):
    a = np.moveaxis(a, axis, -1)
    sh = a.shape
    b = a.reshape(sh[:-1] + (sh[-1] // 64, 2, 32))[..., ::-1, :].reshape(sh)
    return np.moveaxis(b, -1, axis)


def host_consts(inp, b, s):
    cst = np.zeros((128, NCST), np.float32)

    def put(name, arr):
        o, w = CST[name]
        arr = np.asarray(arr, np.float32)
        assert arr.shape == (128, w), (name, arr.shape)
        cst[:, o:o + w] = arr

    for l in range(2):
        put(f"adab{l}", _fm(inp["ada_b"][l]))
        put(f"adabh{l}", _fm(inp["ada_b"][l])[:, s * 36:(s + 1) * 36])
        put(f"g1_{l}", _fm(inp["norm_ffn1"][l]))
        put(f"gm_{l}", _fm(inp["norm_mix"][l]))
        put(f"g2_{l}", _fm(inp["norm_ffn2"][l]))
        put(f"retn{l}", _fm(inp["ret_norm"][l]))
        put(f"gmn{l}", _fm(inp["gmlp_norm"][l]))
        qn = np.asarray(inp["att_q_norm"][l], np.float32)
        kn = np.asarray(inp["att_k_norm"][l], np.float32)
        put(f"aqn{l}", np.tile(qn, 2)[:, None])
        put(f"aqs{l}", np.tile(_swap32(qn), 2)[:, None])
        put(f"akn{l}", np.tile(kn, 2)[:, None])
        put(f"aks{l}", np.tile(_swap32(kn), 2)[:, None])
        put(f"lgf{l}", np.broadcast_to(np.asarray(inp["ret_log_decay_fwd"][l], np.float32)[None, :], (128, 4)))
        put(f"lgb{l}", np.broadcast_to(np.asarray(inp["ret_log_decay_bwd"][l], np.float32)[None, :], (128, 4)))
        bs = np.asarray(inp["gmlp_b_s"][l], np.float32)
        bt = np.zeros((128, 2, 128), np.float32)
        for g in range(4):
            bt[(g % 2) * 64:(g % 2) * 64 + 64, g // 2, :] = bs[g][None, :]
        put(f"bt{l}", bt.reshape(128, 256))
    put("fng", _fm(inp["final_norm"]))
    cc = np.zeros((128, 8, 2), np.float32)
    cc[:, :, 0] = _fm(inp["c"][b])
    cc[:, :, 1] = _fm(inp["c_ctx"])
    put("cc", cc.reshape(128, 16))
    pmd = np.array([(s - r) * 2048 - 128 * m for r in range(2) for m in range(4)], np.float32)
    put("pmd", np.broadcast_to(pmd[None, :], (128, 8)))
    put("pmdn", np.broadcast_to(-pmd[None, :], (128, 8)))
    dlt = np.array([(s - r) * 2048 + 128 * (di - 15) for r in range(2) for di in range(28)], np.float32)
    sgf = (dlt > 0).astype(np.float32)
    put("dlt", np.broadcast_to(dlt[None, :], (128, 56)))
    put("sgf", np.broadcast_to(sgf[None, :], (128, 56)))
    put("sgb", np.broadcast_to((sgf - 1.0)[None, :], (128, 56)))
    d1 = np.zeros(8, np.float32)
    d2 = np.zeros(8, np.float32)
    for qb in range(4):
        for kc in range(2):
            d1[qb * 2 + kc] = s * 2048 + qb * 512 + 256 - kc * 128
            d2[qb * 2 + kc] = 4096 - s * 2048 - qb * 512 + kc * 128
    put("d1", np.broadcast_to(d1[None, :], (128, 8)))
    put("d2", np.broadcast_to(d2[None, :], (128, 8)))
    return cst


def host_static():
    cf = np.zeros((128, NCF), np.float32)
    cf[:, 0:128] = 1.0
    bo = np.zeros((128, 128), np.float32)
    bo[0:64, 0:64] = 1.0
    bo[64:128, 64:128] = 1.0
    cf[:, 128:256] = bo
    sh = np.zeros((128, 64), np.float32)
    sh[64 + np.arange(64), np.arange(64)] = 1.0
    cf[:, 256:320] = sh
    cf[:, 320:832] = np.arange(512, dtype=np.float32)[None, :] - np.arange(128, dtype=np.float32)[:, None]
    cb = np.zeros((128, NCB), np.float32)
    cb[:, 0:128] = np.eye(128)
    cb[:, 128:256] = 1.0
    cb[:, 256:384] = bo
    de = np.outer(np.arange(64), np.arange(64)).astype(np.float64) * (2 * np.pi / 64)
    c64, s64 = np.cos(de), np.sin(de)
    fd = np.zeros((128, 256))
    fd[0:64, 0:64] = c64
    fd[64:128, 64:128] = c64
    fd[0:64, 128:192] = s64
    fd[64:128, 192:256] = s64
    cb[:, 384:640] = fd
    return cf, cb.astype(NPBF)


def host_rope(s):
    p = np.arange(128)
    f = p % 32
    sign = np.where((p % 64) < 32, -1.0, 1.0)[:, None]
    idx = s * SL + np.arange(SL)
    row = (idx // 64).astype(np.float64)
    col = (idx % 64).astype(np.float64)
    ax_freq = 10000.0 ** (-np.arange(16, dtype=np.float64) / 16)
    ang = np.concatenate([row[:, None] * ax_freq, col[:, None] * ax_freq], -1)
    angA = ang[:, f].T
    ret_freq = 1.0 / (10000.0 ** np.linspace(0.0, 1.0, 32))
    angR = ((C + idx)[:, None] * ret_freq)[:, f].T
    angRc = (np.arange(C)[:, None] * ret_freq)[:, f].T
    t = np.zeros((128, 4, NT), np.float32)
    t[:, 0, :SL] = np.cos(angA)
    t[:, 1, :SL] = np.sin(angA) * sign
    t[:, 0, SL:] = 1.0
    t[:, 2, :SL] = np.cos(angR)
    t[:, 3, :SL] = np.sin(angR) * sign
    t[:, 2, SL:] = np.cos(angRc)
    t[:, 3, SL:] = np.sin(angRc) * sign
    return t


_DFT_CACHE = {}


def host_dft(s):
    if s in _DFT_CACHE:
        return _DFT_CACHE[s]
    n = np.arange(S).astype(np.int64)
    kk = (s * SL + np.arange(SL)).astype(np.int64)
    ph = (np.outer(n, kk) % S).astype(np.float64) * (2 * np.pi / S)
    t = np.empty((S, 2, SL), NPBF)
    t[:, 0, :] = (np.cos(ph) / 512.0).astype(NPBF)
    t[:, 1, :] = (-np.sin(ph) / 512.0).astype(NPBF)
    ph = np.outer(np.arange(C), np.arange(C)).astype(np.float64) * (2 * np.pi / C)
    tc = np.empty((C, 2, C), NPBF)
    tc[:, 0, :] = (np.cos(ph) / 128.0).astype(NPBF)
    tc[:, 1, :] = (-np.sin(ph) / 128.0).astype(NPBF)
    _DFT_CACHE[s] = (t, tc)
    return t, tc


def host_w_in_ext(w):
    def sw(x):
        return _swap32(x, axis=1)
    retq, retk, retv, retg = w[:, 0:256], w[:, 256:512], w[:, 512:768], w[:, 768:1024]
    fnet = w[:, 1024:1280]
    aq = w[:, 1280:1536].reshape(1024, 4, 64)
    aq = np.concatenate([aq[:, 0], aq[:, 2], aq[:, 1], aq[:, 3]], axis=1)
    ak, av = w[:, 1536:1664], w[:, 1664:1792]
    gu, gv = w[:, 1792:2048], w[:, 2048:2304]
    return np.ascontiguousarray(np.concatenate(
        [retq, sw(retq), retk, sw(retk), retv, retg, fnet, aq, sw(aq), ak, sw(ak), av, gu, gv], axis=1), np.float32)


class Sem:
    def __init__(self, h, dma=False):
        self.h = h
        self.n = 0
        self.dma = dma


class Eng:
    def __init__(self, name, e, sem):
        self.name = name
        self.e = e
        self.sem = sem
        self.waited = {}
        self.pending = False


class Buf:
    __slots__ = ("w", "r", "name", "small")

    def __init__(self, name=""):
        self.w = {}
        self.r = {}
        self.name = name
        self.small = {}


class KB:
    def __init__(self, nc, es):
        self.nc = nc
        self.es = es
        self.sems = []
        self.pe = Eng("pe", nc.tensor, self.newsem("pe"))
        self.act = Eng("act", nc.scalar, self.newsem("act"))
        self.dve = Eng("dve", nc.vector, self.newsem("dve"))
        self.pool = Eng("pool", nc.gpsimd, None)
        self.sp = Eng("sp", nc.sync, None)
        self.engs = [self.pe, self.act, self.dve, self.pool, self.sp]
        self.ps = es.enter_context(nc.psum_tensor("ps", [128, 6, 512], F32))
        self.psb = es.enter_context(nc.psum_tensor("psb", [128, 2, 1024], BF16))
        self.banks = [Buf(f"bank{i}") for i in range(6)]
        self.bbanks = [Buf(f"bbank{i}") for i in range(2)]
        self.bptr = 0
        self.bbptr = 0
        self.held = set()
        self.dsems = {}

    def newsem(self, name, dma=False):
        s = Sem(self.es.enter_context(self.nc.semaphore(name)), dma)
        self.sems.append(s)
        return s

    def dsem(self, name):
        if name not in self.dsems or self.dsems[name].n > 2400:
            self.nds = getattr(self, "nds", 0) + 1
            self.dsems[name] = self.newsem(f"d{self.nds}_" + name, dma=True)
        return self.dsems[name]

    def bank(self, hold=False):
        for _ in range(8):
            i = self.bptr
            self.bptr = (self.bptr + 1) % 6
            if i not in self.held:
                if hold:
                    self.held.add(i)
                return i, (self.ps[:, i, :], [self.banks[i]])
        raise RuntimeError("no bank")

    def release(self, i):
        self.held.discard(i)

    def bbank(self):
        i = self.bbptr
        self.bbptr = (self.bbptr + 1) % 2
        return (self.psb[:, i, :], [self.bbanks[i]])

    def _wait(self, E, reads, writes):
        deps = {}
        for b in reads:
            for s, v in b.w.items():
                if deps.get(s, 0) < v:
                    deps[s] = v
        for b in writes:
            for s, v in b.w.items():
                if deps.get(s, 0) < v:
                    deps[s] = v
            for s, v in b.r.items():
                if deps.get(s, 0) < v:
                    deps[s] = v
        for s, v in deps.items():
            if s is E.sem:
                if E is self.pe or not SAME_ENG_SYNC:
                    continue
                if v > s.n:
                    continue
                if E is self.dve:
                    sm = 0
                    for b in list(reads) + list(writes):
                        sm = max(sm, b.small.get(s, 0))
                    if sm == 0:
                        continue
                    v = min(v, sm)
            if s.dma:
                v = s.n
            elif s is not E.sem:
                sm = False
                for b in list(reads) + list(writes):
                    if b.small.get(s, 0) == v:
                        sm = True
                        break
                if sm:
                    v = max(v, min(v + 1, s.n))
            if E.waited.get(s, 0) < v:
                E.e.wait_ge(s.h, v)
                E.waited[s] = v

    def op(self, E, fn, reads, writes, inc=True, small=True):
        if E is not self.pe:
            assert not self.pe.pending
        self._wait(E, reads, writes)
        ins = fn()
        if inc:
            E.sem.n += 1
            ins.then_inc(E.sem.h, 1)
            ev = E.sem.n
            E.pending = False
        else:
            assert E is self.pe
            ev = E.sem.n + 1
            E.pending = True
        s = E.sem
        for b in reads:
            if b.r.get(s, 0) < ev:
                b.r[s] = ev
        for b in writes:
            if b.w.get(s, 0) < ev:
                b.w[s] = ev
            if small:
                b.small[s] = ev
        return ins

    def dma(self, E, out_ap, in_ap, reads, writes, sem):
        assert not self.pe.pending
        self._wait(E, reads, writes)
        ins = E.e.dma_start(out=out_ap, in_=in_ap)
        sem.n += 16
        ins.then_inc(sem.h, 16)
        for b in reads:
            b.r[sem] = sem.n
        for b in writes:
            b.w[sem] = sem.n
        return ins

    def barrier(self):
        assert not self.pe.pending
        for E in self.engs:
            for s in self.sems:
                if s is E.sem:
                    continue
                if E.waited.get(s, 0) < s.n:
                    E.e.wait_ge(s.h, s.n)
                    E.waited[s] = s.n
        for E in (self.pe, self.act, self.dve):
            if E.sem.n > 1500:
                self.nes = getattr(self, "nes", 0) + 1
                E.sem = self.newsem(f"{E.name}{self.nes}")

    def mm(self, out, lhsT, rhs, start=True, stop=True, inc=True):
        return self.op(self.pe, lambda: self.nc.tensor.matmul(out[0], lhsT[0], rhs[0], start=start, stop=stop),
                       lhsT[1] + rhs[1], out[1], inc=inc, small=self._small(out[0]))

    def transpose(self, out, in_, ident):
        return self.op(self.pe, lambda: self.nc.tensor.transpose(out[0], in_[0], ident[0]),
                       in_[1] + ident[1], out[1], small=True)

    def actf(self, out, in_, func, scale=1.0, bias=0.0):
        rd = list(in_[1])
        sc, bi = scale, bias
        if isinstance(scale, tuple):
            rd += scale[1]
            sc = scale[0]
        if isinstance(bias, tuple):
            rd += bias[1]
            bi = bias[0]
        return self.op(self.act, lambda: self.nc.scalar.activation(out=out[0], in_=in_[0], func=func, bias=bi, scale=sc),
                       rd, out[1], small=self._small(out[0]))

    @staticmethod
    def _small(ap):
        n = 1
        for d_ in list(ap.shape)[1:]:
            n *= int(d_)
        return n < 256

    def tt(self, out, a, b, op):
        return self.op(self.dve, lambda: self.nc.vector.tensor_tensor(out=out[0], in0=a[0], in1=b[0], op=op),
                       a[1] + b[1], out[1], small=self._small(out[0]))

    def ts(self, out, a, s1, op0, s2=None, op1=None):
        rd = list(a[1])
        v1, v2 = s1, s2
        if isinstance(s1, tuple):
            rd += s1[1]
            v1 = s1[0]
        if isinstance(s2, tuple):
            rd += s2[1]
            v2 = s2[0]
        if op1 is None:
            return self.op(self.dve, lambda: self.nc.vector.tensor_scalar(out=out[0], in0=a[0], scalar1=v1, scalar2=None, op0=op0),
                           rd, out[1], small=self._small(out[0]))
        return self.op(self.dve, lambda: self.nc.vector.tensor_scalar(out=out[0], in0=a[0], scalar1=v1, scalar2=v2, op0=op0, op1=op1),
                       rd, out[1], small=self._small(out[0]))

    def stt(self, out, a, sc, b, op0, op1):
        rd = list(a[1]) + list(b[1])
        v = sc
        if isinstance(sc, tuple):
            rd += sc[1]
            v = sc[0]
        return self.op(self.dve, lambda: self.nc.vector.scalar_tensor_tensor(out=out[0], in0=a[0], scalar=v, in1=b[0], op0=op0, op1=op1),
                       rd, out[1], small=self._small(out[0]))

    def recip(self, out, a):
        return self.op(self.dve, lambda: self.nc.vector.reciprocal(out=out[0], in_=a[0]), a[1], out[1], small=self._small(out[0]))

    def copy(self, out, a):
        return self.op(self.dve, lambda: self.nc.vector.tensor_copy(out=out[0], in_=a[0]), a[1], out[1], small=self._small(out[0]))

    def memset(self, out, val):
        return self.op(self.dve, lambda: self.nc.vector.memset(out[0], val), [], out[1])


class Ring:
    def __init__(self, n):
        self.n = n
        self.i = 0
        self.bufs = [Buf() for _ in range(n)]

    def next(self):
        i = self.i
        self.i = (self.i + 1) % self.n
        return i, self.bufs[i]


class Prog:
    def __init__(self, seg, fused):
        self.phases = ("L0", "L1", "mix", "fft", "ffn2", "ffn1")
        self.seg = seg
        self.fused = fused
        self.in_names = {}
        self.out_names = {}

    def inp(self, name, shape, dt=F32):
        if name not in self.in_names:
            self.in_names[name] = self.nc.dram_tensor(name, list(shape), dt, kind="ExternalInput").ap()
        return self.in_names[name]

    def outp(self, name, shape, dt=F32):
        if name not in self.out_names:
            self.out_names[name] = self.nc.dram_tensor(name, list(shape), dt, kind="ExternalOutput").ap()
        return self.out_names[name]

    def sb(self, es, name, shape, dt):
        self._sbn = getattr(self, "_sbn", 0) + 1
        return es.enter_context(self.nc.sbuf_tensor(f"s{self._sbn}_{name}", list(shape), dt))

    def build(self):
        nc = bass.Bass("TRN2", target_bir_lowering=False)
        self.nc = nc
        seg = self.seg
        with contextlib.ExitStack() as es:
            k = KB(nc, es)
            self.k = k
            self.hT = self.sb(es, "hT", [128, DC, NT], F32)
            self.hb = [[Buf(f"h{dc}_{bi}") for bi in range(5)] for dc in range(DC)]
            self.cst = self.sb(es, "cst", [128, NCST], F32)
            self.cf = self.sb(es, "cf", [128, NCF], F32)
            self.cb = self.sb(es, "cb", [128, NCB], BF16)
            self.modv = self.sb(es, "modv", [128, 72, 2], F32)
            self.der = self.sb(es, "der", [128, 3, 2, 8, 2], F32)
            self.qs = self.sb(es, "qs", [128, 3, 2, NT], BF16)
            self.cbuf = Buf("const")
            self.modb = Buf("modv")
            self.derb = Buf("der")
            self.qb = [[Buf(f"q{i}_{bi}") for bi in range(5)] for i in range(3)]
            cs = k.dsem("const")
            k.dma(k.sp, self.cst[:], self.inp("cst", [128, NCST]), [], [self.cbuf], cs)
            k.dma(k.sp, self.cf[:], self.inp("cf", [128, NCF]), [], [self.cbuf], cs)
            k.dma(k.sp, self.cb[:], self.inp("cb", [128, NCB], BF16), [], [self.cbuf], cs)
            h0 = self.inp("h0", [D, NT])
            for dc in range(DC):
                k.dma(k.sp, self.hT[:, dc, :], h0[dc * 128:(dc + 1) * 128, :], [],
                      [self.hb[dc][bi] for bi in range(5)], k.dsem("hload"))
            self.ccsem = k.newsem("cc")
            for l in range(2):
                need_ctx = (l == 0)
                self.kvl = [nc.dram_tensor(f"kvloc{l}_{p_}", [4 * 128, SL], BF16).ap() for p_ in range(2)]
                self.kva = [nc.dram_tensor(f"kvall{l}_{p_}", [8 * 128, SL], BF16).ap() for p_ in range(2)]
                self.kvc = nc.dram_tensor(f"kvctx{l}", [8 * 128, C], BF16).ap()
                self.kvlb, self.kvab, self.kvcb = Buf("kvl"), Buf("kva"), Buf("kvc")
                self.kvrd = [self.kvab, self.kvcb]
                ph = self.phases
                if f"L{l}" not in ph:
                    continue
                self.ada(l)
                self.derive(l)
                if "ffn1" in ph:
                    self.ffn(l, 1, 0, BLOCKS)
                self.proj_phase(l, need_ctx)
                k._wait(k.pool, [self.kvlb], [self.kvab])
                for p_ in range(2):
                    ins = nc.gpsimd.collective_compute("AllGather", ALU.bypass, replica_groups=[[0, 1], [2, 3], [4, 5], [6, 7]],
                                                       ins=[self.kvl[p_].opt()], outs=[self.kva[p_].opt()])
                    ins.then_inc(self.ccsem.h)
                    self.ccsem.n += 1
                self.kvab.w[self.ccsem] = self.ccsem.n
                self.kvlb.r[self.ccsem] = self.ccsem.n
                if "mix" in ph:
                    self.mix_phase(l, need_ctx)
                if "attn" in ph:
                    self.attn_phase(l, need_ctx)
                if "ret" in ph:
                    self.ret_phase(l, need_ctx)
                if "fft" in ph:
                    self.fft_phase(l, need_ctx)
                if "ffn2" in ph:
                    self.ffn(l, 2, 2, BLOCKS if need_ctx else LAT)
            self.final_norm()
            k.barrier()
        return nc

    def C_(self, name, a=0, w=None):
        o, ww = CST[name]
        if w is None:
            w = ww - a
        return (self.cst[:, o + a:o + a + w], [self.cbuf])

    def CF_(self, name, parts=slice(0, 128)):
        o, w = CF[name]
        return (self.cf[parts, o:o + w], [self.cbuf])

    def CB_(self, name, a=0, w=None, parts=slice(0, 128)):
        o, ww = CB[name]
        if w is None:
            w = ww - a
        return (self.cb[parts, o + a:o + a + w], [self.cbuf])

    def H(self, dc, bi):
        t0, tn, _ = BLOCKS[bi]
        return (self.hT[:, dc, t0:t0 + tn], [self.hb[dc][bi]])

    def Q(self, i, ci, bi, parts=slice(0, 128)):
        t0, tn, _ = BLOCKS[bi]
        return (self.qs[parts, i, ci, t0:t0 + tn], [self.qb[i][bi]])

    def DER(self, sub, kind, dc, mi):
        return (self.der[:, sub, kind, dc, mi:mi + 1], [self.derb])

    def MODV(self, j, mi):
        return (self.modv[:, j, mi:mi + 1], [self.modb])

    def save_state(self):
        k = self.k
        k.barrier()
        ds = k.dsem("state")
        o = self.outp("o_st_h", [128, DC * NT])
        k.dma(k.sp, o, self.hT[:].rearrange("p a b -> p (a b)"), [b for r in self.hb for b in r], [], ds)
        o = self.outp("o_st_mod", [128, 144])
        k.dma(k.sp, o, self.modv[:].rearrange("p a b -> p (a b)"), [self.modb], [], ds)
        o = self.outp("o_st_q", [128, 6 * NT], BF16)
        k.dma(k.sp, o, self.qs[:].rearrange("p a b c -> p (a b c)"), [b for r in self.qb for b in r], [], ds)

    def load_state(self):
        k = self.k
        ds = k.dsem("state")
        i = self.inp("i_st_h", [128, DC * NT])
        k.dma(k.sp, self.hT[:].rearrange("p a b -> p (a b)"), i, [], [b for r in self.hb for b in r], ds)
        i = self.inp("i_st_mod", [128, 144])
        k.dma(k.sp, self.modv[:].rearrange("p a b -> p (a b)"), i, [], [self.modb], ds)
        i = self.inp("i_st_q", [128, 6 * NT], BF16)
        k.dma(k.sp, self.qs[:].rearrange("p a b c -> p (a b c)"), i, [], [b for r in self.qb for b in r], ds)

    def ada(self, l):
        k, nc = self.k, self.nc
        k.barrier()
        w = self.inp(f"ada_wh{l}", [D, 36 * 128])
        wv = w.rearrange("(dc p) f -> p dc f", p=128)
        modl = nc.dram_tensor(f"modl{l}", [128, 72], F32).ap()
        moda = nc.dram_tensor(f"moda{l}", [256, 72], F32).ap()
        mlb, mab = Buf(), Buf()
        with contextlib.ExitStack() as es:
            ring = self.sb(es, "adaw", [128, 2, DC, 512], BF16)
            rb = [Buf(), Buf()]
            sT = self.sb(es, "adas", [128, 16], BF16)
            sb_ = Buf()
            modh = self.sb(es, "modh", [128, 36, 2], F32)
            mhb = Buf()
            k.actf((sT[:], [sb_]), self.C_("cc"), AF.Silu)
            for km in range(2):
                k.dma(k.pool, ring[:, km % 2], wv[:, :, km * 512:(km + 1) * 512], [], [rb[km % 2]], k.dsem(f"wr{km % 2}"))
            for km in range(9):
                sl = km % 2
                bi_, bk = k.bank()
                for jc in range(4):
                    for dc in range(DC):
                        k.mm((bk[0][:, jc * 2:jc * 2 + 2], bk[1]), (ring[:, sl, dc, jc * 128:(jc + 1) * 128], [rb[sl]]),
                             (sT[:, dc * 2:dc * 2 + 2], [sb_]), start=(dc == 0), stop=(dc == DC - 1),
                             inc=(dc == DC - 1))
                for jc in range(4):
                    j = km * 4 + jc
                    k.ts((modh[:, j, :], [mhb]), (bk[0][:, jc * 2:jc * 2 + 2], bk[1]),
                         self.C_(f"adabh{l}", j, 1), ALU.add)
                if km + 2 < 9:
                    k.dma(k.pool, ring[:, sl], wv[:, :, (km + 2) * 512:(km + 3) * 512], [], [rb[sl]], k.dsem(f"wr{sl}"))
            k.dma(k.sp, modl, modh[:].rearrange("p a b -> p (a b)"), [mhb], [mlb], k.dsem("modx"))
            k._wait(k.pool, [mlb], [mab])
            ins = nc.gpsimd.collective_compute("AllGather", ALU.bypass, replica_groups=[[0, 1], [2, 3], [4, 5], [6, 7]],
                                               ins=[modl.opt()], outs=[moda.opt()])
            ins.then_inc(self.ccsem.h)
            self.ccsem.n += 1
            mab.w[self.ccsem] = self.ccsem.n
            mlb.r[self.ccsem] = self.ccsem.n
            for r in range(2):
                k.dma(k.sp, self.modv[:, r * 36:(r + 1) * 36, :].rearrange("p a b -> p (a b)"), moda[r * 128:(r + 1) * 128, :],
                      [mab], [self.modb], k.dsem("modx"))
            k.barrier()

    def derive(self, l):
        k = self.k
        gn = [f"g1_{l}", f"gm_{l}", f"g2_{l}"]
        for sub in range(3):
            k0 = 3 * sub
            for mi in range(2):
                k.stt((self.der[:, sub, 0, :, mi], [self.derb]), (self.modv[:, (k0 + 1) * 8:(k0 + 2) * 8, mi], [self.modb]),
                      1.0, self.C_(gn[sub]), ALU.add, ALU.mult)
                k.ts((self.der[:, sub, 1, :, mi], [self.derb]), (self.modv[:, (k0 + 2) * 8:(k0 + 3) * 8, mi], [self.modb]),
                     1.0 if sub == 1 else 0.5, ALU.mult)

    def modulate(self, l, sub, blocks, nT, nb, es):
        k = self.k
        sq8 = self.sb(es, "sq8", [128, 3, 512], BF16)
        sqr = Ring(3)
        rs = self.sb(es, "rs", [128, 2, 512], F32)
        rsr = Ring(2)
        tt = self.sb(es, "mtt", [128, 2, 512], F32)
        ttr = Ring(2)
        for (t0, tn, mi) in blocks:
            bi = t0 // 512
            _, bk = k.bank()
            for dc in range(DC):
                qi, qbf = sqr.next()
                k.actf((sq8[:, qi, :tn], [qbf]), self.H(dc, bi), AF.Square)
                k.mm((bk[0][:, :tn], bk[1]), self.CB_("ones"), (sq8[:, qi, :tn], [qbf]), start=(dc == 0),
                     stop=(dc == DC - 1))
            ri, rbuf = rsr.next()
            r = (rs[:, ri, :tn], [rbuf])
            k.actf(r, (bk[0][:, :tn], bk[1]), AF.Sqrt, scale=1.0 / D, bias=self.epsap)
            k.recip(r, r)
            for dc in range(DC):
                ti, tb = ttr.next()
                t = (tt[:, ti, :tn], [tb])
                k.tt(t, self.H(dc, bi), r, ALU.mult)
                k.actf((nT[:, dc, t0:t0 + tn], [nb[bi]]), t, AF.Identity, scale=self.DER(sub, 0, dc, mi),
                       bias=self.MODV(3 * sub * 8 + dc, mi))

    def ffn(self, l, which, sub, blocks):
        k = self.k
        k.barrier()
        wgu = self.inp(f"wgu{which}_{l}", [D, 2 * DFF]).rearrange("(dc p) f -> p dc f", p=128)
        wdn = self.inp(f"wd{which}_{l}", [DFF, D]).rearrange("(c p) f -> p c f", p=128)
        with contextlib.ExitStack() as es:
            nT = self.sb(es, "nT", [128, DC, NT], BF16)
            nb = [Buf() for _ in range(5)]
            ring = self.sb(es, "wring", [128, 2, 9216], BF16)
            rb = [Buf(), Buf()]
            hid = self.sb(es, "hid", [128, 2, 3, 512], BF16)
            hr = Ring(2)
            sa = self.sb(es, "sa", [128, 2, 512], F32)
            sar = Ring(2)

            def load(gi):
                c0, G = FFN_GENS[gi]
                sl = gi % 2
                wa = ring[:, sl, 0:8 * G * 128].rearrange("p (c f) -> p c f", c=8)
                wb = ring[:, sl, 3072:3072 + 8 * G * 128].rearrange("p (c f) -> p c f", c=8)
                wd = ring[:, sl, 6144:6144 + G * 1024].rearrange("p (c f) -> p c f", c=G)
                ds = k.dsem(f"wr{sl}")
                k.dma(k.pool, wa, wgu[:, :, c0 * 128:(c0 + G) * 128], [], [rb[sl]], ds)
                k.dma(k.pool, wb, wgu[:, :, DFF + c0 * 128:DFF + (c0 + G) * 128], [], [rb[sl]], ds)
                k.dma(k.pool, wd, wdn[:, c0:c0 + G, :], [], [rb[sl]], ds)

            load(0)
            load(1)
            with contextlib.ExitStack() as es2:
                self.modulate(l, sub, blocks, nT, nb, es2)
            for gi, (c0, G) in enumerate(FFN_GENS):
                sl = gi % 2
                wa = ring[:, sl, 0:8 * G * 128].rearrange("p (c f) -> p c f", c=8)
                wb = ring[:, sl, 3072:3072 + 8 * G * 128].rearrange("p (c f) -> p c f", c=8)
                wd = ring[:, sl, 6144:6144 + G * 1024].rearrange("p (c f) -> p c f", c=G)
                for (t0, tn, mi) in blocks:
                    bi = t0 // 512
                    hi, hbuf = hr.next()
                    for c in range(G):
                        _, pa = k.bank()
                        for dc in range(DC):
                            k.mm((pa[0][:, :tn], pa[1]), (wa[:, dc, c * 128:(c + 1) * 128], [rb[sl]]),
                                 (nT[:, dc, t0:t0 + tn], [nb[bi]]), start=(dc == 0), stop=(dc == DC - 1), inc=(dc == DC - 1))
                        _, pb = k.bank()
                        for dc in range(DC):
                            k.mm((pb[0][:, :tn], pb[1]), (wb[:, dc, c * 128:(c + 1) * 128], [rb[sl]]),
                                 (nT[:, dc, t0:t0 + tn], [nb[bi]]), start=(dc == 0), stop=(dc == DC - 1), inc=(dc == DC - 1))
                        si, sbuf = sar.next()
                        s_ = (sa[:, si, :tn], [sbuf])
                        k.actf(s_, (pa[0][:, :tn], pa[1]), AF.Silu)
                        k.tt((hid[:, hi, c, :tn], [hbuf]), s_, (pb[0][:, :tn], pb[1]), ALU.mult)
                    for j in range(DC):
                        _, pd = k.bank()
                        for c in range(G):
                            k.mm((pd[0][:, :tn], pd[1]), (wd[:, c, j * 128:(j + 1) * 128], [rb[sl]]),
                                 (hid[:, hi, c, :tn], [hbuf]), start=(c == 0), stop=(c == G - 1), inc=(c == G - 1))
                        k.stt(self.H(j, bi), (pd[0][:, :tn], pd[1]), self.DER(sub, 1, j, mi), self.H(j, bi), ALU.mult, ALU.add)
                if gi + 2 < len(FFN_GENS):
                    load(gi + 2)
            k.barrier()

    def proj_phase(self, l, need_ctx):
        k = self.k
        k.barrier()
        win = self.inp(f"win{l}", [D, 25 * 128]).rearrange("(dc p) f -> p dc f", p=128)
        wout = self.inp(f"wout{l}", [D, D])
        rope = self.inp("rope", [128, 4, NT])
        wsT = self.inp(f"wsT{l}", [128, 512])
        with contextlib.ExitStack() as es:
            nT = self.sb(es, "nT", [128, DC, NT], BF16)
            nb = [Buf() for _ in range(5)]
            wr = self.sb(es, "pw", [128, 2, DC, 512], BF16)
            wrr = Ring(2)
            pre = {}

            def loadw(c0, n):
                if c0 in pre:
                    return pre.pop(c0)
                i, b = wrr.next()
                k.dma(k.pool, wr[:, i, :, 0:n * 128], win[:, :, c0 * 128:(c0 + n) * 128], [], [b], k.dsem(f"wr{i}"))
                return i, b

            def prefetch(c0, n):
                pre[c0] = loadw(c0, n)

            prefetch(21, 4)
            with contextlib.ExitStack() as es2:
                self.modulate(l, 1, BLOCKS, nT, nb, es2)
                k.barrier()

            def proj(wi, wbuf, ci, bi):
                t0, tn, mi = BLOCKS[bi]
                _, bk = k.bank()
                for dc in range(DC):
                    k.mm((bk[0][:, :tn], bk[1]), (wr[:, wi, dc, ci * 128:(ci + 1) * 128], [wbuf]),
                         (nT[:, dc, t0:t0 + tn], [nb[bi]]), start=(dc == 0), stop=(dc == DC - 1), inc=(dc == DC - 1))
                return (bk[0][:, :tn], bk[1])

            with contextlib.ExitStack() as es3:
                wi, wbuf = loadw(21, 4)
                prefetch(0, 4)
                wo = self.sb(es3, "gwo", [128, 2, D], BF16)
                wob = Buf()
                k.dma(k.pool, wo[:], wout[768:1024, :].rearrange("(c p) f -> p c f", p=128), [], [wob], k.dsem("wo"))
                ws = self.sb(es3, "gws", [128, 4, 128], BF16)
                wsb = Buf()
                k.dma(k.pool, ws[:].rearrange("p a b -> p (a b)"), wsT, [], [wsb], k.dsem("wo"))
                u = self.sb(es3, "gu", [128, 2, 512], F32)
                ub = Buf()
                gv = self.sb(es3, "gv", [128, 2, 512], F32)
                gvb = Buf()
                gq = self.sb(es3, "gq", [128, 2, 512], F32)
                gqb = Buf()
                st = self.sb(es3, "gst", [128, 4, 512], F32)
                stb = [Buf() for _ in range(4)]
                vn = self.sb(es3, "gvn", [128, 2, 512], BF16)
                vnb = Buf()
                vp = self.sb(es3, "gvp", [128, 2, 4, 128], BF16)
                vpr = Ring(2)
                go = self.sb(es3, "ggo", [128, 2, 512], BF16)
                gob = Buf()
                gt = self.sb(es3, "ggt", [128, 2, 128], F32)
                gtr = Ring(2)
                k.memset((vp[:], vpr.bufs), 0.0)
                for bi, (t0, tn, mi) in enumerate(BLOCKS):
                    if mi == 1 and not need_ctx:
                        continue
                    def gelu(dst, P):
                        a_ = (st[:, 0, :tn], [stb[0]])
                        b_ = (st[:, 1, :tn], [stb[1]])
                        k.actf(a_, P, AF.Square)
                        k.ts(a_, a_, 0.044715, ALU.mult, 1.0, ALU.add)
                        k.tt(a_, a_, P, ALU.mult)
                        k.actf(b_, a_, AF.Sigmoid, scale=1.5957691216057308)
                        k.tt(dst, b_, P, ALU.mult)

                    for c in range(2):
                        gelu((u[:, c, :tn], [ub]), proj(wi, wbuf, c, bi))
                    for c in range(2):
                        gelu((gv[:, c, :tn], [gvb]), proj(wi, wbuf, 2 + c, bi))
                        k.actf((gq[:, c, :tn], [gqb]), (gv[:, c, :tn], [gvb]), AF.Square)
                    _, b1 = k.bank()
                    for c in range(2):
                        k.mm((b1[0][:, :tn], b1[1]), self.CF_("ones"), (gv[:, c, :tn], [gvb]), start=(c == 0), stop=(c == 1), inc=(c == 1))
                    _, b2 = k.bank()
                    for c in range(2):
                        k.mm((b2[0][:, :tn], b2[1]), self.CF_("ones"), (gq[:, c, :tn], [gqb]), start=(c == 0), stop=(c == 1), inc=(c == 1))
                    mean = (st[:, 0, :tn], [stb[0]])
                    msq = (st[:, 1, :tn], [stb[1]])
                    var = (st[:, 2, :tn], [stb[2]])
                    dd = (st[:, 3, :tn], [stb[3]])
                    k.actf(mean, (b1[0][:, :tn], b1[1]), AF.Identity, scale=1.0 / 256)
                    k.tt(msq, mean, mean, ALU.mult)
                    k.stt(var, (b2[0][:, :tn], b2[1]), 1.0 / 256, msq, ALU.mult, ALU.subtract)
                    k.actf(var, var, AF.Sqrt, bias=self.epsap)
                    k.recip(var, var)
                    for c in range(2):
                        k.tt(dd, (gv[:, c, :tn], [gvb]), mean, ALU.subtract)
                        k.stt((vn[:, c, :tn], [vnb]), dd, self.C_(f"gmn{l}", c, 1), var, ALU.mult, ALU.mult)
                    for tl in range(tn // 128):
                        vi, vb = vpr.next()
                        for c in range(2):
                            pt = k.bbank()
                            k.transpose((pt[0][:, 0:128], pt[1]), (vn[:, c, tl * 128:(tl + 1) * 128], [vnb]), self.CB_("ident"))
                            k.copy((vp[:, vi, 2 * c, 0:64], [vb]), (pt[0][:, 0:64], pt[1]))
                            k.copy((vp[:, vi, 2 * c + 1, 64:128], [vb]), (pt[0][:, 64:128], pt[1]))
                        for c in range(2):
                            _, mb = k.bank()
                            for gg in range(2):
                                k.mm((mb[0][:, 0:128], mb[1]), (vp[:, vi, 2 * c + gg, :], [vb]), (ws[:, 2 * c + gg, :], [wsb]),
                                     start=(gg == 0), stop=(gg == 1), inc=(gg == 1))
                            gi_, gb_ = gtr.next()
                            g_ = (gt[:, gi_, :], [gb_])
                            o_, w_ = CST[f"bt{l}"]
                            k.tt(g_, (mb[0][:, 0:128], mb[1]), (self.cst[:, o_ + c * 128:o_ + (c + 1) * 128], [self.cbuf]), ALU.add)
                            k.tt((go[:, c, tl * 128:(tl + 1) * 128], [gob]), g_, (u[:, c, tl * 128:(tl + 1) * 128], [ub]), ALU.mult)
                    for j in range(DC):
                        _, pd = k.bank()
                        for c in range(2):
                            k.mm((pd[0][:, :tn], pd[1]), (wo[:, c, j * 128:(j + 1) * 128], [wob]), (go[:, c, :tn], [gob]),
                                 start=(c == 0), stop=(c == 1), inc=(c == 1))
                        k.stt(self.H(j, bi), (pd[0][:, :tn], pd[1]), self.DER(1, 1, j, mi), self.H(j, bi), ALU.mult, ALU.add)
                k.barrier()

            with contextlib.ExitStack() as es3:
                rr = self.sb(es3, "rope", [128, 2, 2, 512], F32)
                rrr = Ring(2)
                tmp = self.sb(es3, "ptmp", [128, 4, 512], F32)
                tr = Ring(4)
                stg = self.sb(es3, "pstg", [128, 3, 512], BF16)
                sr = Ring(3)
                sqt = self.sb(es3, "psq", [128, 2, 512], BF16)
                sqr = Ring(2)
                rst = self.sb(es3, "prs", [128, 2, 512], F32)
                rsr = Ring(2)

                def T_(tn):
                    i, b = tr.next()
                    return (tmp[:, i, :tn], [b])

                def store(src, row, bi):
                    t0, tn, mi = BLOCKS[bi]
                    if mi == 0:
                        k.dma(k.sp, self.kvl[row // 4][(row % 4) * 128:(row % 4 + 1) * 128, t0:t0 + tn], src[0], src[1], [self.kvlb], k.dsem("kvst"))
                    else:
                        k.dma(k.sp, self.kvc[row * 128:(row + 1) * 128, 0:tn], src[0], src[1], [self.kvcb], k.dsem("kvst"))

                NEXT = {0: (4, 4), 4: (8, 2), 8: (10, 2), 10: (12, 2), 12: (14, 4), 14: (18, 2), 18: (20, 1)}

                def rope_unit(c0, nch, tab, kind, dst):
                    wi, wbuf = loadw(c0, 2 * nch)
                    if c0 in NEXT:
                        prefetch(*NEXT[c0])
                    for bi, (t0, tn, mi) in enumerate(BLOCKS):
                        if mi == 1 and not need_ctx and kind in ("retq", "attq"):
                            continue
                        ri, rbuf = rrr.next()
                        k.dma(k.sp, rr[:, ri, :, :tn], rope[:, tab:tab + 2, t0:t0 + tn], [], [rbuf], k.dsem(f"rope{ri}"))
                        cosT = (rr[:, ri, 0, :tn], [rbuf])
                        sinT = (rr[:, ri, 1, :tn], [rbuf])
                        for ci in range(nch):
                            P = proj(wi, wbuf, ci, bi)
                            Ps = proj(wi, wbuf, nch + ci, bi)
                            t1, t2 = T_(tn), T_(tn)
                            if kind in ("retq", "retk"):
                                if kind == "retq":
                                    k.stt(t1, P, 0.125, cosT, ALU.mult, ALU.mult)
                                    k.stt(t2, Ps, 0.125, sinT, ALU.mult, ALU.mult)
                                else:
                                    k.tt(t1, P, cosT, ALU.mult)
                                    k.tt(t2, Ps, sinT, ALU.mult)
                                if kind == "retq":
                                    k.tt(self.Q(1, ci, bi), t1, t2, ALU.add)
                                else:
                                    si, sbuf = sr.next()
                                    o = (stg[:, si, :tn], [sbuf])
                                    k.tt(o, t1, t2, ALU.add)
                                    store(o, dst + ci, bi)
                            else:
                                gname = "aq" if kind == "attq" else "ak"
                                qi, qbuf = sqr.next()
                                sq = (sqt[:, qi, :tn], [qbuf])
                                k.actf(sq, P, AF.Square)
                                _, sb_ = k.bank()
                                k.mm((sb_[0][:, :tn], sb_[1]), self.CB_("bones"), sq)
                                ri2, rb2 = rsr.next()
                                r = (rst[:, ri2, :tn], [rb2])
                                k.actf(r, (sb_[0][:, :tn], sb_[1]), AF.Sqrt, scale=1.0 / 64, bias=self.epsap)
                                k.recip(r, r)
                                k.stt(t1, P, self.C_(f"{gname}n{l}"), cosT, ALU.mult, ALU.mult)
                                k.stt(t2, Ps, self.C_(f"{gname}s{l}"), sinT, ALU.mult, ALU.mult)
                                k.tt(t1, t1, t2, ALU.add)
                                if kind == "attq":
                                    k.tt(self.Q(0, ci, bi), t1, r, ALU.mult)
                                else:
                                    si, sbuf = sr.next()
                                    o = (stg[:, si, :tn], [sbuf])
                                    k.tt(o, t1, r, ALU.mult)
                                    store(o, dst + ci, bi)

                def plain_unit(c0, nch, kind, dst):
                    wi, wbuf = loadw(c0, nch)
                    if c0 in NEXT:
                        prefetch(*NEXT[c0])
                    for bi, (t0, tn, mi) in enumerate(BLOCKS):
                        if mi == 1 and not need_ctx and kind in ("retg", "fnet"):
                            continue
                        for ci in range(nch):
                            P = proj(wi, wbuf, ci, bi)
                            if kind == "retg":
                                k.actf(self.Q(2, ci, bi), P, AF.Silu)
                            else:
                                si, sbuf = sr.next()
                                o = (stg[:, si, :tn], [sbuf])
                                k.actf(o, P, AF.Identity)
                                store(o, dst + ci, bi)

                rope_unit(0, 2, 2, "retq", None)
                rope_unit(4, 2, 2, "retk", 2)
                plain_unit(8, 2, "retv", 4)
                plain_unit(10, 2, "retg", None)
                plain_unit(12, 2, "fnet", 6)
                rope_unit(14, 2, 0, "attq", None)
                rope_unit(18, 1, 0, "attk", 0)
                plain_unit(20, 1, "attv", 1)
                k.barrier()

    def kv_src(self, row, c0, n):
        out = []
        segs = [(0, C, self.kvc[row * 128:(row + 1) * 128, :]),
                (C, SL, self.kva[row // 4][(row % 4) * 128:(row % 4 + 1) * 128, :]),
                (C + SL, SL, self.kva[row // 4][(4 + row % 4) * 128:(5 + row % 4) * 128, :])]
        for (s0, sn, t) in segs:
            a = max(c0, s0)
            b = min(c0 + n, s0 + sn)
            if a < b:
                out.append((t[:, a - s0:b - s0], a - c0, b - a))
        return out

    def attn_phase(self, l, need_ctx):
        k = self.k
        k.barrier()
        wout = self.inp(f"wout{l}", [D, D])
        with contextlib.ExitStack() as es:
            Kt = self.sb(es, "aK", [128, NKC * 128], BF16)
            Kb = Buf()
            Va = self.sb(es, "aV", [128, NKC, 2, 128], BF16)
            Vb = Buf()
            vst = self.sb(es, "avst", [128, 2, 512], BF16)
            vsr = Ring(2)
            E = self.sb(es, "aE", [128, 4, 512], BF16)
            Er = Ring(4)
            accs = self.sb(es, "aacc", [128, 2, 512], F32)
            acr = Ring(2)
            rden = self.sb(es, "arden", [64, 2, 512], F32)
            rdr = Ring(2)
            aout = self.sb(es, "aout", [64, 4, 512], BF16)
            aob = Buf()
            wo = self.sb(es, "awo", [64, 4, D], BF16)
            wob = Buf()
            for (src, off, n) in self.kv_src(0, 0, NKC * 128):
                k.dma(k.sp, Kt[:, off:off + n], src, self.kvrd, [Kb], k.dsem("kload"))
            k.dma(k.pool, wo[:], wout[512:768, :].rearrange("(h p) f -> p h f", p=64), [], [wob], k.dsem("wo"))
            dbg = int(os.environ.get("KDBG", "99"))
            if dbg <= 0:
                k.barrier()
                return
            k.memset((Va[:, :, :, 64:128], [Vb]), 1.0)
            if dbg <= 1:
                k.barrier()
                return
            for pc in range(0, NKC * 128, 512):
                n = min(512, NKC * 128 - pc)
                vi, vb = vsr.next()
                for (src, off, nn) in self.kv_src(1, pc, n):
                    k.dma(k.sp, vst[:, vi, off:off + nn], src, self.kvrd, [vb], k.dsem(f"vst{vi}"))
                for tl in range(n // 128):
                    kc = pc // 128 + tl
                    if dbg == 12:
                        continue
                    pt = k.bbank()
                    k.transpose((pt[0][:, 0:128], pt[1]), (vst[:, vi, tl * 128:(tl + 1) * 128], [vb]), self.CB_("ident"))
                    if dbg == 13:
                        continue
                    if dbg == 14:
                        k.copy((accs[:, 0, 0:64], [acr.bufs[0]]), (pt[0][:, 0:64], pt[1]))
                        continue
                    if dbg == 15:
                        k.copy((Va[:, kc, 0, 0:64], [Vb]), (accs[:, 0, 0:64], [acr.bufs[0]]))
                        continue
                    k.copy((Va[:, kc, 0, 0:64], [Vb]), (pt[0][:, 0:64], pt[1]))
                    k.copy((Va[:, kc, 1, 0:64], [Vb]), (pt[0][:, 64:128], pt[1]))
            if dbg <= 2 or dbg in (12, 13, 14, 15):
                k.barrier()
                return
            for bi, (t0, tn, mi) in enumerate(BLOCKS):
                if mi == 1 and not need_ctx:
                    continue
                if dbg <= 6 and bi > 0:
                    continue
                kcs = list(range(NKC)) if mi == 0 else [0, 1]
                if dbg <= 3:
                    kcs = kcs[:3]
                for h in range(4):
                    kvh = h // 2
                    pr = slice(kvh * 64, kvh * 64 + 64)
                    q = self.Q(0, h % 2, bi, parts=pr)
                    ai, acc = k.bank(hold=True)
                    pend = []

                    def score(kc):
                        _, sb_ = k.bank()
                        k.mm((sb_[0][:, :tn], sb_[1]), (Kt[pr, kc * 128:(kc + 1) * 128], [Kb]), q)
                        return sb_

                    def finish(kc, sb_, first, last):
                        ei, eb = Er.next()
                        e = (E[:, ei, :tn], [eb])
                        k.actf(e, (sb_[0][:, :tn], sb_[1]), AF.Exp, scale=0.125)
                        k.mm((acc[0][:, :tn], acc[1]), (Va[:, kc, kvh, :], [Vb]), e, start=first, stop=last)

                    for idx, kc in enumerate(kcs):
                        pend.append((kc, score(kc)))
                        if len(pend) > 3:
                            kc0, s0 = pend.pop(0)
                            finish(kc0, s0, kc0 == kcs[0], False)
                    while pend:
                        kc0, s0 = pend.pop(0)
                        finish(kc0, s0, kc0 == kcs[0], len(pend) == 0)
                    ci, cbf = acr.next()
                    a_ = (accs[:, ci, :tn], [cbf])
                    k.actf(a_, (acc[0][:, :tn], acc[1]), AF.Identity)
                    k.release(ai)
                    if dbg <= 4:
                        continue
                    _, dn = k.bank()
                    k.mm((dn[0][0:64, :tn], dn[1]), self.CF_("shift"), a_)
                    di, dbf = rdr.next()
                    rd = (rden[:, di, :tn], [dbf])
                    k.recip(rd, (dn[0][0:64, :tn], dn[1]))
                    k.tt((aout[:, h, :tn], [aob]), (accs[0:64, ci, :tn], [cbf]), rd, ALU.mult)
                if dbg <= 5:
                    continue
                for j in range(DC):
                    _, pd = k.bank()
                    for h in range(4):
                        k.mm((pd[0][:, :tn], pd[1]), (wo[:, h, j * 128:(j + 1) * 128], [wob]), (aout[:, h, :tn], [aob]),
                             start=(h == 0), stop=(h == 3), inc=(h == 3))
                    k.stt(self.H(j, bi), (pd[0][:, :tn], pd[1]), self.DER(1, 1, j, mi), self.H(j, bi), ALU.mult, ALU.add)
            k.barrier()

    def ret_phase(self, l, need_ctx):
        k = self.k
        k.barrier()
        wout = self.inp(f"wout{l}", [D, D])
        with contextlib.ExitStack() as es:
            Kr = self.sb(es, "rK", [128, NKC * 128], BF16)
            Kb = Buf()
            Vr = self.sb(es, "rV", [128, NKC, 128], BF16)
            Vb = Buf()
            vst = self.sb(es, "rvst", [128, 2, 256], BF16)
            vsr = Ring(2)
            PM = self.sb(es, "rPM", [128, 2, 2, 4, 512], BF16)
            PMb = Buf()
            PMc = self.sb(es, "rPMc", [128, 2, 2, 256], BF16) if need_ctx else None
            EE = self.sb(es, "rEE", [128, 2, 2, 2, 512], BF16)
            EEb = Buf()
            mg = self.sb(es, "rmg", [128, 2, 512], BF16)
            mgr = Ring(2)
            mg2 = self.sb(es, "rmg2", [128, 2, 512], BF16)
            mg2r = Ring(2)
            Pm = self.sb(es, "rPm", [128, 4, 512], BF16)
            Pmr = Ring(4)
            st = self.sb(es, "rst", [128, 5, 512], F32)
            stb = [Buf() for _ in range(5)]
            rout = self.sb(es, "rout", [128, 2, 512], BF16)
            ror = Ring(2)
            wo = self.sb(es, "rwo", [128, 2, D], BF16)
            wob = Buf()
            ct = self.sb(es, "rct", [128, 4, 132], F32)
            ctb = Buf()
            sx = self.sb(es, "rsx", [128, 4, 72], F32)
            k.dma(k.pool, wo[:], wout[0:256, :].rearrange("(c p) f -> p c f", p=128), [], [wob], k.dsem("wo"))
            T = self.CF_("iota")
            for h in range(4):
                lgf = self.C_(f"lgf{l}", h, 1)
                lgb = self.C_(f"lgb{l}", h, 1)
                c1 = (ct[:, h, 0:56], [ctb])
                k.ts(c1, self.C_("sgf"), lgf, ALU.mult)
                k.stt(c1, self.C_("sgb"), lgb, c1, ALU.mult, ALU.add)
                k.tt((ct[:, h, 56:112], [ctb]), c1, self.C_("dlt"), ALU.mult)
                k.ts((ct[:, h, 112:120], [ctb]), self.C_("d1"), lgf, ALU.mult)
                k.ts((ct[:, h, 120:128], [ctb]), self.C_("d2"), lgb, ALU.mult)
                k.ts((ct[:, h, 128:129], [ctb]), lgb, -1.0, ALU.mult)
                k.actf((sx[:, h, :], [ctb]), (ct[:, h, 56:128], [ctb]), AF.Exp)

            def CT(h, a):
                return (ct[:, h, a:a + 1], [ctb])

            def SX(h, a):
                return (sx[:, h, a:a + 1], [ctb])

            for hp in range(2):
                for (src, off, n) in self.kv_src(2 + hp, 0, NKC * 128):
                    k.dma(k.sp, Kr[:, off:off + n], src, self.kvrd, [Kb], k.dsem("kload"))
                for pc in range(0, NKC * 128, 256):
                    vi, vb = vsr.next()
                    for (src, off, nn) in self.kv_src(4 + hp, pc, 256):
                        k.dma(k.sp, vst[:, vi, off:off + nn], src, self.kvrd, [vb], k.dsem(f"vst{vi}"))
                    for tl in range(2):
                        kc = pc // 128 + tl
                        pt = k.bbank()
                        k.transpose((pt[0][:, 0:128], pt[1]), (vst[:, vi, tl * 128:(tl + 1) * 128], [vb]), self.CB_("ident"))
                        k.copy((Vr[:, kc, 0:64], [Vb]), (pt[0][:, 0:64], pt[1]))
                        k.copy((Vr[:, kc, 64:128], [Vb]), (pt[0][:, 64:128], pt[1]))
                r1 = (st[:, 0, :], [stb[0]])
                r2 = (st[:, 1, :], [stb[1]])
                e1 = (st[:, 2, :], [stb[2]])
                dg = (st[:, 3, :], [stb[3]])
                for r in range(2):
                    for m in range(4):
                        dcol = self.C_("pmd", r * 4 + m, 1)
                        ncol = self.C_("pmdn", r * 4 + m, 1)
                        k.actf(r1, T, AF.Relu, scale=1.0, bias=dcol)
                        k.actf(r2, T, AF.Relu, scale=-1.0, bias=ncol)
                        k.ts(dg, T, ncol, ALU.is_equal)
                        for hh in range(2):
                            h = 2 * hp + hh
                            k.ts(e1, r1, self.C_(f"lgf{l}", h, 1), ALU.mult)
                            k.stt(e1, r2, self.C_(f"lgb{l}", h, 1), e1, ALU.mult, ALU.add)
                            k.actf(e1, e1, AF.Exp)
                            k.tt((PM[:, hh, r, m, :], [PMb]), e1, dg, ALU.add)
                    for hh in range(2):
                        h = 2 * hp + hh
                        for cls in range(2):
                            k.actf((EE[:, hh, r, cls, :], [EEb]), T, AF.Exp, scale=CT(h, r * 28 + (16 if cls == 0 else 11)))
                if need_ctx:
                    for m in range(2):
                        dl = -128.0 * m
                        r1c = (st[:, 0, 0:C], [stb[0]])
                        r2c = (st[:, 1, 0:C], [stb[1]])
                        e1c = (st[:, 2, 0:C], [stb[2]])
                        dgc = (st[:, 3, 0:C], [stb[3]])
                        Tc = (T[0][:, 0:C], T[1])
                        k.ts(r1c, Tc, dl, ALU.add, 0.0, ALU.max)
                        k.ts(r2c, Tc, -1.0, ALU.mult, -dl, ALU.add)
                        k.ts(r2c, r2c, 0.0, ALU.max)
                        k.ts(dgc, Tc, -dl, ALU.is_equal)
                        for hh in range(2):
                            h = 2 * hp + hh
                            k.ts(e1c, r1c, self.C_(f"lgf{l}", h, 1), ALU.mult)
                            k.stt(e1c, r2c, self.C_(f"lgb{l}", h, 1), e1c, ALU.mult, ALU.add)
                            k.actf(e1c, e1c, AF.Exp)
                            k.tt((PMc[:, hh, m, :], [PMb]), e1c, dgc, ALU.add)

                for bi, (t0, tn, mi) in enumerate(BLOCKS):
                    if mi == 1 and not need_ctx:
                        continue
                    kcs = list(range(NKC)) if mi == 0 else [0, 1]
                    qb_ = bi
                    q0 = t0
                    ai, acc = k.bank(hold=True)
                    for hh in range(2):
                        h = 2 * hp + hh
                        pr = slice(hh * 64, hh * 64 + 64)
                        q = self.Q(1, hp, bi, parts=pr)
                        pend = []

                        def score(kc):
                            _, sb_ = k.bank()
                            k.mm((sb_[0][:, :tn], sb_[1]), (Kr[pr, kc * 128:(kc + 1) * 128], [Kb]), q)
                            return sb_

                        def finish(kc, sb_, first, last):
                            pi, pb = Pmr.next()
                            p_ = (Pm[:, pi, :tn], [pb])
                            S_ = (sb_[0][:, :tn], sb_[1])
                            if mi == 1:
                                k.tt(p_, S_, (PMc[:, hh, kc, :tn], [PMb]), ALU.mult)
                            elif kc < 2:
                                i1, b1 = mgr.next()
                                m1 = (mg[:, i1, :tn], [b1])
                                i2, b2 = mg2r.next()
                                m2 = (mg2[:, i2, :tn], [b2])
                                k.actf(m1, (T[0][:, :tn], T[1]), AF.Exp, scale=self.C_(f"lgf{l}", h, 1), bias=CT(h, 112 + qb_ * 2 + kc))
                                k.actf(m2, (T[0][:, :tn], T[1]), AF.Exp, scale=CT(h, 128), bias=CT(h, 120 + qb_ * 2 + kc))
                                k.tt(m1, m1, m2, ALU.add)
                                k.tt(p_, S_, m1, ALU.mult)
                            else:
                                r = (kc - 2) // 16
                                dl = q0 - ((kc - 2) % 16) * 128
                                if -384 <= dl <= 0:
                                    k.tt(p_, S_, (PM[:, hh, r, (-dl) // 128, :tn], [PMb]), ALU.mult)
                                else:
                                    didx = dl // 128 + 15
                                    cls = 0 if dl >= 128 else 1
                                    k.stt(p_, S_, SX(h, r * 28 + didx), (EE[:, hh, r, cls, :tn], [EEb]), ALU.mult, ALU.mult)
                            k.mm((acc[0][pr, :tn], acc[1]), (Vr[:, kc, hh * 64:(hh + 1) * 64], [Vb]), p_, start=first, stop=last)

                        for kc in kcs:
                            pend.append((kc, score(kc)))
                            if len(pend) > 3:
                                kc0, s0 = pend.pop(0)
                                finish(kc0, s0, kc0 == kcs[0], False)
                        while pend:
                            kc0, s0 = pend.pop(0)
                            finish(kc0, s0, kc0 == kcs[0], len(pend) == 0)
                    o = (st[:, 0, :tn], [stb[0]])
                    sq = (st[:, 1, :tn], [stb[1]])
                    mean = (st[:, 2, :tn], [stb[2]])
                    msq = (st[:, 3, :tn], [stb[3]])
                    var = (st[:, 4, :tn], [stb[4]])
                    dd = (st[:, 1, :tn], [stb[1]])
                    k.actf(o, (acc[0][:, :tn], acc[1]), AF.Identity)
                    k.actf(sq, (acc[0][:, :tn], acc[1]), AF.Square)
                    k.release(ai)
                    _, b1 = k.bank()
                    k.mm((b1[0][:, :tn], b1[1]), self.CF_("bones"), o)
                    _, b2 = k.bank()
                    k.mm((b2[0][:, :tn], b2[1]), self.CF_("bones"), sq)
                    k.actf(mean, (b1[0][:, :tn], b1[1]), AF.Identity, scale=1.0 / 64)
                    k.tt(msq, mean, mean, ALU.mult)
                    k.stt(var, (b2[0][:, :tn], b2[1]), 1.0 / 64, msq, ALU.mult, ALU.subtract)
                    k.actf(var, var, AF.Sqrt, bias=self.epsap)
                    k.recip(var, var)
                    k.tt(dd, o, mean, ALU.subtract)
                    k.stt(dd, dd, self.C_(f"retn{l}", hp, 1), var, ALU.mult, ALU.mult)
                    ri_, rb_ = ror.next()
                    ro = (rout[:, ri_, :tn], [rb_])
                    k.tt(ro, dd, self.Q(2, hp, bi), ALU.mult)
                    for j in range(DC):
                        _, pd = k.bank()
                        k.mm((pd[0][:, :tn], pd[1]), (wo[:, hp, j * 128:(j + 1) * 128], [wob]), ro)
                        k.stt(self.H(j, bi), (pd[0][:, :tn], pd[1]), self.DER(1, 1, j, mi), self.H(j, bi), ALU.mult, ALU.add)
            k.barrier()

    def mix_phase(self, l, need_ctx):
        k = self.k
        k.barrier()
        wout = self.inp(f"wout{l}", [D, D])
        with contextlib.ExitStack() as es:
            Kt = self.sb(es, "aK", [128, NKC * 128], BF16)
            Ktb = Buf()
            Va = self.sb(es, "aV", [128, NKC, 128], BF16)
            Vab = Buf()
            E = self.sb(es, "aE", [128, 3, 512], BF16)
            Er = Ring(3)
            accs = self.sb(es, "aacc", [128, 512], F32)
            acb = Buf()
            aout = self.sb(es, "aout", [64, 2, 512], BF16)
            aob = Buf()
            woa = self.sb(es, "awo", [64, 2, D], BF16)
            woab = Buf()
            Kr = self.sb(es, "rK", [128, NKC * 128], BF16)
            Krb = Buf()
            Vr = self.sb(es, "rV", [128, NKC, 128], BF16)
            Vrb = Buf()
            vst = self.sb(es, "rvst", [128, 2, 256], BF16)
            vsr = Ring(2)
            PM = self.sb(es, "rPM", [128, 2, 2, 4, 512], BF16)
            PMb = Buf()
            PMc = self.sb(es, "rPMc", [128, 2, 2, 256], BF16) if need_ctx else None
            AP_ = self.sb(es, "rAp", [128, 6, 512], BF16)
            APb = Buf()
            QA = self.sb(es, "rQA", [128, 2, 512], BF16)
            QAr = Ring(2)
            Kbt = self.sb(es, "rKb", [128, 2, 128], BF16)
            Kbr = Ring(2)
            Wsum = self.sb(es, "rWs", [128, 4, 6, 64], F32)
            Wsb = Buf()
            Wsq = [[Buf() for _ in range(6)] for _ in range(4)]
            Wb = self.sb(es, "rWb", [128, 1, 6, 64], BF16)
            Wbr = Ring(1)
            c1p = self.sb(es, "rc1p", [128, 6, 4], F32)
            c1b = Buf()
            bcol = self.sb(es, "rbcol", [128, 6, 2], F32)
            sxp = self.sb(es, "rsxp", [128, 72], F32)
            sxb = Buf()
            Pm = self.sb(es, "rPm", [128, 3, 512], BF16)
            Pmr = Ring(3)
            st = self.sb(es, "rst", [128, 5, 512], F32)
            stb = [Buf() for _ in range(5)]
            rout = self.sb(es, "rout", [128, 512], BF16)
            rob = Buf()
            wor = self.sb(es, "rwo", [128, D], BF16)
            worb = Buf()
            ct = self.sb(es, "rct", [128, 4, 132], F32)
            ctb = Buf()
            sx = self.sb(es, "rsx", [128, 4, 72], F32)
            for (src, off, n) in self.kv_src(0, 0, NKC * 128):
                k.dma(k.sp, Kt[:, off:off + n], src, self.kvrd, [Ktb], k.dsem("kload"))
            k.memset((Va[:, :, 64:128], [Vab]), 1.0)
            T = self.CF_("iota")
            for h in range(4):
                lgf = self.C_(f"lgf{l}", h, 1)
                lgb = self.C_(f"lgb{l}", h, 1)
                c1 = (ct[:, h, 0:56], [ctb])
                k.ts(c1, self.C_("sgf"), lgf, ALU.mult)
                k.stt(c1, self.C_("sgb"), lgb, c1, ALU.mult, ALU.add)
                k.tt((ct[:, h, 56:112], [ctb]), c1, self.C_("dlt"), ALU.mult)
                k.ts((ct[:, h, 112:120], [ctb]), self.C_("d1"), lgf, ALU.mult)
                k.ts((ct[:, h, 120:128], [ctb]), self.C_("d2"), lgb, ALU.mult)
                k.ts((ct[:, h, 128:129], [ctb]), lgb, -1.0, ALU.mult)
                k.actf((sx[:, h, :], [ctb]), (ct[:, h, 56:128], [ctb]), AF.Exp)

            def CT(h, a):
                return (ct[:, h, a:a + 1], [ctb])

            def SX(h, a):
                return (sx[:, h, a:a + 1], [ctb])

            for p in range(2):
                k.dma(k.pool, woa[:], wout[512 + p * 128:512 + (p + 1) * 128, :].rearrange("(h p) f -> p h f", p=64), [], [woab], k.dsem("wo"))
                k.dma(k.pool, wor[:], wout[p * 128:(p + 1) * 128, :], [], [worb], k.dsem("wo"))
                for (src, off, n) in self.kv_src(2 + p, 0, NKC * 128):
                    k.dma(k.sp, Kr[:, off:off + n], src, self.kvrd, [Krb], k.dsem("kload"))
                for which in range(2):
                    for pc in range(0, NKC * 128, 256):
                        vi, vb = vsr.next()
                        for (src, off, nn) in self.kv_src(1 if which == 0 else 4 + p, pc, 256):
                            k.dma(k.sp, vst[:, vi, off:off + nn], src, self.kvrd, [vb], k.dsem(f"vst{vi}"))
                        for tl in range(2):
                            kc = pc // 128 + tl
                            pt = k.bbank()
                            k.transpose((pt[0][:, 0:128], pt[1]), (vst[:, vi, tl * 128:(tl + 1) * 128], [vb]), self.CB_("ident"))
                            if which == 0:
                                k.copy((Va[:, kc, 0:64], [Vab]), (pt[0][:, p * 64:(p + 1) * 64], pt[1]))
                            else:
                                k.copy((Vr[:, kc, 0:64], [Vrb]), (pt[0][:, 0:64], pt[1]))
                                k.copy((Vr[:, kc, 64:128], [Vrb]), (pt[0][:, 64:128], pt[1]))
                r1 = (st[:, 0, :], [stb[0]])
                r2 = (st[:, 1, :], [stb[1]])
                e1 = (st[:, 2, :], [stb[2]])
                dg = (st[:, 3, :], [stb[3]])
                for r in range(2):
                    for m in range(4):
                        dcol = self.C_("pmd", r * 4 + m, 1)
                        ncol = self.C_("pmdn", r * 4 + m, 1)
                        k.actf(r1, T, AF.Relu, scale=1.0, bias=dcol)
                        k.actf(r2, T, AF.Relu, scale=-1.0, bias=ncol)
                        k.ts(dg, T, ncol, ALU.is_equal)
                        for hh in range(2):
                            h = 2 * p + hh
                            k.ts(e1, r1, self.C_(f"lgf{l}", h, 1), ALU.mult)
                            k.stt(e1, r2, self.C_(f"lgb{l}", h, 1), e1, ALU.mult, ALU.add)
                            k.actf(e1, e1, AF.Exp)
                            k.tt((PM[:, hh, r, m, :], [PMb]), e1, dg, ALU.add)
                if need_ctx:
                    for m in range(2):
                        dl = -128.0 * m
                        r1c = (st[:, 0, 0:C], [stb[0]])
                        r2c = (st[:, 1, 0:C], [stb[1]])
                        e1c = (st[:, 2, 0:C], [stb[2]])
                        dgc = (st[:, 3, 0:C], [stb[3]])
                        Tc = (T[0][:, 0:C], T[1])
                        k.ts(r1c, Tc, dl, ALU.add, 0.0, ALU.max)
                        k.ts(r2c, Tc, -1.0, ALU.mult, -dl, ALU.add)
                        k.ts(r2c, r2c, 0.0, ALU.max)
                        k.ts(dgc, Tc, -dl, ALU.is_equal)
                        for hh in range(2):
                            h = 2 * p + hh
                            k.ts(e1c, r1c, self.C_(f"lgf{l}", h, 1), ALU.mult)
                            k.stt(e1c, r2c, self.C_(f"lgb{l}", h, 1), e1c, ALU.mult, ALU.add)
                            k.actf(e1c, e1c, AF.Exp)
                            k.tt((PMc[:, hh, m, :], [PMb]), e1c, dgc, ALU.add)

                T0 = (self.cf[:, CF["iota"][0]:CF["iota"][0] + 1], [self.cbuf])
                for hh in range(2):
                    h = 2 * p + hh
                    prs = slice(hh * 64, hh * 64 + 64)
                    k.copy((sxp[prs, :], [sxb]), (sx[prs, h, :], [ctb]))
                    for t in range(6):
                        if t < 4:
                            ci_ = (t // 2) * 28 + (16 if t % 2 == 0 else 11)
                            src = (ct[prs, h, ci_:ci_ + 1], [ctb])
                            srcf = (ct[:, h, ci_:ci_ + 1], [ctb])
                        elif t == 4:
                            o_ = CST[f"lgf{l}"][0] + h
                            src = (self.cst[prs, o_:o_ + 1], [self.cbuf])
                            srcf = (self.cst[:, o_:o_ + 1], [self.cbuf])
                        else:
                            src = (ct[prs, h, 128:129], [ctb])
                            srcf = (ct[:, h, 128:129], [ctb])
                        k.copy((c1p[prs, t, 0:1], [c1b]), src)
                        k.actf((bcol[:, t, hh:hh + 1], [c1b]), T0, AF.Exp, scale=srcf)
                for t in range(6):
                    k.ts((c1p[:, t, 1:2], [c1b]), (c1p[:, t, 0:1], [c1b]), -1.0, ALU.mult)
                    k.tt((c1p[:, t, 2:3], [c1b]), T0, (c1p[:, t, 1:2], [c1b]), ALU.mult)
                    k.actf((AP_[:, t, :], [APb]), T, AF.Exp, scale=(c1p[:, t, 0:1], [c1b]), bias=(c1p[:, t, 2:3], [c1b]))
                k.memset((Wsum[:], [Wsb] + [b_ for r_ in Wsq for b_ in r_]), 0.0)
                for kc in range(NKC):
                    pt = k.bbank()
                    k.transpose((pt[0][:, 0:128], pt[1]), (Kr[:, kc * 128:(kc + 1) * 128], [Krb]), self.CB_("ident"))
                    terms = [4, 5] if kc < 2 else [((kc - 2) // 16) * 2, ((kc - 2) // 16) * 2 + 1]
                    for t in terms:
                        uses = []
                        for qb in range(4):
                            if kc < 2:
                                uses.append((qb, (56 if t == 4 else 64) + qb * 2 + kc))
                            else:
                                r = (kc - 2) // 16
                                dl = qb * 512 - ((kc - 2) % 16) * 128
                                if -384 <= dl <= 0:
                                    continue
                                if (0 if dl >= 128 else 1) == t % 2:
                                    uses.append((qb, r * 28 + dl // 128 + 15))
                        if not uses:
                            continue
                        ki, kb_ = Kbr.next()
                        for hh in range(2):
                            k.ts((Kbt[:, ki, hh * 64:(hh + 1) * 64], [kb_]), (pt[0][:, hh * 64:(hh + 1) * 64], pt[1]),
                                 (bcol[:, t, hh:hh + 1], [c1b]), ALU.mult)
                        _, wb_ = k.bank()
                        for hh in range(2):
                            k.mm((wb_[0][hh * 64:(hh + 1) * 64, 0:64], wb_[1]), (Kbt[:, ki, hh * 64:(hh + 1) * 64], [kb_]),
                                 (Vr[:, kc, hh * 64:(hh + 1) * 64], [Vrb]), inc=(hh == 1))
                        for (qb, sidx) in uses:
                            k.stt((Wsum[:, qb, t, :], [Wsq[qb][t]]), (wb_[0][:, 0:64], wb_[1]), (sxp[:, sidx:sidx + 1], [sxb]),
                                  (Wsum[:, qb, t, :], [Wsq[qb][t]]), ALU.mult, ALU.add)

                def att_gen(bi):
                    t0, tn, mi = BLOCKS[bi]
                    kcs = list(range(NKC)) if mi == 0 else [0, 1]
                    pr = slice(p * 64, p * 64 + 64)
                    for j in range(2):
                        q = self.Q(0, j, bi, parts=pr)
                        ai, acc = k.bank(hold=True)
                        pend = []

                        def score(kc):
                            bi_, sb_ = k.bank(hold=True)
                            k.mm((sb_[0][:, :tn], sb_[1]), (Kt[pr, kc * 128:(kc + 1) * 128], [Ktb]), q)
                            return (bi_, sb_)

                        def finish(kc, sbh, first, last):
                            bi_, sb_ = sbh
                            ei, eb = Er.next()
                            e = (E[:, ei, :tn], [eb])
                            k.actf(e, (sb_[0][:, :tn], sb_[1]), AF.Exp, scale=0.125)
                            k.release(bi_)
                            k.mm((acc[0][:, :tn], acc[1]), (Va[:, kc, :], [Vab]), e, start=first, stop=last)

                        groups = [kcs[i_:i_ + 2] for i_ in range(0, len(kcs), 2)]
                        pendg = []
                        seen = [0]

                        def flush(g0, sc0, last_group):
                            es = []
                            for kc0, sbh in zip(g0, sc0):
                                bi_, sb_ = sbh
                                ei, eb = Er.next()
                                e = (E[:, ei, :tn], [eb])
                                k.actf(e, (sb_[0][:, :tn], sb_[1]), AF.Exp, scale=0.125)
                                k.release(bi_)
                                es.append(e)
                            order = list(range(len(g0)))[::-1]
                            for n_, idx in enumerate(order):
                                first = (seen[0] == 0)
                                seen[0] += 1
                                last = last_group and (n_ == len(order) - 1)
                                k.mm((acc[0][:, :tn], acc[1]), (Va[:, g0[idx], :], [Vab]), es[idx], start=first, stop=last)

                        for gi_, g in enumerate(groups):
                            pendg.append((g, [score(kc) for kc in g]))
                            if len(pendg) > 1:
                                g0, sc0 = pendg.pop(0)
                                flush(g0, sc0, False)
                                yield
                        while pendg:
                            g0, sc0 = pendg.pop(0)
                            flush(g0, sc0, len(pendg) == 0)
                            yield
                        a_ = (accs[:, :tn], [acb])
                        k.actf(a_, (acc[0][:, :tn], acc[1]), AF.Identity)
                        k.release(ai)
                        _, dn = k.bank()
                        k.mm((dn[0][0:64, :tn], dn[1]), self.CF_("shift"), a_)
                        rd = (st[0:64, 4, :tn], [stb[4]])
                        k.recip(rd, (dn[0][0:64, :tn], dn[1]))
                        k.tt((aout[:, j, :tn], [aob]), (accs[0:64, :tn], [acb]), rd, ALU.mult)
                        yield
                    for jj in range(DC):
                        _, pd = k.bank()
                        for j in range(2):
                            k.mm((pd[0][:, :tn], pd[1]), (woa[:, j, jj * 128:(jj + 1) * 128], [woab]), (aout[:, j, :tn], [aob]),
                                 start=(j == 0), stop=(j == 1), inc=(j == 1))
                        k.stt(self.H(jj, bi), (pd[0][:, :tn], pd[1]), self.DER(1, 1, jj, mi), self.H(jj, bi), ALU.mult, ALU.add)
                        yield

                def ret_gen(bi):
                    t0, tn, mi = BLOCKS[bi]
                    q0 = t0
                    ai, acc = k.bank(hold=True)
                    started = [False, False]
                    work = []
                    if mi == 1:
                        for hh in range(2):
                            for kc in (0, 1):
                                work.append((hh, kc))
                    else:
                        wi_, wbb = Wbr.next()
                        k.copy((Wb[:, wi_].rearrange("p a b -> p (a b)"), [wbb]), (Wsum[:, bi].rearrange("p a b -> p (a b)"), [Wsb] + Wsq[bi]))
                        for t in range(6):
                            qi_, qab = QAr.next()
                            qa = (QA[:, qi_, :tn], [qab])
                            k.tt(qa, self.Q(1, p, bi), (AP_[:, t, :tn], [APb]), ALU.mult)
                            for hh in range(2):
                                pr = slice(hh * 64, hh * 64 + 64)
                                k.mm((acc[0][pr, :tn], acc[1]), (Wb[pr, wi_, t, :], [wbb]), (QA[pr, qi_, :tn], [qab]),
                                     start=(not started[hh]), stop=False)
                                started[hh] = True
                            yield
                        for hh in range(2):
                            for kc in range(2, NKC):
                                dl = q0 - ((kc - 2) % 16) * 128
                                if -384 <= dl <= 0:
                                    work.append((hh, kc))
                    last_of = {}
                    for (hh, kc) in work:
                        last_of[hh] = kc
                    def rscore(hh, kc):
                        pr = slice(hh * 64, hh * 64 + 64)
                        sbi_, sb_ = k.bank(hold=True)
                        k.mm((sb_[0][:, :tn], sb_[1]), (Kr[pr, kc * 128:(kc + 1) * 128], [Krb]), self.Q(1, p, bi, parts=pr))
                        return (sbi_, sb_)

                    def rfinish(hh, kc, sbh):
                        sbi_, sb_ = sbh
                        pr = slice(hh * 64, hh * 64 + 64)
                        pi, pb = Pmr.next()
                        p_ = (Pm[:, pi, :tn], [pb])
                        S_ = (sb_[0][:, :tn], sb_[1])
                        if mi == 1:
                            k.tt(p_, S_, (PMc[:, hh, kc, :tn], [PMb]), ALU.mult)
                        else:
                            r = (kc - 2) // 16
                            k.tt(p_, S_, (PM[:, hh, r, (-(q0 - ((kc - 2) % 16) * 128)) // 128, :tn], [PMb]), ALU.mult)
                        k.release(sbi_)
                        k.mm((acc[0][pr, :tn], acc[1]), (Vr[:, kc, hh * 64:(hh + 1) * 64], [Vrb]), p_,
                             start=(not started[hh]), stop=(last_of[hh] == kc))
                        started[hh] = True

                    rp = []
                    for (hh, kc) in work:
                        rp.append((hh, kc, rscore(hh, kc)))
                        if len(rp) > 2:
                            a_, b_, c_ = rp.pop(0)
                            rfinish(a_, b_, c_)
                            yield
                    while rp:
                        a_, b_, c_ = rp.pop(0)
                        rfinish(a_, b_, c_)
                        yield
                    o = (st[:, 0, :tn], [stb[0]])
                    sq = (st[:, 1, :tn], [stb[1]])
                    mean = (st[:, 2, :tn], [stb[2]])
                    msq = (st[:, 3, :tn], [stb[3]])
                    var = (st[:, 4, :tn], [stb[4]])
                    dd = (st[:, 1, :tn], [stb[1]])
                    k.actf(o, (acc[0][:, :tn], acc[1]), AF.Identity)
                    k.actf(sq, (acc[0][:, :tn], acc[1]), AF.Square)
                    k.release(ai)
                    _, b1 = k.bank()
                    k.mm((b1[0][:, :tn], b1[1]), self.CF_("bones"), o)
                    k.actf(mean, (b1[0][:, :tn], b1[1]), AF.Identity, scale=1.0 / 64)
                    _, b2 = k.bank()
                    k.mm((b2[0][:, :tn], b2[1]), self.CF_("bones"), sq)
                    k.tt(msq, mean, mean, ALU.mult)
                    k.stt(var, (b2[0][:, :tn], b2[1]), 1.0 / 64, msq, ALU.mult, ALU.subtract)
                    k.actf(var, var, AF.Sqrt, bias=self.epsap)
                    k.recip(var, var)
                    k.tt(dd, o, mean, ALU.subtract)
                    k.stt(dd, dd, self.C_(f"retn{l}", p, 1), var, ALU.mult, ALU.mult)
                    ro = (rout[:, :tn], [rob])
                    k.tt(ro, dd, self.Q(2, p, bi), ALU.mult)
                    yield
                    for jj in range(DC):
                        _, pd = k.bank()
                        k.mm((pd[0][:, :tn], pd[1]), (wor[:, jj * 128:(jj + 1) * 128], [worb]), ro)
                        k.stt(self.H(jj, bi), (pd[0][:, :tn], pd[1]), self.DER(1, 1, jj, mi), self.H(jj, bi), ALU.mult, ALU.add)
                        yield

                for bi, (t0, tn, mi) in enumerate(BLOCKS):
                    if mi == 1 and not need_ctx:
                        continue
                    for g in (ret_gen(bi), att_gen(bi)):
                        for _ in g:
                            pass
            k.barrier()

    def fft_phase(self, l, need_ctx):
        k = self.k
        k.barrier()
        wout = self.inp(f"wout{l}", [D, D])
        dft = self.inp("dft", [S, 2, SL], BF16).rearrange("(nt p) c k -> p nt c k", p=128)
        with contextlib.ExitStack() as es:
            AB = self.sb(es, "fAB", [128, 32, 2, 256], BF16)
            ABb = Buf()
            fst = self.sb(es, "fst", [128, 2, 2, 512], BF16)
            fsr = Ring(2)
            tr = self.sb(es, "ftr", [128, 6, 2, 512], BF16)
            trr = Ring(6)
            fo = self.sb(es, "fo", [128, 2, 512], BF16)
            fob = Buf()
            wo = self.sb(es, "fwo", [128, 2, D], BF16)
            wob = Buf()
            k.dma(k.pool, wo[:], wout[256:512, :].rearrange("(c p) f -> p c f", p=128), [], [wob], k.dsem("wo"))

            def build_ab(dst, dbuf, srcs, ntok):
                for pc in range(0, ntok, 512):
                    n = min(512, ntok - pc)
                    fi, fb = fsr.next()
                    for ci in range(2):
                        for (src, off, nn) in srcs(ci, pc, n):
                            k.dma(k.sp, fst[:, fi, ci, off:off + nn], src, self.kvrd, [fb], k.dsem(f"vst{fi}"))
                    for tl in range(n // 128):
                        tt_ = pc // 128 + tl
                        for ci in range(2):
                            _, bk = k.bank()
                            k.mm((bk[0][:, 0:256], bk[1]), (fst[:, fi, ci, tl * 128:(tl + 1) * 128], [fb]), self.CB_("fd"))
                            k.actf((dst[:, tt_, ci, :], [dbuf]), (bk[0][:, 0:256], bk[1]), AF.Identity)

            def lat_src(ci, c0, n):
                out = []
                for (s0, t) in ((0, self.kva[1][(2 + ci) * 128:(3 + ci) * 128, :]), (SL, self.kva[1][(6 + ci) * 128:(7 + ci) * 128, :])):
                    a, b = max(c0, s0), min(c0 + n, s0 + SL)
                    if a < b:
                        out.append((t[:, a - s0:b - s0], a - c0, b - a))
                return out

            build_ab(AB, ABb, lat_src, S)
            for kb in range(4):
                t0, tn, mi = BLOCKS[kb]
                a0, acc0 = k.bank(hold=True)
                a1, acc1 = k.bank(hold=True)
                accs = [acc0, acc1]
                for nt_ in range(32):
                    ti, tb = trr.next()
                    k.dma(k.sp, tr[:, ti], dft[:, nt_, :, kb * 512:(kb + 1) * 512], [], [tb], k.dsem(f"dft{ti}"))
                    for c in range(2):
                        k.mm((accs[c][0], accs[c][1]), (AB[:, nt_, c, 0:128], [ABb]), (tr[:, ti, 0, :], [tb]),
                             start=(nt_ == 0), stop=False, inc=False)
                        k.mm((accs[c][0], accs[c][1]), (AB[:, nt_, c, 128:256], [ABb]), (tr[:, ti, 1, :], [tb]),
                             start=False, stop=(nt_ == 31), inc=(c == 1))
                for c in range(2):
                    k.actf((fo[:, c, :], [fob]), accs[c], AF.Identity)
                k.release(a0)
                k.release(a1)
                for j in range(DC):
                    _, pd = k.bank()
                    for c in range(2):
                        k.mm((pd[0][:, :tn], pd[1]), (wo[:, c, j * 128:(j + 1) * 128], [wob]), (fo[:, c, :tn], [fob]),
                             start=(c == 0), stop=(c == 1), inc=(c == 1))
                    k.stt(self.H(j, kb), (pd[0][:, :tn], pd[1]), self.DER(1, 1, j, mi), self.H(j, kb), ALU.mult, ALU.add)
            if need_ctx:
                dftc = self.inp("dftc", [C, 2, C], BF16).rearrange("(nt p) c k -> p nt c k", p=128)
                ABc = self.sb(es, "fABc", [128, 2, 2, 256], BF16)
                ABcb = Buf()
                trc = self.sb(es, "ftrc", [128, 2, 2, C], BF16)
                trcb = Buf()
                k.dma(k.sp, trc[:], dftc, [], [trcb], k.dsem("dft0"))
                build_ab(ABc, ABcb, lambda ci, c0, n: [(self.kvc[(6 + ci) * 128:(7 + ci) * 128, c0:c0 + n], 0, n)], C)
                t0, tn, mi = BLOCKS[4]
                a0, acc0 = k.bank(hold=True)
                a1, acc1 = k.bank(hold=True)
                accs = [acc0, acc1]
                for nt_ in range(2):
                    for c in range(2):
                        k.mm((accs[c][0][:, :tn], accs[c][1]), (ABc[:, nt_, c, 0:128], [ABcb]), (trc[:, nt_, 0, :], [trcb]),
                             start=(nt_ == 0), stop=False, inc=False)
                        k.mm((accs[c][0][:, :tn], accs[c][1]), (ABc[:, nt_, c, 128:256], [ABcb]), (trc[:, nt_, 1, :], [trcb]),
                             start=False, stop=(nt_ == 1), inc=(c == 1))
                for c in range(2):
                    k.actf((fo[:, c, :tn], [fob]), (accs[c][0][:, :tn], accs[c][1]), AF.Identity)
                k.release(a0)
                k.release(a1)
                for j in range(DC):
                    _, pd = k.bank()
                    for c in range(2):
                        k.mm((pd[0][:, :tn], pd[1]), (wo[:, c, j * 128:(j + 1) * 128], [wob]), (fo[:, c, :tn], [fob]),
                             start=(c == 0), stop=(c == 1), inc=(c == 1))
                    k.stt(self.H(j, 4), (pd[0][:, :tn], pd[1]), self.DER(1, 1, j, mi), self.H(j, 4), ALU.mult, ALU.add)
            k.barrier()

    def final_norm(self):
        k = self.k
        k.barrier()
        y = self.outp("yT", [D, SL])
        with contextlib.ExitStack() as es:
            sq8 = self.sb(es, "sq8", [128, 3, 512], BF16)
            sqr = Ring(3)
            rs = self.sb(es, "rs", [128, 2, 512], F32)
            rsr = Ring(2)
            tt = self.sb(es, "mtt", [128, 2, 512], F32)
            ttr = Ring(2)
            yo = self.sb(es, "yo", [128, 3, 512], F32)
            yr = Ring(3)
            for bi, (t0, tn, mi) in enumerate(LAT):
                _, bk = k.bank()
                for dc in range(DC):
                    qi, qbf = sqr.next()
                    k.actf((sq8[:, qi, :tn], [qbf]), self.H(dc, bi), AF.Square)
                    k.mm((bk[0][:, :tn], bk[1]), self.CB_("ones"), (sq8[:, qi, :tn], [qbf]), start=(dc == 0),
                         stop=(dc == DC - 1))
                ri, rbuf = rsr.next()
                r = (rs[:, ri, :tn], [rbuf])
                k.actf(r, (bk[0][:, :tn], bk[1]), AF.Sqrt, scale=1.0 / D, bias=self.epsap)
                k.recip(r, r)
                for dc in range(DC):
                    yi, yb = yr.next()
                    o = (yo[:, yi, :tn], [yb])
                    k.stt(o, self.H(dc, bi), self.C_("fng", dc, 1), r, ALU.mult, ALU.mult)
                    k.dma(k.sp, y[dc * 128:(dc + 1) * 128, t0:t0 + tn], o[0], o[1], [], k.dsem("yout"))
            k.barrier()


def build_prog(seg, fused=False, phases=None):
    p = Prog(seg, fused)
    if phases is not None:
        p.phases = phases
    p.epsap = EPS
    nc = p.build()
    return p, nc


_PROGS = {}


def get_prog(seg):
    if seg not in _PROGS:
        _PROGS[seg] = build_prog(seg)
    return _PROGS[seg]


def kernel(**inp):
    inp = {k_: np.asarray(v) for k_, v in inp.items()}
    cf, cb = host_static()
    cores = list(range(8))
    csts = [host_consts(inp, c // 2, c % 2) for c in cores]
    ropes = [host_rope(s) for s in range(2)]
    wext = [host_w_in_ext(inp["w_in"][l]) for l in range(2)]
    wsT = [np.ascontiguousarray(np.transpose(inp["gmlp_w_s"][l], (2, 0, 1)).reshape(128, 512), np.float32) for l in range(2)]
    adawh = [[np.ascontiguousarray(inp["ada_w"][l][:, r * 4608:(r + 1) * 4608]) for r in range(2)] for l in range(2)]

    def base(c):
        return {"cst": csts[c], "cf": cf, "cb": cb}

    def full(name, c, state):
        b, s = c // 2, c % 2
        if name in ("cst", "cf", "cb"):
            return base(c)[name]
        if name == "h0":
            return np.ascontiguousarray(np.concatenate([inp["x"][b, s * SL:(s + 1) * SL].T, inp["ctx"][b].T], axis=1), np.float32)
        if name == "rope":
            return ropes[s]
        if name == "dft":
            return host_dft(s)[0]
        if name == "dftc":
            return host_dft(s)[1]
        for l in range(2):
            if name == f"ada_wh{l}":
                return adawh[l][s]
            if name == f"win{l}":
                return wext[l]
            if name == f"wout{l}":
                return np.ascontiguousarray(inp["w_out"][l])
            if name == f"wsT{l}":
                return wsT[l]
            for w in (1, 2):
                if name == f"wgu{w}_{l}":
                    return np.ascontiguousarray(inp[f"ffn{w}_w_gu"][l])
                if name == f"wd{w}_{l}":
                    return np.ascontiguousarray(inp[f"ffn{w}_w_down"][l])
        if name.startswith("i_st_"):
            return state[c]["o_st_" + name[5:]]
        if name == "kvf_own":
            return state[c]["kvf_loc"]
        if name == "kvf_oth":
            return state[c ^ 1]["kvf_loc"]
        if name == "kvf_ctx_in":
            return state[c]["kvf_ctx"]
        raise KeyError(name)

    p, nc = get_prog(0)
    in_maps = [{n: full(n, c, None) for n in p.in_names} for c in cores]
    res = run_bass_kernel_spmd(nc, in_maps, core_ids=cores)
    state = res.results
    out = np.empty((4, S, D), np.float32)
    for c in cores:
        b, s = c // 2, c % 2
        out[b, s * SL:(s + 1) * SL, :] = state[c]["yT"].T
    return out
```
